# Optimizing a Trainium2 kernel written in Bass

```python
import jax, jax.numpy as jnp
from jax import lax
import numpy as np

D_MODEL = 1024
BATCH = 32
SEQ = 256
DEPTH = 1
DEC_BATCH = 8
DEC_SEQ = 1024
PAST_LEN = 256

GRID_W = 64
N_HEADS = 8
N_KV_HEADS = 2
GROUP = N_HEADS // N_KV_HEADS
HEAD_DIM = 64
WINDOW = 128
BLOCK = 128
F_GROUPS = 4
F_GROUP_DIM = 128
F_WIDTH = F_GROUPS * F_GROUP_DIM
ATTN_WIDTH = N_HEADS * HEAD_DIM
KV_WIDTH = N_KV_HEADS * HEAD_DIM
N_BRANCHES = 2
IN_WIDTH = F_WIDTH + ATTN_WIDTH + 2 * KV_WIDTH + N_BRANCHES * D_MODEL
D_FF = ((8 * D_MODEL + 3 * 256 - 1) // (3 * 256)) * 256
ROPE_THETA = 10000.0
EPS = 1e-6
NEG_INF = -1e30

kernel_name = "hybrid_fnet_swa_diffusion_step"


def rms_norm(x, g):
    xf = x.astype(jnp.float32)
    y = xf * lax.rsqrt(jnp.mean(xf * xf, axis=-1, keepdims=True) + EPS)
    return (y * g.astype(jnp.float32)).astype(x.dtype)


def adaln(cvec, w_ada, b_ada):
    mod = jax.nn.silu(cvec) @ w_ada + b_ada
    return jnp.split(mod, 6, axis=-1)


def mixer_project(h, w_in, g_q, g_k):
    b, n, _ = h.shape
    z = h @ w_in
    zf, zq, zk, zv, zg = jnp.split(
        z, [F_WIDTH, F_WIDTH + ATTN_WIDTH, F_WIDTH + ATTN_WIDTH + KV_WIDTH,
            F_WIDTH + ATTN_WIDTH + 2 * KV_WIDTH], axis=-1)
    q = rms_norm(zq.reshape(b, n, N_HEADS, HEAD_DIM), g_q)
    k = rms_norm(zk.reshape(b, n, N_KV_HEADS, HEAD_DIM), g_k)
    v = zv.reshape(b, n, N_KV_HEADS, HEAD_DIM)
    return zf, q, k, v, zg


def fourier_mix(zf, w_f):
    b, n, _ = zf.shape
    zgrp = zf.reshape(b, n, F_GROUPS, F_GROUP_DIM).astype(jnp.float32)
    mixed = jnp.real(jnp.fft.fft2(zgrp, axes=(1, 3), norm="ortho"))
    return mixed.reshape(b, n, F_WIDTH).astype(zf.dtype) @ w_f


def axial_rope(x):
    n = x.shape[1]
    rows = n // GRID_W
    row = jnp.repeat(jnp.arange(rows, dtype=jnp.float32), GRID_W)
    col = jnp.tile(jnp.arange(GRID_W, dtype=jnp.float32), rows)
    axis_dim = HEAD_DIM // 2
    inv_freq = ROPE_THETA ** (-jnp.arange(0, axis_dim, 2, dtype=jnp.float32) / axis_dim)
    xf = x.astype(jnp.float32)

    def rotate(xa, pos):
        ang = pos[:, None] * inv_freq[None, :]
        cos = jnp.cos(ang)[None, :, None, :]
        sin = jnp.sin(ang)[None, :, None, :]
        x1, x2 = jnp.split(xa, 2, axis=-1)
        return jnp.concatenate([x1 * cos - x2 * sin, x1 * sin + x2 * cos], axis=-1)

    out = jnp.concatenate([rotate(xf[..., :axis_dim], row), rotate(xf[..., axis_dim:], col)], axis=-1)
    return out.astype(x.dtype)


def sink_softmax(sc, sink):
    s = sink[:, :, None]
    m = jnp.maximum(jnp.max(sc, axis=-1), s)
    p = jnp.exp(sc - m[..., None])
    return p / (jnp.sum(p, axis=-1) + jnp.exp(s - m))[..., None]


def context_attention(q, k, v, sink):
    b, s = q.shape[:2]
    scale = HEAD_DIM ** -0.5
    qb = jnp.moveaxis(q.reshape(b, s // BLOCK, BLOCK, N_KV_HEADS, GROUP, HEAD_DIM), 1, 0)

    def one_block(qblk):
        sc = jnp.einsum("bqkgd,bjkd->bkgqj", qblk, k).astype(jnp.float32) * scale
        p = sink_softmax(sc, sink).astype(v.dtype)
        return jnp.einsum("bkgqj,bjkd->bqkgd", p, v)

    out = lax.map(one_block, qb)
    return jnp.moveaxis(out, 0, 1).reshape(b, s, ATTN_WIDTH)


def latent_attention(q, k, v, ck, cv, sink):
    b, n = q.shape[:2]
    nb = n // BLOCK
    scale = HEAD_DIM ** -0.5
    qb = q.reshape(b, nb, BLOCK, N_KV_HEADS, GROUP, HEAD_DIM)
    pad = ((0, 0), (BLOCK, BLOCK), (0, 0), (0, 0))
    kb = jnp.pad(k, pad).reshape(b, nb + 2, BLOCK, N_KV_HEADS, HEAD_DIM)
    vb = jnp.pad(v, pad).reshape(b, nb + 2, BLOCK, N_KV_HEADS, HEAD_DIM)
    kwin = jnp.concatenate([kb[:, :-2], kb[:, 1:-1], kb[:, 2:]], axis=2)
    vwin = jnp.concatenate([vb[:, :-2], vb[:, 1:-1], vb[:, 2:]], axis=2)
    q_pos = jnp.arange(n).reshape(nb, BLOCK)
    k_pos = (jnp.arange(nb)[:, None] - 1) * BLOCK + jnp.arange(3 * BLOCK)[None, :]
    mask = ((jnp.abs(q_pos[:, :, None] - k_pos[:, None, :]) <= WINDOW)
            & (k_pos[:, None, :] >= 0) & (k_pos[:, None, :] < n))
    s_loc = jnp.einsum("bnqkgd,bnjkd->bnkgqj", qb, kwin).astype(jnp.float32) * scale
    s_loc = jnp.where(mask[None, :, None, None, :, :], s_loc, NEG_INF)
    s_ctx = jnp.einsum("bnqkgd,bjkd->bnkgqj", qb, ck).astype(jnp.float32) * scale
    p = sink_softmax(jnp.concatenate([s_loc, s_ctx], axis=-1), sink).astype(v.dtype)
    out = (jnp.einsum("bnkgqj,bnjkd->bnqkgd", p[..., :3 * BLOCK], vwin)
           + jnp.einsum("bnkgqj,bjkd->bnqkgd", p[..., 3 * BLOCK:], cv))
    return out.reshape(b, n, ATTN_WIDTH)


def merge_branches(yf, ya, zg, w_out):
    gf, ga = jnp.split(jax.nn.sigmoid(zg), N_BRANCHES, axis=-1)
    return (gf * yf + ga * ya) @ w_out


def swiglu(h, w_up, w_down):
    a, u = jnp.split(h @ w_up, 2, axis=-1)
    return (jax.nn.silu(a) * u) @ w_down


def trunk_layer(x, cvec, attend, w_ada, b_ada, g_norm1, g_norm2, w_in, g_q, g_k,
                w_f, w_ao, w_out, w_up, w_down):
    sh1, sc1, gt1, sh2, sc2, gt2 = adaln(cvec, w_ada, b_ada)
    h = rms_norm(x, g_norm1) * (1.0 + sc1) + sh1
    zf, q, k, v, zg = mixer_project(h, w_in, g_q, g_k)
    yf = fourier_mix(zf, w_f)
    ya = attend(q, k, v) @ w_ao
    x = x + gt1 * merge_branches(yf, ya, zg, w_out)
    h = rms_norm(x, g_norm2) * (1.0 + sc2) + sh2
    x = x + gt2 * swiglu(h, w_up, w_down)
    return x, k, v


def setup_inputs(seed: int = 0) -> dict:
    key = jax.random.key(seed)
    ks = jax.random.split(key, 20)

    def nrm(k, shape, s):
        return jax.random.normal(k, shape, jnp.float32) * s

    cache_shape = (DEC_BATCH, DEPTH, PAST_LEN, N_KV_HEADS, HEAD_DIM)
    return {
        "x_prompt": nrm(ks[0], (BATCH, SEQ, D_MODEL), 1.0),
        "x_sample": nrm(ks[1], (DEC_BATCH, DEC_SEQ, D_MODEL), 1.0),
        "cache_k": nrm(ks[2], cache_shape, 1.0),
        "cache_v": nrm(ks[3], cache_shape, 1.0),
        "c": nrm(ks[4], (DEC_BATCH, D_MODEL), 1.0),
        "c_ctx": nrm(ks[5], (D_MODEL,), 1.0),
        "w_ada": nrm(ks[6], (DEPTH, D_MODEL, 6 * D_MODEL), 0.5 * D_MODEL ** -0.5),
        "b_ada": nrm(ks[7], (DEPTH, 6 * D_MODEL), 0.01),
        "g_norm1": 1.0 + nrm(ks[8], (DEPTH, D_MODEL), 0.02),
        "g_norm2": 1.0 + nrm(ks[9], (DEPTH, D_MODEL), 0.02),
        "w_in": nrm(ks[10], (DEPTH, D_MODEL, IN_WIDTH), D_MODEL ** -0.5),
        "g_q": 1.0 + nrm(ks[11], (DEPTH, HEAD_DIM), 0.02),
        "g_k": 1.0 + nrm(ks[12], (DEPTH, HEAD_DIM), 0.02),
        "sinks": nrm(ks[13], (DEPTH, N_HEADS), 1.0),
        "w_f": nrm(ks[14], (DEPTH, F_WIDTH, D_MODEL), F_WIDTH ** -0.5),
        "w_ao": nrm(ks[15], (DEPTH, ATTN_WIDTH, D_MODEL), ATTN_WIDTH ** -0.5),
        "w_out": nrm(ks[16], (DEPTH, D_MODEL, D_MODEL), D_MODEL ** -0.5),
        "w_up": nrm(ks[17], (DEPTH, D_MODEL, 2 * D_FF), D_MODEL ** -0.5),
        "w_down": nrm(ks[18], (DEPTH, D_FF, D_MODEL), D_FF ** -0.5),
    }


def reference(x_prompt, x_sample, cache_k, cache_v, c, c_ctx, w_ada, b_ada, g_norm1, g_norm2,
              w_in, g_q, g_k, sinks, w_f, w_ao, w_out, w_up, w_down):
    xp = x_prompt
    xs = x_sample
    new_k = []
    new_v = []
    for l in range(DEPTH):
        sink = sinks[l].reshape(N_KV_HEADS, GROUP).astype(jnp.float32)
        params = (w_ada[l], b_ada[l], g_norm1[l], g_norm2[l], w_in[l], g_q[l], g_k[l],
                  w_f[l], w_ao[l], w_out[l], w_up[l], w_down[l])

        def attend_ctx(q, k, v, sink=sink):
            return context_attention(q, k, v, sink)

        xp, k_ctx, v_ctx = trunk_layer(xp, c_ctx[None, None, :], attend_ctx, *params)
        new_k.append(k_ctx)
        new_v.append(v_ctx)

        ck = cache_k[:, l]
        cv = cache_v[:, l]

        def attend_lat(q, k, v, sink=sink, ck=ck, cv=cv):
            return latent_attention(axial_rope(q), axial_rope(k), v, ck, cv, sink)

        xs, _, _ = trunk_layer(xs, c[:, None, :], attend_lat, *params)

    new_cache_k = jnp.stack(new_k, axis=1)
    new_cache_v = jnp.stack(new_v, axis=1)
    return (xp, xs, new_cache_k, new_cache_v)
```

```python
import numpy as np
from contextlib import ExitStack
import concourse.bass as bass
import concourse.mybir as mybir
from concourse.bass_utils import run_bass_kernel_spmd
from concourse.alu_op_type import AluOpType as ALU

F32 = mybir.dt.float32
BF16 = mybir.dt.bfloat16
I32 = mybir.dt.int32
AF = mybir.ActivationFunctionType
AX = mybir.AxisListType

D = 1024
NT = 8
DFF = 2816
EPS = 1e-6
NCORES = 8
SCHED_WINDOW = 40
N_TBANKS = 3
CTX_INTERLEAVE = True
PV_LAG = 2
PTL_RING = 6
CACHE_QH1_AT = 4
LAT_EXTRA_BANKS = [0, 1, 2]
CTX_EXTRA_BANKS = [0, 1, 2]
P3_EXTRA_BANKS = [0, 1, 2]
P4_EXTRA_BANKS = []
LATF_EXTRA_BANKS = [2]
SCHED_REORDER = True
_LAST = {}


class Sched:
    QUEUES = ["pe", "act", "dve", "pool", "sp"]
    DEFCOST = {"pe": 1000.0, "act": 600.0, "dve": 690.0, "pool": 1260.0, "sp": 100.0}

    def __init__(self, nc, stack, n_sp_slots=16, n_pool_slots=24):
        self.nc = nc
        self.sem = {e: stack.enter_context(nc.semaphore(f"s_{e}")) for e in self.QUEUES}
        self.dpool = {
            "sp": [stack.enter_context(nc.semaphore(f"dsp_{i}")) for i in range(n_sp_slots)],
            "pool": [stack.enter_context(nc.semaphore(f"dpl_{i}")) for i in range(n_pool_slots)],
        }
        self.ops = []
        self.lastw = {}
        self.readers = {}
        self.final_wait = None

    def _mkdeps(self, reads, writes, q=None):
        deps = {}

        def add(i, raw):
            deps[i] = deps.get(i, False) or raw
        for r in reads:
            i = self.lastw.get(r)
            if i is not None:
                add(i, True)
            if isinstance(r, tuple) and r[0] == "P":
                for i in self.readers.get(r, ()):
                    if self.ops[i]["q"] != q:
                        add(i, False)
        for w in writes:
            i = self.lastw.get(w)
            if i is not None:
                add(i, False)
            for i in self.readers.get(w, ()):
                add(i, False)
        return deps

    def _record(self, idx, reads, writes):
        for r in reads:
            self.readers.setdefault(r, []).append(idx)
        for w in writes:
            self.lastw[w] = idx
            self.readers[w] = []

    def op(self, q, fn, reads=(), writes=(), cost=None, n=None):
        if cost is None and n is not None:
            cost = {"act": 170 + 0.833 * n, "dve": 155 + 1.04 * n, "pool": 130 + 2.2 * n, "pe": 1000.0}[q]
        idx = len(self.ops)
        deps = self._mkdeps(reads, writes, q)
        import sys as _sys
        self.ops.append(dict(q=q, kind="op", fn=fn, deps=deps, cost=cost if cost is not None else self.DEFCOST[q],
                             line=_sys._getframe(1).f_lineno))
        self._record(idx, reads, writes)
        return idx

    def dma(self, q, out, in_, reads=(), writes=(), nbytes=65536, **kw):
        idx = len(self.ops)
        deps = self._mkdeps(reads, writes)

        def fn(e, out=out, in_=in_, kw=kw):
            return e.dma_start(out=out, in_=in_, **kw)
        import sys as _sys
        self.ops.append(dict(q=q, kind="dma", fn=fn, deps=deps, cost=float(nbytes) / 330.0, line=_sys._getframe(1).f_lineno))
        self._record(idx, reads, writes)
        return idx

    def finish(self, q="sp"):
        self.final_wait = q

    def schedule(self, window=40, reorder=True):
        ops = self.ops
        n = len(ops)
        fin = [None] * n
        pend = {q: [i for i in range(n) if ops[i]["q"] == q] for q in self.QUEUES}
        head = {q: 0 for q in self.QUEUES}
        done = [False] * n
        qfree = {q: 0.0 for q in self.QUEUES}
        dma_free = 0.0
        order = {q: [] for q in self.QUEUES}
        remaining = n
        while remaining:
            best = None
            for q in self.QUEUES:
                lst = pend[q]
                h = head[q]
                while h < len(lst) and done[lst[h]]:
                    h += 1
                head[q] = h
                cnt = 0
                j = h
                while j < len(lst) and cnt < window:
                    i = lst[j]
                    j += 1
                    if done[i]:
                        continue
                    cnt += 1
                    t = qfree[q]
                    ok = True
                    for d in ops[i]["deps"]:
                        f = fin[d]
                        if f is None:
                            ok = False
                            break
                        if f > t:
                            t = f
                    if ok:
                        kt = t - 4000.0 if ops[i]["kind"] == "dma" else t
                        if best is None or (kt, i) < best[0]:
                            best = ((kt, i), q, i, t)
                    if not reorder:
                        break
            assert best is not None, "scheduler deadlock"
            _, q, i, t = best
            o = ops[i]
            if o["kind"] == "dma":
                issue = 1000.0 if q == "pool" else 120.0
                qfree[q] = t + issue
                xs = max(t + issue, dma_free)
                dma_free = xs + o["cost"]
                fin[i] = dma_free + 2000.0
            else:
                qfree[q] = t + o["cost"]
                fin[i] = qfree[q] + (100.0 if q != "pe" else 250.0)
            o["start"] = t
            o["fin"] = fin[i]
            done[i] = True
            order[q].append(i)
            remaining -= 1
        self.order = order
        self.sim_time = max(f for f in fin if f is not None) if n else 0.0
        return order

    def emit(self, window=40, reorder=True):
        ops = self.ops
        order = self.schedule(window=window, reorder=reorder)
        duse = {}
        for q in self.QUEUES:
            seq = 0
            k = 0
            for i in order[q]:
                o = ops[i]
                if o["kind"] == "op":
                    seq += 1
                    o["tok"] = (q, seq)
                else:
                    pool = self.dpool[q]
                    slot = k % len(pool)
                    k += 1
                    key = ("d", q, slot)
                    prev = duse.get(key, 0)
                    o["prev_tok"] = (key, 16 * prev) if prev else None
                    duse[key] = prev + 1
                    o["tok"] = (key, 16 * (prev + 1))
        self._duse = duse

        def semobj(semkey):
            if isinstance(semkey, tuple):
                return self.dpool[semkey[1]][semkey[2]]
            return self.sem[semkey]

        def run(q, e):
            seen = {}
            for i in order[q]:
                o = ops[i]
                need = {}
                for d, raw in o["deps"].items():
                    od = ops[d]
                    if od["kind"] == "op" and od["q"] == q:
                        if q == "pe":
                            continue
                    sk, val = od["tok"]
                    if need.get(sk, 0) < val:
                        need[sk] = val
                if o["kind"] == "dma" and o["prev_tok"] is not None:
                    sk, val = o["prev_tok"]
                    if need.get(sk, 0) < val:
                        need[sk] = val
                for sk, val in need.items():
                    if seen.get(sk, 0) >= val:
                        continue
                    seen[sk] = val
                    e.wait_ge(semobj(sk), val)
                ins = o["fn"](e)
                sk, val = o["tok"]
                if o["kind"] == "op":
                    ins.then_inc(self.sem[q], 1)
                else:
                    ins.then_inc(semobj(sk), 16)
            if self.final_wait == q:
                for key, u in duse.items():
                    if seen.get(key, 0) < 16 * u:
                        e.wait_ge(semobj(key), 16 * u)

        with self.nc.Block() as block:
            @block.tensor
            def _(e):
                run("pe", e)

            @block.scalar
            def _(e):
                run("act", e)

            @block.vector
            def _(e):
                run("dve", e)

            @block.gpsimd
            def _(e):
                run("pool", e)

            @block.sync
            def _(e):
                run("sp", e)


class _Stop(Exception):
    pass


def build_program(dbg=(), stop=None):
    nc = bass.Bass("TRN2", target_bir_lowering=False)

    def din(name, shape, dt=F32):
        return nc.dram_tensor(name, list(shape), dt, kind="ExternalInput").ap()

    def dout(name, shape, dt=F32):
        return nc.dram_tensor(name, list(shape), dt, kind="ExternalOutput").ap()

    xin = din("xin", [2, 1024, D])
    ck_in = din("ck", [256, 128])
    cv_in = din("cv", [256, 128])
    cT_in = din("cT", [128, 16])
    badaT_in = din("b_adaT", [128, 48])
    gn_in = din("gn", [128, 16])
    gqk_in = din("gqk", [128, 640])
    sinks_in = din("sinks_bc", [128, 8])
    w_ada = din("w_ada", [D, 6 * D])
    w_in = din("w_in", [D, 3328])
    w_f = din("w_f", [512, D])
    w_ao = din("w_ao", [512, D])
    w_out = din("w_out", [D, D])
    w_up = din("w_up", [D, 2 * DFF])
    w_down = din("w_down", [DFF, D])
    ident_in = din("ident", [128, 128])
    dftc_in = din("dft_c", [1024, 1024])
    dfts_in = din("dft_s", [1024, 1024])
    dftc256_in = din("dft_c256", [256, 256])
    dfts256_in = din("dft_s256", [256, 256])
    chan_in = din("chan", [128, 4, 128])
    ropec_in = din("rope_cos", [1024, 64])
    ropes_in = din("rope_sin", [1024, 64])
    mask_in = din("mask3", [128, 384])

    y_out = dout("y", [2, 1024, D])
    nk_out = dout("nk", [1024, 128])
    nv_out = dout("nv", [1024, 128])

    dbg_outs = {}

    with ExitStack() as st:
        S = Sched(nc, st)
        _cnt = [0]

        def sb(name, shape, dt):
            _cnt[0] += 1
            nb = int(np.prod(shape[1:])) * (4 if dt in (F32, I32) else 2)
            _LAST.setdefault('alloc', []).append((name, nb))
            return st.enter_context(nc.sbuf_tensor(f"sb_{name}", list(shape), dt))

        x_res = sb("x_res", [128, NT, D], F32)
        hT = sb("hT", [128, 8, 1024], BF16)
        mT = sb("mT", [128, 8, 1024], BF16)
        AR = sb("arena", [128, 22 * 1024], BF16)
        NSLOT = 5
        wslot = [sb(f"w{i}", [128, 4096], BF16) for i in range(NSLOT)]
        v_aug = sb("v_aug", [128, NT, 2, 65], BF16)
        cv_aug = sb("cv_aug", [128, 2, 2, 65], BF16)
        ckT = sb("ckT", [128, 2, 2, 128], BF16)
        kTz = sb("kTz", [128, 2, 1024], BF16)
        dftc256 = sb("dftc256", [128, 2, 256], BF16)
        dfts256 = sb("dfts256", [128, 2, 256], BF16)
        chan = sb("chan", [128, 4, 128], BF16)
        identf = sb("identf", [128, 128], F32)
        identb = sb("identb", [128, 128], BF16)
        ropec = sb("ropec", [128, NT, 64], F32)
        ropes = sb("ropes", [128, NT, 64], F32)
        mask3 = sb("mask3", [128, 384], BF16)
        gqk = sb("gqk", [128, 640], F32)
        esink = sb("esink", [128, 8], F32)
        cT = sb("cT", [128, 16], F32)
        siluT = sb("siluT", [128, 16], BF16)
        badaT = sb("badaT", [128, 48], F32)
        gn = sb("gn", [128, 16], F32)
        modT = sb("modT", [128, 48, 2], F32)
        s1T = sb("s1T", [128, 8, 2], F32)
        s2T = sb("s2T", [128, 8, 2], F32)
        gtrep = sb("gtrep", [128, 128], F32)
        gt_bc = [sb("gt1bc", [128, D], F32), sb("gt2bc", [128, D], F32)]
        xnb = [sb(f"xnb{i}", [128, D], BF16) for i in range(2)]
        ss = sb("ss", [128, 2 * NT], F32)
        vv = sb("vv", [128, 2 * NT], F32)
        rstd = sb("rstd", [128, 2 * NT], F32)
        neghalf = sb("neghalf", [128, 16], F32)
        sq = [sb(f"sq{i}", [128, 640], F32) for i in range(2)]
        ssqk = [sb(f"ssqk{i}", [128, 10], F32) for i in range(2)]
        vqk = [sb(f"vqk{i}", [128, 10], F32) for i in range(2)]
        rqk = [sb(f"rqk{i}", [128, 10], F32) for i in range(2)]
        qkn = [sb(f"qkn{i}", [128, 640], F32) for i in range(2)]
        rt2 = sb("rt2", [128, 640], F32)
        qb = [sb(f"qb{i}", [128, 640], BF16) for i in range(2)]
        vf = [sb(f"vf{i}", [128, 128], F32) for i in range(2)]
        ckb = sb("ckb", [128, 2, 128], BF16)
        den = [sb(f"den{i}", [128, 4], F32) for i in range(4)]
        rec = [sb(f"rec{i}", [128, 4], F32) for i in range(4)]
        tg = [sb(f"tg{i}", [128, 1024], BF16) for i in range(2)]
        sa = [sb(f"sa{i}", [128, 512], BF16) for i in range(2)]

        pbank = [st.enter_context(nc.psum_tensor(f"pb{i}", [128, 512], F32)) for i in range(8)]
        _tnext = [0]
        _mnext = [0]

        def tbank():
            i = _tnext[0] % N_TBANKS
            _tnext[0] += 1
            return i

        MB = list(range(N_TBANKS, 8))
        _mpool = [list(MB)]

        def mbank():
            pool = _mpool[0]
            i = pool[_mnext[0] % len(pool)]
            _mnext[0] += 1
            return i

        def PB(i):
            return pbank[i][:]

        def PBb(i):
            return pbank[i][:].bitcast(BF16)

        def ak(off, n):
            return [("A", b) for b in range(off // 128, (off + n - 1) // 128 + 1)]

        def qT_off(c, t0, n):
            return c * 1024 + t0 * 128, n * 128
        KT_OFF = 4 * 1024
        MIX_OFF = 5 * 1024
        OT_OFF = 9 * 1024
        ZF_OFF = 13 * 1024
        PTL_OFF = 18 * 1024
        PTC_OFF = 20 * 1024

        def ptl_off(jb):
            k = jb % PTL_RING
            return PTL_OFF + k * 384 if k < 4 else 4 * 1024 + (k - 4) * 384

        def dbg_dump(name, ap, shape, reads, dt=F32):
            if name not in dbg:
                return
            o = dout("dbg_" + name, shape, dt)
            dbg_outs[name] = (shape, dt)
            S.dma("sp", o, ap, reads=reads)

        chunks = []

        def add_chunk(parts):
            chunks.append(parts)
            return len(chunks) - 1

        emitted = [0]
        slot_of = {}
        free_slots = list(range(NSLOT))

        def pump():
            while emitted[0] < len(chunks) and free_slots:
                i = emitted[0]
                s = free_slots.pop(0)
                slot_of[i] = s
                for (mk_dst, src) in chunks[i]:
                    S.dma("pool", mk_dst(wslot[s]), src, writes=[("W", s)], nbytes=4 * int(np.prod(src.shape)))
                emitted[0] += 1

        def prefetch(upto=None):
            pump()

        def use(i):
            pump()
            assert i in slot_of, f"weight chunk {i} not resident"
            return wslot[slot_of[i]], ("W", slot_of[i])

        def release(i):
            free_slots.append(slot_of[i])
            pump()

        def kview(rows_k, cols):
            return lambda w, rows_k=rows_k, cols=cols: w[:, 0:rows_k * cols].rearrange("p (k c) -> p k c", c=cols)

        def kview_off(off, rows_k, cols):
            return lambda w, off=off, rows_k=rows_k, cols=cols: w[:, off:off + rows_k * cols].rearrange("p (k c) -> p k c", c=cols)

        def wsrc(wap, r0, nr, c0, ncol):
            return wap[r0:r0 + nr, c0:c0 + ncol].rearrange("(k p) c -> p k c", p=128)

        ada_chunks = [add_chunk([(kview(8, 512), wsrc(w_ada, 0, 1024, cc * 512, 512))]) for cc in range(4)]
        pass_chunks = []
        for p in range(2):
            pc = {}
            pc["zf"] = add_chunk([(kview(8, 512), wsrc(w_in, 0, 1024, 0, 512))])
            pc["q"] = add_chunk([(kview(8, 512), wsrc(w_in, 0, 1024, 512, 512))])
            pc["kv"] = add_chunk([(kview(8, 256), wsrc(w_in, 0, 1024, 1024, 256))])
            if p == 0:
                ada_chunks += [add_chunk([(kview(8, 512), wsrc(w_ada, 0, 1024, cc * 512, 512))]) for cc in range(4, 12)]
            if p == 1:
                for nh in range(2):
                    pc["dc", nh] = add_chunk([(kview(8, 512), wsrc(dftc_in, 0, 1024, nh * 512, 512))])
                    pc["ds", nh] = add_chunk([(kview(8, 512), wsrc(dfts_in, 0, 1024, nh * 512, 512))])
            for j in range(2):
                pc["gf", j] = add_chunk([(kview(8, 512), wsrc(w_in, 0, 1024, 1280 + 512 * j, 512))])
                pc["ga", j] = add_chunk([(kview(8, 512), wsrc(w_in, 0, 1024, 2304 + 512 * j, 512))])
                pc["fo", j] = add_chunk([(kview(4, 512), wsrc(w_f, 0, 512, 512 * j, 512)),
                                         (kview_off(2048, 4, 512), wsrc(w_ao, 0, 512, 512 * j, 512))])
            for ch in range(2):
                pc["wo", ch] = add_chunk([(kview(8, 512), wsrc(w_out, 0, 1024, 512 * ch, 512))])
            for jc in range(11):
                pc["up", jc] = add_chunk([
                    (lambda w: w[:, :].rearrange("p (k c) -> p k c", c=512)[:, :, 0:256], wsrc(w_up, 0, 1024, jc * 256, 256)),
                    (lambda w: w[:, :].rearrange("p (k c) -> p k c", c=512)[:, :, 256:512], wsrc(w_up, 0, 1024, DFF + jc * 256, 256)),
                ])
            for cq in range(4):
                pc["dn", cq, 0] = add_chunk([(kview(11, 256), wsrc(w_down, 0, 1408, cq * 256, 256))])
                pc["dn", cq, 1] = add_chunk([(kview(11, 256), wsrc(w_down, 1408, 1408, cq * 256, 256))])
            pass_chunks.append(pc)

        S.dma("sp", cT[:], cT_in, writes=["cT"])
        S.dma("sp", badaT[:], badaT_in, writes=["badaT"])
        S.dma("sp", gn[:], gn_in, writes=["gn"])
        S.dma("sp", identf[:], ident_in, writes=["identf"])
        S.dma("sp", gqk[:], gqk_in, writes=["gqk"])
        S.dma("sp", esink[:], sinks_in, writes=["esink"])
        S.dma("sp", ropec[:], ropec_in.rearrange("(t p) d -> p t d", p=128), writes=["ropec"])
        S.dma("sp", ropes[:], ropes_in.rearrange("(t p) d -> p t d", p=128), writes=["ropes"])
        S.dma("pool", identb[:], ident_in, writes=["identb"])
        S.dma("pool", chan[:], chan_in, writes=["chan"])
        S.dma("pool", mask3[:], mask_in, writes=["mask3"])
        S.dma("pool", dftc256[:], dftc256_in.rearrange("(k p) n -> p k n", p=128), writes=["dftc256"])
        S.dma("pool", dfts256[:], dfts256_in.rearrange("(k p) n -> p k n", p=128), writes=["dfts256"])
        pump()
        S.op("pool", lambda e: e.memset(neghalf[:], -0.5), writes=["neghalf"], n=16)
        S.op("pool", lambda e: e.memset(kTz[64:128, 0, :], 0.0), writes=["kTz_z0"], n=512)
        S.op("pool", lambda e: e.memset(kTz[0:64, 1, :], 0.0), writes=["kTz_z1"], n=512)
        S.op("pool", lambda e: e.memset(ckT[:], 0.0), writes=["ckT"], n=256)
        S.op("pool", lambda e: e.memset(v_aug[:, :, :, 64:65], 1.0), writes=["v_aug_ones"], n=16)
        S.op("pool", lambda e: e.memset(cv_aug[:, :, :, 64:65], 1.0), writes=["cv_aug"], n=8)

        S.op("act", lambda e: e.activation(out=siluT[:], in_=cT[:], func=AF.Silu), reads=["cT"], writes=["siluT"], n=16)
        def adaln_part(ccs):
            for cc in ccs:
                W, wk = use(ada_chunks[cc])
                Wv = W[:, :].rearrange("p (k c) -> p k c", c=512)
                b = mbank()
                mp = PB(b)[:, 0:8].rearrange("p (c j) -> p c j", j=2)

                def mm(e, Wv=Wv, mp=mp):
                    for f in range(4):
                        for k in range(8):
                            ins = e.matmul(mp[:, f, :], lhsT=Wv[:, k, f * 128:(f + 1) * 128],
                                           rhs=siluT[:, 2 * k:2 * k + 2], start=(k == 0), stop=(k == 7))
                    return ins
                S.op("pe", mm, reads=[wk, "siluT"], writes=[("P", b)], cost=1000)
                release(ada_chunks[cc])
                S.op("dve", lambda e, mp=mp, cc=cc: e.tensor_tensor(
                    out=modT[:, 4 * cc:4 * cc + 4, :], in0=mp,
                    in1=badaT[:, 4 * cc:4 * cc + 4].unsqueeze(2).to_broadcast([128, 4, 2]), op=ALU.add),
                    reads=[("P", b), "badaT"], writes=[("modT", cc)], n=8)
                if cc == 3:
                    S.op("dve", lambda e: e.scalar_tensor_tensor(
                        out=s1T[:], in0=modT[:, 8:16, :], scalar=1.0,
                        in1=gn[:, 0:8].unsqueeze(2).to_broadcast([128, 8, 2]), op0=ALU.add, op1=ALU.mult),
                        reads=[("modT", 2), ("modT", 3), "gn"], writes=["s1T"], n=16)
                if cc == 9:
                    S.op("dve", lambda e: e.scalar_tensor_tensor(
                        out=s2T[:], in0=modT[:, 32:40, :], scalar=1.0,
                        in1=gn[:, 8:16].unsqueeze(2).to_broadcast([128, 8, 2]), op0=ALU.add, op1=ALU.mult),
                        reads=[("modT", 8), ("modT", 9), "gn"], writes=["s2T"], n=16)

        adaln_part(range(4))

        S.op("act", lambda e: e.activation(out=esink[:], in_=esink[:], func=AF.Exp), reads=["esink"], writes=["esink"], n=8)

        S.dma("pool", ckb[:], ck_in.rearrange("(b p) d -> p b d", p=128), writes=["ckb"])
        tb = tbank()

        def trck(e, tb=tb):
            for b in range(2):
                ins = e.transpose(out=PBb(tb)[:, b * 128:(b + 1) * 128], in_=ckb[:, b, :], identity=identb[:])
            return ins
        S.op("pe", trck, reads=["ckb", "identb"], writes=[("P", tb)], cost=300)
        S.op("dve", lambda e, tb=tb: e.tensor_copy(out=ckT[0:64, 0, :, :], in_=PBb(tb)[0:64, 0:256].rearrange("p (b k) -> p b k", k=128)),
             reads=[("P", tb), "ckT"], writes=["ckT"], n=256)
        S.op("dve", lambda e, tb=tb: e.tensor_copy(out=ckT[64:128, 1, :, :], in_=PBb(tb)[64:128, 0:256].rearrange("p (b k) -> p b k", k=128)),
             reads=[("P", tb), "ckT"], writes=["ckT"], n=256)
        for b_ in range(2):
            S.dma("pool", cv_aug[:, b_, :, 0:64], cv_in[b_ * 128:(b_ + 1) * 128, :].rearrange("p (k d) -> p k d", d=64),
                  reads=["cv_aug"], writes=["cv_aug"])

        def build_gt_bc(p, which, dst, scale):
            base = which * 8
            for half in range(2):
                b = mbank()
                for kk in range(4):
                    k = half * 4 + kk
                    S.op("dve", lambda e, k=k: e.tensor_scalar(
                        out=gtrep[:], in0=modT[:, base + k, p:p + 1].to_broadcast([128, 128]),
                        scalar1=scale, scalar2=None, op0=ALU.mult),
                        reads=[("modT", 2 * which), ("modT", 2 * which + 1)], writes=["gtrep"], n=128)
                    S.op("pe", lambda e, b=b, kk=kk: e.transpose(out=PB(b)[:, kk * 128:(kk + 1) * 128], in_=gtrep[:], identity=identf[:]),
                         reads=["gtrep", "identf"], writes=[("P", b)], cost=450)
                S.op("act", lambda e, b=b, half=half: e.copy(out=dst[:, half * 512:(half + 1) * 512], in_=PB(b)),
                     reads=[("P", b)], writes=[("gtbc", id(dst), half)])

        def norm_tile(p, t, which, sT, bidx):
            col = which * NT + t
            S.op("act", lambda e: e.activation(out=xnb[t % 2][:], in_=x_res[:, t, :], func=AF.Square,
                                               accum_out=ss[:, col:col + 1]),
                 reads=[("x", t)], writes=[("xnb", t % 2), ("ss", col)], n=1024)
            S.op("pool", lambda e: e.tensor_scalar(out=vv[:, col:col + 1], in0=ss[:, col:col + 1],
                                                   scalar1=1.0 / D, scalar2=EPS, op0=ALU.mult, op1=ALU.add),
                 reads=[("ss", col)], writes=[("vv", col)], cost=200)
            S.op("pool", lambda e: e.tensor_tensor(out=rstd[:, col:col + 1], in0=vv[:, col:col + 1],
                                                   in1=neghalf[:, 0:1], op=ALU.pow),
                 reads=[("vv", col), "neghalf"], writes=[("rstd", col)], cost=780)
            xb = xnb[t % 2]
            xk = ("xnb", t % 2)
            S.op("dve", lambda e: e.tensor_scalar(out=xb[:], in0=x_res[:, t, :], scalar1=rstd[:, col:col + 1],
                                                  scalar2=None, op0=ALU.mult),
                 reads=[("x", t), ("rstd", col)], writes=[xk], cost=713)
            tbA, tbB = tbank(), tbank()

            def tr(e):
                for k in range(8):
                    tb_ = tbA if k < 5 else tbB
                    kk_ = k if k < 5 else k - 5
                    ins = e.transpose(out=PBb(tb_)[:, kk_ * 128:(kk_ + 1) * 128], in_=xb[:, k * 128:(k + 1) * 128], identity=identb[:])
                return ins
            S.op("pe", tr, reads=[xk, "identb"], writes=[("P", tbA), ("P", tbB)], cost=700)
            mk_ = (["s1T", ("modT", 0), ("modT", 1)] if which == 0 else ["s2T", ("modT", 6), ("modT", 7)])
            for k in range(8):
                if k < 5:
                    S.op("dve", lambda e, k=k: e.tensor_scalar(
                        out=hT[:, k, t * 128:(t + 1) * 128], in0=PBb(tbA)[:, k * 128:(k + 1) * 128],
                        scalar1=sT[:, k, p:p + 1], scalar2=modT[:, bidx + k, p:p + 1], op0=ALU.mult, op1=ALU.add),
                        reads=[("P", tbA)] + mk_, writes=[("hT", t, k)], cost=280)
                else:
                    S.op("act", lambda e, k=k: e.activation(
                        out=hT[:, k, t * 128:(t + 1) * 128], in_=PBb(tbB)[:, (k - 5) * 128:(k - 4) * 128],
                        func=AF.Identity, scale=sT[:, k, p:p + 1], bias=modT[:, bidx + k, p:p + 1]),
                        reads=[("P", tbB)] + mk_, writes=[("hT", t, k)], cost=480)

        def hT_keys(t):
            return [("hT", t, k) for k in range(8)]

        def stop_at(name, p):
            if stop == (name, p):
                raise _Stop()

        try:
          for p in range(2):
            stop_at('p0', p)
            pc = pass_chunks[p]
            latent = (p == 1)
            if p == 1:
                build_gt_bc(p, 2, gt_bc[0], 0.5)
                build_gt_bc(p, 5, gt_bc[1], 1.0)
            gt1k = [("gtbc", id(gt_bc[0]), h) for h in range(2)]
            gt2k = [("gtbc", id(gt_bc[1]), h) for h in range(2)]

            for t in range(NT):
                S.dma("sp", x_res[:, t, :], xin[p, t * 128:(t + 1) * 128, :], writes=[("x", t)], nbytes=512 * 1024,
                      reads=([("modT", 2)] if (p == 0 and t >= 2) else []))
            Wzf, kzf = use(pc["zf"])
            Wq, kq = use(pc["q"])
            Wkv, kkv = use(pc["kv"])
            Wzfv = Wzf[:, :].rearrange("p (k c) -> p k c", c=512)
            Wqv = Wq[:, :].rearrange("p (k c) -> p k c", c=512)
            Wkvv = Wkv[:, 0:2048].rearrange("p (k c) -> p k c", c=256)
            norm_tile(p, 0, 0, s1T, 0)
            for t in range(NT):
                if t + 1 < NT:
                    norm_tile(p, t + 1, 0, s1T, 0)
                if p == 0:
                    adaln_part([4 + t])
                tcols = slice(t * 128, (t + 1) * 128)
                bzf, bq, bkv = mbank(), mbank(), mbank()

                def mmz(e, b=bzf, Wv=Wzfv, n=512, tcols=tcols):
                    for k in range(8):
                        ins = e.matmul(PB(b)[:, 0:n], lhsT=hT[:, k, tcols], rhs=Wv[:, k, :], start=(k == 0), stop=(k == 7))
                    return ins
                S.op("pe", mmz, reads=hT_keys(t) + [kzf], writes=[("P", bzf)], cost=1900)
                S.op("pe", lambda e, b=bq, Wv=Wqv, tcols=tcols: mmz(e, b, Wv, 512, tcols), reads=hT_keys(t) + [kq], writes=[("P", bq)], cost=1900)
                S.op("pe", lambda e, b=bkv, Wv=Wkvv, tcols=tcols: mmz(e, b, Wv, 256, tcols), reads=hT_keys(t) + [kkv], writes=[("P", bkv)], cost=870)
                zoff = ZF_OFF + t * 512
                S.op("act", lambda e, b=bzf, zoff=zoff: e.copy(out=AR[:, zoff:zoff + 512], in_=PB(b)),
                     reads=[("P", bzf)], writes=ak(zoff, 512))
                i2 = t % 2
                S.op("act", lambda e, b=bq, i2=i2: e.activation(out=sq[i2][:, 0:512], in_=PB(b), func=AF.Square),
                     reads=[("P", bq)], writes=[("sq", i2, 0)])
                S.op("act", lambda e, b=bkv, i2=i2: e.activation(out=sq[i2][:, 512:640], in_=PB(b)[:, 0:128], func=AF.Square),
                     reads=[("P", bkv)], writes=[("sq", i2, 1)], n=128)
                S.op("dve", lambda e, b=bq, i2=i2: e.tensor_tensor(out=qkn[i2][:, 0:512], in0=PB(b), in1=gqk[:, 0:512], op=ALU.mult),
                     reads=[("P", bq), "gqk"], writes=[("qkn", i2, 0)])
                S.op("dve", lambda e, b=bkv, i2=i2: e.tensor_tensor(out=qkn[i2][:, 512:640], in0=PB(b)[:, 0:128], in1=gqk[:, 512:640], op=ALU.mult),
                     reads=[("P", bkv), "gqk"], writes=[("qkn", i2, 1)], n=128)
                S.op("dve", lambda e, i2=i2: e.tensor_reduce(out=ssqk[i2][:], in_=sq[i2][:].rearrange("p (h d) -> p h d", d=64),
                                                             axis=AX.X, op=ALU.add),
                     reads=[("sq", i2, 0), ("sq", i2, 1)], writes=[("ssqk", i2)], n=640)
                S.op("pool", lambda e, i2=i2: e.tensor_scalar(out=vqk[i2][:], in0=ssqk[i2][:], scalar1=1.0 / 64, scalar2=EPS,
                                                              op0=ALU.mult, op1=ALU.add),
                     reads=[("ssqk", i2)], writes=[("vqk", i2)], cost=225)
                S.op("pool", lambda e, i2=i2: e.tensor_tensor(out=rqk[i2][:], in0=vqk[i2][:], in1=neghalf[:, 0:10], op=ALU.pow),
                     reads=[("vqk", i2), "neghalf"], writes=[("rqk", i2)], cost=1974)
                S.op("pool", lambda e, i2=i2: e.tensor_tensor(
                    out=qkn[i2][:].rearrange("p (h d) -> p h d", d=64), in0=qkn[i2][:].rearrange("p (h d) -> p h d", d=64),
                    in1=rqk[i2][:, 0:10].unsqueeze(2).to_broadcast([128, 10, 64]), op=ALU.mult),
                    reads=[("qkn", i2, 0), ("qkn", i2, 1), ("rqk", i2)], writes=[("qkn", i2, 0), ("qkn", i2, 1)], n=640)
                qkk = [("qkn", i2, 0), ("qkn", i2, 1)]
                S.op("act", lambda e, b=bkv, t=t: e.copy(out=v_aug[:, t, :, 0:64], in_=PB(b)[:, 128:256].rearrange("p (k d) -> p k d", d=64)),
                     reads=[("P", bkv), "v_aug_ones"], writes=[("v_aug", t)], n=128)
                if not latent:
                    S.op("act", lambda e, b=bkv, i2=i2: e.copy(out=vf[i2][:], in_=PB(b)[:, 128:256]),
                         reads=[("P", bkv)], writes=[("vf", i2)], n=128)
                    S.dma("sp", nv_out[t * 128:(t + 1) * 128, :], vf[i2][:], reads=[("vf", i2)])
                    S.dma("sp", nk_out[t * 128:(t + 1) * 128, :], qkn[i2][:, 512:640], reads=qkk)
                    src = qkn[i2]
                    srck = qkk
                else:
                    S.op("dve", lambda e, i2=i2, t=t: e.tensor_tensor(
                        out=sq[i2][:].rearrange("p (h d) -> p h d", d=64), in0=qkn[i2][:].rearrange("p (h d) -> p h d", d=64),
                        in1=ropec[:, t, :].unsqueeze(1).to_broadcast([128, 10, 64]), op=ALU.mult),
                        reads=qkk + ["ropec"], writes=[("sq", i2, 0), ("sq", i2, 1)], n=640)
                    for hf in range(2):
                        S.op("pool", lambda e, i2=i2, t=t, hf=hf: e.tensor_tensor(
                            out=rt2[:].rearrange("p (h a f d) -> p h a f d", a=2, f=2, d=16)[:, :, :, hf, :],
                            in0=qkn[i2][:].rearrange("p (h a f d) -> p h a f d", a=2, f=2, d=16)[:, :, :, 1 - hf, :],
                            in1=ropes[:, t, :].rearrange("p (a f d) -> p a f d", a=2, f=2)[:, :, hf, :].unsqueeze(1).to_broadcast([128, 10, 2, 16]),
                            op=ALU.mult),
                            reads=qkk + ["ropes"], writes=[("rt2", hf)], n=320)
                    S.op("dve", lambda e, i2=i2: e.tensor_tensor(out=sq[i2][:], in0=sq[i2][:], in1=rt2[:], op=ALU.add),
                         reads=[("sq", i2, 0), ("sq", i2, 1), ("rt2", 0), ("rt2", 1)], writes=[("sq", i2, 0), ("sq", i2, 1)], n=640)
                    src = sq[i2]
                    srck = [("sq", i2, 0), ("sq", i2, 1)]
                S.op("act", lambda e, i2=i2, src=src: e.copy(
                    out=qb[i2][:, 0:512].rearrange("p (g k d) -> p k g d", k=2, d=64),
                    in_=src[:, 0:512].rearrange("p (k g d) -> p k g d", k=2, d=64)),
                    reads=srck, writes=[("qb", i2, 0)])
                S.op("act", lambda e, i2=i2, src=src: e.copy(out=qb[i2][:, 512:640], in_=src[:, 512:640]),
                     reads=srck, writes=[("qb", i2, 1)], n=128)
                tb = tbank()

                def trq(e, i2=i2, tb=tb):
                    for c in range(5):
                        ins = e.transpose(out=PBb(tb)[:, c * 128:(c + 1) * 128], in_=qb[i2][:, c * 128:(c + 1) * 128], identity=identb[:])
                    return ins
                S.op("pe", trq, reads=[("qb", i2, 0), ("qb", i2, 1), "identb"], writes=[("P", tb)], cost=650)
                qkeys = []
                for c in range(4):
                    qkeys += ak(c * 1024 + t * 128, 128)
                S.op("dve", lambda e, tb=tb, t=t: e.tensor_copy(
                    out=AR[:, 0:4096].rearrange("p (c n) -> p c n", n=1024)[:, :, t * 128:(t + 1) * 128],
                    in_=PBb(tb)[:, 0:512].rearrange("p (c n) -> p c n", n=128)),
                    reads=[("P", tb)], writes=qkeys, cost=430)
                S.op("dve", lambda e, tb=tb, t=t: e.tensor_copy(out=kTz[0:64, 0, t * 128:(t + 1) * 128], in_=PBb(tb)[0:64, 512:640]),
                     reads=[("P", tb), "kTz_z0"], writes=[("kTz", t, 0)], cost=250)
                S.op("dve", lambda e, tb=tb, t=t: e.tensor_copy(out=kTz[64:128, 1, t * 128:(t + 1) * 128], in_=PBb(tb)[64:128, 512:640]),
                     reads=[("P", tb), "kTz_z1"], writes=[("kTz", t, 1)], cost=250)

            release(pc["zf"])
            release(pc["q"])
            release(pc["kv"])
            dbg_dump("hT", hT[:], [128, 8, 1024], [k for t in range(NT) for k in hT_keys(t)], BF16) if p == dbg_pass(dbg) else None
            dbg_dump("AR1", AR[:], [128, 22 * 1024], [("A", b) for b in range(176)], BF16) if p == dbg_pass(dbg) else None
            dbg_dump("v_aug", v_aug[:], [128, NT, 2, 65], [("v_aug", t) for t in range(NT)], BF16) if p == dbg_pass(dbg) else None

            if p == 0:
                build_gt_bc(p, 2, gt_bc[0], 0.5)
                build_gt_bc(p, 5, gt_bc[1], 1.0)
            stop_at('p1', p)
            Y12_OFF = 17 * 1024
            def fourier_ctx(s):
                for gp in range(2):
                    bo = mbank()
                    for gg in range(2):
                        g = gp * 2 + gg
                        b1 = mbank()

                        def mmy(e, b1=b1, g=g, s=s):
                            for (half, Dm) in ((0, dftc256), (1, dfts256)):
                                for tt in range(2):
                                    zoff = ZF_OFF + (2 * s + tt) * 512 + g * 128
                                    ins = e.matmul(PB(b1)[:, half * 256:(half + 1) * 256], lhsT=AR[:, zoff:zoff + 128],
                                                   rhs=Dm[:, tt, :], start=(tt == 0), stop=(tt == 1))
                            return ins
                        S.op("pe", mmy, reads=ak(ZF_OFF + 2 * s * 512, 1024) + ["dftc256", "dfts256"], writes=[("P", b1)], cost=520)
                        yo = Y12_OFF + (g % 2) * 512
                        S.op("act", lambda e, b1=b1, yo=yo: e.copy(out=AR[:, yo:yo + 512], in_=PB(b1)),
                             reads=[("P", b1)], writes=ak(yo, 512))

                        def mmc(e, bo=bo, gg=gg, yo=yo):
                            e.matmul(PB(bo)[:, gg * 256:(gg + 1) * 256], lhsT=chan[:, 0, :], rhs=AR[:, yo:yo + 256], start=True, stop=False)
                            return e.matmul(PB(bo)[:, gg * 256:(gg + 1) * 256], lhsT=chan[:, 1, :], rhs=AR[:, yo + 256:yo + 512], start=False, stop=True)
                        S.op("pe", mmc, reads=ak(yo, 512) + ["chan"], writes=[("P", bo)], cost=260)
                    mk = []
                    for gg in range(2):
                        mk += ak(MIX_OFF + (gp * 2 + gg) * 1024 + s * 256, 256)
                    S.op("dve", lambda e, bo=bo, gp=gp, s=s: e.tensor_copy(
                        out=AR[:, MIX_OFF + gp * 2048:MIX_OFF + (gp + 1) * 2048].rearrange("p (g n) -> p g n", n=1024)[:, :, s * 256:(s + 1) * 256],
                        in_=PB(bo).rearrange("p (g n) -> p g n", n=256)),
                        reads=[("P", bo)], writes=mk, n=512)
            if not latent:
                _mpool[0] = list(MB) + CTX_EXTRA_BANKS
                if not CTX_INTERLEAVE:
                    for s in range(4):
                        fourier_ctx(s)
            else:
                _mpool[0] = list(MB) + LATF_EXTRA_BANKS
                for nh in range(2):
                    Wc, kwc = use(pc["dc", nh])
                    Ws, kws = use(pc["ds", nh])
                    Wcv = Wc[:, :].rearrange("p (k c) -> p k c", c=512)
                    Wsv = Ws[:, :].rearrange("p (k c) -> p k c", c=512)
                    for g in range(4):
                        b1, b2 = mbank(), mbank()
                        for (b, Dm, dk) in ((b1, Wcv, kwc), (b2, Wsv, kws)):
                            def mmy(e, b=b, Dm=Dm, g=g):
                                for tt in range(8):
                                    zoff = ZF_OFF + tt * 512 + g * 128
                                    ins = e.matmul(PB(b), lhsT=AR[:, zoff:zoff + 128], rhs=Dm[:, tt, :],
                                                   start=(tt == 0), stop=(tt == 7))
                                return ins
                            S.op("pe", mmy, reads=ak(ZF_OFF, 4096) + [dk], writes=[("P", b)], cost=1900)
                        S.op("act", lambda e, b1=b1: e.copy(out=AR[:, Y12_OFF:Y12_OFF + 512], in_=PB(b1)),
                             reads=[("P", b1)], writes=ak(Y12_OFF, 512))
                        S.op("dve", lambda e, b2=b2: e.tensor_copy(out=AR[:, Y12_OFF + 512:Y12_OFF + 1024], in_=PB(b2)),
                             reads=[("P", b2)], writes=ak(Y12_OFF + 512, 512))
                        bo = mbank()

                        def mmc(e, bo=bo):
                            e.matmul(PB(bo), lhsT=chan[:, 2, :], rhs=AR[:, Y12_OFF:Y12_OFF + 512], start=True, stop=False)
                            return e.matmul(PB(bo), lhsT=chan[:, 3, :], rhs=AR[:, Y12_OFF + 512:Y12_OFF + 1024], start=False, stop=True)
                        S.op("pe", mmc, reads=ak(Y12_OFF, 1024) + ["chan"], writes=[("P", bo)], cost=480)
                        moff = MIX_OFF + g * 1024 + nh * 512
                        S.op("act", lambda e, bo=bo, moff=moff: e.copy(out=AR[:, moff:moff + 512], in_=PB(bo)),
                             reads=[("P", bo)], writes=ak(moff, 512))
                    release(pc["dc", nh])
                    release(pc["ds", nh])

            stop_at('f', p)
            def o_off(t):
                return ZF_OFF + t * 512

            def evac_pv(bank, ntile, t0, h, di):
                pv = PB(bank)[:, 0:ntile * 65].rearrange("p (n d) -> p n d", d=65)
                S.op("dve", lambda e: e.tensor_scalar(out=den[di][:, 0:ntile], in0=pv[:, :, 64], scalar1=esink[:, h:h + 1],
                                                      scalar2=None, op0=ALU.add),
                     reads=[("P", bank), "esink"], writes=[("den", di)], cost=220)
                S.op("dve", lambda e: e.reciprocal(out=rec[di][:, 0:ntile], in_=den[di][:, 0:ntile]),
                     reads=[("den", di)], writes=[("rec", di)], cost=170)
                okeys = []
                for i in range(ntile):
                    okeys += ak(o_off(t0 + i) + h * 64, 64)
                S.op("dve", lambda e: e.tensor_tensor(
                    out=AR[:, o_off(t0):o_off(t0) + ntile * 512].rearrange("p (n c) -> p n c", c=512)[:, :, h * 64:(h + 1) * 64],
                    in0=pv[:, :, 0:64], in1=rec[di][:, 0:ntile].unsqueeze(2).to_broadcast([128, ntile, 64]), op=ALU.mult),
                    reads=[("P", bank), ("rec", di)], writes=okeys, n=200)

            cntl = [0]
            def attn_ctx(s):
                for kv in range(2):
                    kp = slice(kv * 64, (kv + 1) * 64)
                    for g in range(4):
                        h = kv * 4 + g
                        bs = mbank()

                        def mms(e, bs=bs, kv=kv, g=g, s=s):
                            for jb in range(2):
                                ko = (2 * s + jb) * 128
                                qo = g * 1024 + s * 256
                                ins = e.matmul(PB(bs)[:, jb * 256:(jb + 1) * 256], lhsT=kTz[:, kv, ko:ko + 128], rhs=AR[:, qo:qo + 256],
                                               start=True, stop=True)
                            return ins
                        S.op("pe", mms, reads=[("kTz", 2 * s, kv), ("kTz", 2 * s + 1, kv)] + ak(g * 1024 + s * 256, 256), writes=[("P", bs)], cost=260)
                        pi = cntl[0] % 4
                        cntl[0] += 1
                        po = PTC_OFF + pi * 512
                        S.op("act", lambda e, bs=bs, po=po: e.activation(out=AR[:, po:po + 512], in_=PB(bs), func=AF.Exp, scale=0.125),
                             reads=[("P", bs)], writes=ak(po, 512))
                        bp = mbank()

                        def mmpv(e, bp=bp, po=po, s=s, kv=kv):
                            for qt in range(2):
                                for jb in range(2):
                                    ins = e.matmul(PB(bp)[:, qt * 65:(qt + 1) * 65],
                                                   lhsT=AR[:, po + jb * 256 + qt * 128:po + jb * 256 + (qt + 1) * 128],
                                                   rhs=v_aug[:, 2 * s + jb, kv, :], start=(jb == 0), stop=(jb == 1))
                            return ins
                        S.op("pe", mmpv, reads=ak(po, 512) + [("v_aug", 2 * s), ("v_aug", 2 * s + 1)], writes=[("P", bp)], cost=420)
                        evac_pv(bp, 2, 2 * s, h, cntl[0] % 4)
            if not latent:
                for s in range(4):
                    if CTX_INTERLEAVE:
                        fourier_ctx(s)
                    attn_ctx(s)
            else:
                _mpool[0] = MB[2:] + LAT_EXTRA_BANKS
                for kv in range(2):
                    kp = slice(kv * 64, (kv + 1) * 64)
                    for g in range(4):
                        h = kv * 4 + g
                        def cache_scores(qh, kv=kv, g=g):
                            for cb in range(2):
                                bs = mbank()
                                qo = g * 1024 + qh * 512
                                S.op("pe", lambda e, bs=bs, cb=cb, qo=qo, kv=kv: e.matmul(
                                    PB(bs), lhsT=ckT[:, kv, cb, :], rhs=AR[:, qo:qo + 512], start=True, stop=True),
                                    reads=["ckT"] + ak(qo, 512), writes=[("P", bs)], cost=240)
                                po = PTC_OFF + (qh * 2 + cb) * 512
                                S.op("act", lambda e, bs=bs, po=po: e.activation(out=AR[:, po:po + 512], in_=PB(bs), func=AF.Exp, scale=0.125),
                                     reads=[("P", bs)], writes=ak(po, 512))
                        cache_scores(0)
                        bpv = MB[0:2]

                        def pv_group(qt, kv=kv, g=g, bpv=bpv):
                            jbs = [jb for jb in (qt - 1, qt, qt + 1) if 0 <= jb < 8]
                            reads = []
                            for jb in jbs:
                                reads += ak(ptl_off(jb), 384) + [("v_aug", jb)]
                            reads += ak(PTC_OFF, 2048) + ["cv_aug"]
                            bank = bpv[qt // 4]

                            def fn(e):
                                n = len(jbs) + 2
                                i = 0
                                for jb in jbs:
                                    lo = ptl_off(jb) + (qt - jb + 1) * 128
                                    e.matmul(PB(bank)[:, (qt % 4) * 65:(qt % 4 + 1) * 65], lhsT=AR[:, lo:lo + 128],
                                             rhs=v_aug[:, jb, kv, :], start=(i == 0), stop=False)
                                    i += 1
                                for cb in range(2):
                                    lo = PTC_OFF + ((qt // 4) * 2 + cb) * 512 + (qt % 4) * 128
                                    ins = e.matmul(PB(bank)[:, (qt % 4) * 65:(qt % 4 + 1) * 65], lhsT=AR[:, lo:lo + 128],
                                                   rhs=cv_aug[:, cb, kv, :], start=False, stop=(cb == 1))
                                return ins
                            S.op("pe", fn, reads=reads, writes=[("P", bank)], cost=520)

                        for jb in range(8):
                            if jb == CACHE_QH1_AT:
                                cache_scores(1)
                            qlo, qhi = max(jb - 1, 0), min(jb + 1, 7)
                            nq = (qhi - qlo + 1) * 128
                            c0 = (qlo - (jb - 1)) * 128
                            bs = mbank()
                            ko = jb * 128
                            qo = g * 1024 + qlo * 128
                            S.op("pe", lambda e, bs=bs, ko=ko, qo=qo, nq=nq, c0=c0, kv=kv: e.matmul(
                                PB(bs)[:, c0:c0 + nq], lhsT=kTz[:, kv, ko:ko + 128], rhs=AR[:, qo:qo + nq], start=True, stop=True),
                                reads=[("kTz", jb, kv)] + ak(qo, nq), writes=[("P", bs)], cost=190)
                            po = ptl_off(jb)
                            S.op("act", lambda e, bs=bs, po=po, c0=c0, nq=nq: e.activation(
                                out=AR[:, po + c0:po + c0 + nq], in_=PB(bs)[:, c0:c0 + nq], func=AF.Exp, scale=0.125),
                                reads=[("P", bs)], writes=ak(po, 384), n=384)
                            if 1 <= jb <= 6:
                                S.op("pool", lambda e, po=po: e.tensor_tensor(
                                    out=AR[:, po:po + 384].rearrange("p (b n) -> p b n", n=128)[:, 0:3:2, :],
                                    in0=AR[:, po:po + 384].rearrange("p (b n) -> p b n", n=128)[:, 0:3:2, :],
                                    in1=mask3[:, :].rearrange("p (b n) -> p b n", n=128)[:, 0:3:2, :], op=ALU.mult),
                                    reads=ak(po, 384) + ["mask3"], writes=ak(po, 384), n=256)
                            else:
                                mb_ = 2 if jb == 0 else 0
                                S.op("pool", lambda e, po=po, mb_=mb_: e.tensor_tensor(
                                    out=AR[:, po + mb_ * 128:po + (mb_ + 1) * 128], in0=AR[:, po + mb_ * 128:po + (mb_ + 1) * 128],
                                    in1=mask3[:, mb_ * 128:(mb_ + 1) * 128], op=ALU.mult),
                                    reads=ak(po, 384) + ["mask3"], writes=ak(po, 384), n=128)
                            if jb >= PV_LAG:
                                pv_group(jb - PV_LAG)
                                if jb - PV_LAG == 3:
                                    evac_pv(bpv[0], 4, 0, h, (2 * h) % 4)
                        for qt_ in range(8 - PV_LAG, 8):
                            pv_group(qt_)
                            if qt_ == 3:
                                evac_pv(bpv[0], 4, 0, h, (2 * h) % 4)
                        evac_pv(bpv[1], 4, 4, h, (2 * h + 1) % 4)
                _mpool[0] = list(MB)

            dbg_dump("o_tm", AR[:, ZF_OFF:ZF_OFF + 4096], [128, 4096], ak(ZF_OFF, 4096), BF16) if p == dbg_pass(dbg) else None

            _mpool[0] = list(MB)
            for t in range(NT):
                tb = tbank()

                def tro(e, tb=tb, t=t):
                    for c in range(4):
                        ins = e.transpose(out=PBb(tb)[:, c * 128:(c + 1) * 128], in_=AR[:, o_off(t) + c * 128:o_off(t) + (c + 1) * 128],
                                          identity=identb[:])
                    return ins
                S.op("pe", tro, reads=ak(o_off(t), 512) + ["identb"], writes=[("P", tb)], cost=520)
                okeys = []
                for c in range(4):
                    okeys += ak(OT_OFF + c * 1024 + t * 128, 128)
                S.op("act", lambda e, tb=tb, t=t: e.copy(
                    out=AR[:, OT_OFF:OT_OFF + 4096].rearrange("p (c n) -> p c n", n=1024)[:, :, t * 128:(t + 1) * 128],
                    in_=PBb(tb)[:, 0:512].rearrange("p (c n) -> p c n", n=128)),
                    reads=[("P", tb)], writes=okeys)

            dbg_dump("AR2", AR[:], [128, 22 * 1024], [("A", b) for b in range(176)], BF16) if p == dbg_pass(dbg) else None

            stop_at('a', p)
            _mpool[0] = list(MB) + P3_EXTRA_BANKS
            for j in range(2):
                Wgf, kgf = use(pc["gf", j])
                Wga, kga = use(pc["ga", j])
                Wfo, kfo = use(pc["fo", j])
                Wgfv = Wgf[:, :].rearrange("p (k c) -> p k c", c=512)
                Wgav = Wga[:, :].rearrange("p (k c) -> p k c", c=512)
                Wfov = Wfo[:, :].rearrange("p (k c) -> p k c", c=512)
                for fc in range(4):
                    f = j * 4 + fc
                    fcols = slice(fc * 128, (fc + 1) * 128)
                    for stl in range(2):
                        ncols = slice(stl * 512, (stl + 1) * 512)
                        hk = [k for t in range(4 * stl, 4 * stl + 4) for k in hT_keys(t)]
                        bg1, bg2, by1, by2 = mbank(), mbank(), mbank(), mbank()

                        def mmg(e, b, Wv, fcols=fcols, ncols=ncols):
                            for k in range(8):
                                ins = e.matmul(PB(b), lhsT=Wv[:, k, fcols], rhs=hT[:, k, ncols], start=(k == 0), stop=(k == 7))
                            return ins
                        S.op("pe", lambda e, b=bg1, Wv=Wgfv, mmg=mmg: mmg(e, b, Wv), reads=hk + [kgf], writes=[("P", bg1)], cost=1810)
                        S.op("pe", lambda e, b=bg2, Wv=Wgav, mmg=mmg: mmg(e, b, Wv), reads=hk + [kga], writes=[("P", bg2)], cost=1810)

                        def mmy2(e, b, koff, aoff, fcols=fcols, stl=stl, Wfov=Wfov):
                            for k in range(4):
                                o = aoff + k * 1024 + stl * 512
                                ins = e.matmul(PB(b), lhsT=Wfov[:, koff + k, fcols], rhs=AR[:, o:o + 512], start=(k == 0), stop=(k == 3))
                            return ins
                        mixk = [x for k in range(4) for x in ak(MIX_OFF + k * 1024 + stl * 512, 512)]
                        otk = [x for k in range(4) for x in ak(OT_OFF + k * 1024 + stl * 512, 512)]
                        S.op("pe", lambda e, b=by1, mmy2=mmy2: mmy2(e, b, 0, MIX_OFF), reads=mixk + [kfo], writes=[("P", by1)], cost=940)
                        S.op("pe", lambda e, b=by2, mmy2=mmy2: mmy2(e, b, 4, OT_OFF), reads=otk + [kfo], writes=[("P", by2)], cost=940)
                        ti = (f * 2 + stl) % 2
                        S.op("act", lambda e, b=bg1, ti=ti: e.activation(out=tg[ti][:, 0:512], in_=PB(b), func=AF.Tanh, scale=0.5),
                             reads=[("P", bg1)], writes=[("tg", ti, 0)])
                        S.op("act", lambda e, b=bg2, ti=ti: e.activation(out=tg[ti][:, 512:1024], in_=PB(b), func=AF.Tanh, scale=0.5),
                             reads=[("P", bg2)], writes=[("tg", ti, 1)])
                        S.op("dve", lambda e, b=by1, ti=ti: e.scalar_tensor_tensor(
                            out=tg[ti][:, 0:512], in0=tg[ti][:, 0:512], scalar=1.0, in1=PB(b), op0=ALU.add, op1=ALU.mult),
                            reads=[("tg", ti, 0), ("P", by1)], writes=[("tg", ti, 0)])
                        S.op("dve", lambda e, b=by2, ti=ti: e.scalar_tensor_tensor(
                            out=tg[ti][:, 512:1024], in0=tg[ti][:, 512:1024], scalar=1.0, in1=PB(b), op0=ALU.add, op1=ALU.mult),
                            reads=[("tg", ti, 1), ("P", by2)], writes=[("tg", ti, 1)])
                        S.op("dve", lambda e, ti=ti, f=f, ncols=ncols: e.tensor_tensor(
                            out=mT[:, f, ncols], in0=tg[ti][:, 0:512], in1=tg[ti][:, 512:1024], op=ALU.add),
                            reads=[("tg", ti, 0), ("tg", ti, 1)], writes=[("mT", f, stl)], cost=420)
                release(pc["gf", j])
                release(pc["ga", j])
                release(pc["fo", j])

            dbg_dump("mT", mT[:], [128, 8, 1024], [("mT", f, s_) for f in range(8) for s_ in range(2)], BF16) if p == dbg_pass(dbg) else None

            _mpool[0] = list(MB)
            stop_at('m', p)
            Wos = []
            for ch in range(2):
                Wo, kwo = use(pc["wo", ch])
                Wov_ = Wo[:, :].rearrange("p (k c) -> p k c", c=512)
                for kq in range(4):
                    S.op("dve", lambda e, Wov_=Wov_, ch=ch, kq=kq: e.tensor_tensor(
                        out=Wov_[:, 2 * kq:2 * kq + 2, :], in0=Wov_[:, 2 * kq:2 * kq + 2, :],
                        in1=gt_bc[0][:, ch * 512:(ch + 1) * 512].unsqueeze(1).to_broadcast([128, 2, 512]), op=ALU.mult),
                        reads=[kwo, gt1k[ch]], writes=[kwo], n=1024)
                Wos.append((Wov_, kwo))
            for t in range(NT):
                for ch in range(2):
                    Wov, kwo = Wos[ch]
                    b = mbank()

                    def mmo(e, b=b, t=t, Wov=Wov):
                        for k in range(8):
                            ins = e.matmul(PB(b), lhsT=mT[:, k, t * 128:(t + 1) * 128], rhs=Wov[:, k, :], start=(k == 0), stop=(k == 7))
                        return ins
                    S.op("pe", mmo, reads=[("mT", f, t // 4) for f in range(8)] + [kwo], writes=[("P", b)], cost=1900)
                    S.op("dve", lambda e, b=b, t=t, ch=ch: e.tensor_tensor(
                        out=x_res[:, t, ch * 512:(ch + 1) * 512], in0=PB(b), in1=x_res[:, t, ch * 512:(ch + 1) * 512], op=ALU.add),
                        reads=[("P", b), ("x", t)], writes=[("x", t)])
                norm_tile(p, t, 1, s2T, 24)
            release(pc["wo", 0])
            release(pc["wo", 1])

            dbg_dump("x1", x_res[:], [128, NT, D], [("x", t) for t in range(NT)]) if p == dbg_pass(dbg) else None
            stop_at('x1', p)
            stop_at('n2', p)

            _mpool[0] = list(MB) + P4_EXTRA_BANKS
            for jc in range(11):
                Wu, kwu = use(pc["up", jc])
                Wuv = Wu[:, :].rearrange("p (k c) -> p k c", c=512)
                for half in range(2):
                    jj = jc * 2 + half
                    for stl in range(2):
                        ncols = slice(stl * 512, (stl + 1) * 512)
                        hk = [k for t in range(4 * stl, 4 * stl + 4) for k in hT_keys(t)]
                        ba, bu = mbank(), mbank()

                        def mmu(e, b, c0, Wuv=Wuv, ncols=ncols):
                            for k in range(8):
                                ins = e.matmul(PB(b), lhsT=Wuv[:, k, c0:c0 + 128], rhs=hT[:, k, ncols], start=(k == 0), stop=(k == 7))
                            return ins
                        S.op("pe", lambda e, b=ba, c0=half * 128, mmu=mmu: mmu(e, b, c0), reads=hk + [kwu], writes=[("P", ba)], cost=1800)
                        S.op("pe", lambda e, b=bu, c0=256 + half * 128, mmu=mmu: mmu(e, b, c0), reads=hk + [kwu], writes=[("P", bu)], cost=1800)
                        si = (jj * 2 + stl) % 2
                        S.op("act", lambda e, b=ba, si=si: e.activation(out=sa[si][:], in_=PB(b), func=AF.Silu),
                             reads=[("P", ba)], writes=[("sa", si)])
                        ao = jj * 1024 + stl * 512
                        S.op("dve", lambda e, b=bu, si=si, ao=ao: e.tensor_tensor(out=AR[:, ao:ao + 512], in0=sa[si][:], in1=PB(b), op=ALU.mult),
                             reads=[("sa", si), ("P", bu)], writes=ak(ao, 512))
                release(pc["up", jc])

            stop_at('up', p)
            _mpool[0] = list(MB)
            for cq in range(4):
                Wd0, kd0 = use(pc["dn", cq, 0])
                Wd1, kd1 = use(pc["dn", cq, 1])
                Wd = [Wd0[:, 0:2816].rearrange("p (k c) -> p k c", c=256), Wd1[:, 0:2816].rearrange("p (k c) -> p k c", c=256)]
                for (Wd_, kd_) in ((Wd[0], kd0), (Wd[1], kd1)):
                    S.op("pool", lambda e, Wd_=Wd_, cq=cq: e.tensor_tensor(
                        out=Wd_, in0=Wd_, in1=gt_bc[1][:, cq * 256:(cq + 1) * 256].unsqueeze(1).to_broadcast([128, 11, 256]), op=ALU.mult),
                        reads=[kd_, gt2k[cq // 2]], writes=[kd_], n=2816)
                for t in range(NT):
                    b = mbank()

                    def mmd(e, b=b, t=t, Wd=Wd):
                        for k in range(22):
                            o = k * 1024 + t * 128
                            ins = e.matmul(PB(b)[:, 0:256], lhsT=AR[:, o:o + 128], rhs=Wd[k // 11][:, k % 11, :], start=(k == 0), stop=(k == 21))
                        return ins
                    S.op("pe", mmd, reads=[x for k in range(22) for x in ak(k * 1024 + t * 128, 128)] + [kd0, kd1], writes=[("P", b)], cost=2400)
                    S.op("dve", lambda e, b=b, t=t, cq=cq: e.tensor_tensor(
                        out=x_res[:, t, cq * 256:(cq + 1) * 256], in0=PB(b)[:, 0:256], in1=x_res[:, t, cq * 256:(cq + 1) * 256], op=ALU.add),
                        reads=[("P", b), ("x", t)], writes=[("x", t)], n=256)
                    if cq == 3:
                        S.dma("sp", y_out[p, t * 128:(t + 1) * 128, :], x_res[:, t, :], reads=[("x", t)], nbytes=512 * 1024)
                release(pc["dn", cq, 0])
                release(pc["dn", cq, 1])

        except _Stop:
            pass
        S.finish("sp")
        S.emit(window=SCHED_WINDOW, reorder=SCHED_REORDER)
        _LAST['S'] = S
    return nc, dbg_outs


def dbg_pass(dbg):
    for d in dbg:
        if isinstance(d, tuple) and d[0] == "pass":
            return d[1]
    return 0


def _constants():
    c = {}
    c["ident"] = np.eye(128, dtype=np.float32)
    n = np.arange(1024, dtype=np.float64)
    ang = 2.0 * np.pi * np.outer(n, n) / 1024.0
    c["dft_c"] = np.cos(ang).astype(np.float32)
    c["dft_s"] = np.sin(ang).astype(np.float32)
    n2 = np.arange(256, dtype=np.float64)
    ang2 = 2.0 * np.pi * np.outer(n2, n2) / 256.0
    c["dft_c256"] = np.cos(ang2).astype(np.float32)
    c["dft_s256"] = np.sin(ang2).astype(np.float32)
    m = np.arange(128, dtype=np.float64)
    angc = 2.0 * np.pi * np.outer(m, m) / 128.0
    chan = np.zeros((128, 4, 128), np.float32)
    for i, N in enumerate((256, 1024)):
        sc = 1.0 / np.sqrt(N * 128.0)
        chan[:, 2 * i, :] = np.cos(angc) * sc
        chan[:, 2 * i + 1, :] = -np.sin(angc) * sc
    c["chan"] = chan
    pos = np.arange(1024)
    row = (pos // 64).astype(np.float32)
    col = (pos % 64).astype(np.float32)
    inv = (10000.0 ** (-np.arange(0, 32, 2, dtype=np.float32) / 32)).astype(np.float32)
    ar = row[:, None] * inv[None, :]
    ac = col[:, None] * inv[None, :]
    c["rope_cos"] = np.concatenate([np.cos(ar), np.cos(ar), np.cos(ac), np.cos(ac)], axis=1).astype(np.float32)
    c["rope_sin"] = np.concatenate([-np.sin(ar), np.sin(ar), -np.sin(ac), np.sin(ac)], axis=1).astype(np.float32)
    a = np.arange(128)[:, None]
    b = np.arange(128)[None, :]
    mask3 = np.concatenate([(a <= b), np.ones((128, 128), bool), (b <= a)], axis=1).astype(np.float32)
    c["mask3"] = mask3
    return c


def make_in_maps(x_prompt, x_sample, cache_k, cache_v, c, c_ctx, w_ada, b_ada, g_norm1, g_norm2,
                 w_in, g_q, g_k, sinks, w_f, w_ao, w_out, w_up, w_down):
    f = lambda a: np.ascontiguousarray(np.asarray(a, dtype=np.float32))
    consts = _constants()
    shared = {
        "w_ada": f(w_ada[0]), "w_in": f(w_in[0]), "w_f": f(w_f[0]), "w_ao": f(w_ao[0]),
        "w_out": f(w_out[0]), "w_up": f(w_up[0]), "w_down": f(w_down[0]),
        "b_adaT": f(np.asarray(b_ada[0]).reshape(48, 128).T),
        "gn": f(np.concatenate([np.asarray(g_norm1[0]).reshape(8, 128).T, np.asarray(g_norm2[0]).reshape(8, 128).T], axis=1)),
        "gqk": f(np.broadcast_to(np.concatenate([np.tile(np.asarray(g_q[0]), 8), np.tile(np.asarray(g_k[0]), 2)])[None, :], (128, 640))),
        "sinks_bc": f(np.broadcast_to(np.asarray(sinks[0])[None, :], (128, 8))),
    }
    shared.update(consts)
    xp = np.asarray(x_prompt, dtype=np.float32)
    xs = np.asarray(x_sample, dtype=np.float32)
    maps = []
    for i in range(NCORES):
        m = dict(shared)
        m["xin"] = f(np.stack([xp[4 * i:4 * i + 4].reshape(1024, D), xs[i]], axis=0))
        m["ck"] = f(np.asarray(cache_k)[i, 0].reshape(256, 128))
        m["cv"] = f(np.asarray(cache_v)[i, 0].reshape(256, 128))
        cv2 = np.stack([np.asarray(c_ctx), np.asarray(c)[i]], axis=1)
        m["cT"] = f(cv2.reshape(8, 128, 2).transpose(1, 0, 2).reshape(128, 16))
        maps.append(m)
    return maps


_NC_CACHE = {}


def kernel(**inputs):
    if "nc" not in _NC_CACHE:
        _NC_CACHE["nc"] = build_program()[0]
    nc = _NC_CACHE["nc"]
    in_maps = make_in_maps(**inputs)
    res = run_bass_kernel_spmd(nc, in_maps, core_ids=list(range(NCORES)))
    r = res.results
    y_prompt = np.concatenate([r[i]["y"][0].reshape(4, 256, D) for i in range(NCORES)], axis=0)
    y_sample = np.stack([r[i]["y"][1] for i in range(NCORES)], axis=0)
    nk = np.concatenate([r[i]["nk"].reshape(4, 1, 256, 2, 64) for i in range(NCORES)], axis=0)
    nv = np.concatenate([r[i]["nv"].reshape(4, 1, 256, 2, 64) for i in range(NCORES)], axis=0)
    return (y_prompt.astype(np.float32), y_sample.astype(np.float32), nk.astype(np.float32), nv.astype(np.float32))
```

```python
import numpy as np
from contextlib import ExitStack
import concourse.bass as bass
import concourse.mybir as mybir
from concourse.bass_utils import run_bass_kernel_spmd
from concourse.alu_op_type import AluOpType as ALU

F32 = mybir.dt.float32
BF16 = mybir.dt.bfloat16
I32 = mybir.dt.int32
AF = mybir.ActivationFunctionType
AX = mybir.AxisListType

D = 1024
NT = 8
DFF = 2816
EPS = 1e-6
NCORES = 8
SCHED_WINDOW = 40
N_TBANKS = 3
CTX_INTERLEAVE = True
XN_ON_ACT = ((1, 0),)
CACHE_QH1_AT = 4
LAT_EXTRA_BANKS = [0, 1, 2]
CTX_EXTRA_BANKS = [0, 1, 2]
P3_EXTRA_BANKS = [0, 1, 2]
P4_EXTRA_BANKS = []
LATF_EXTRA_BANKS = [2]
SCHED_REORDER = True
_LAST = {}


class Sched:
    QUEUES = ["pe", "act", "dve", "pool", "sp"]
    DEFCOST = {"pe": 1000.0, "act": 600.0, "dve": 690.0, "pool": 1260.0, "sp": 100.0}

    def __init__(self, nc, stack, n_sp_slots=16, n_pool_slots=24):
        self.nc = nc
        self.sem = {e: stack.enter_context(nc.semaphore(f"s_{e}")) for e in self.QUEUES}
        self.dpool = {
            "sp": [stack.enter_context(nc.semaphore(f"dsp_{i}")) for i in range(n_sp_slots)],
            "pool": [stack.enter_context(nc.semaphore(f"dpl_{i}")) for i in range(n_pool_slots)],
        }
        self.ops = []
        self.lastw = {}
        self.readers = {}
        self.final_wait = None

    def _mkdeps(self, reads, writes, q=None):
        deps = {}

        def add(i, raw):
            deps[i] = deps.get(i, False) or raw
        for r in reads:
            i = self.lastw.get(r)
            if i is not None:
                add(i, True)
            if isinstance(r, tuple) and r[0] == "P":
                for i in self.readers.get(r, ()):
                    if self.ops[i]["q"] != q:
                        add(i, False)
        for w in writes:
            i = self.lastw.get(w)
            if i is not None:
                add(i, False)
            for i in self.readers.get(w, ()):
                add(i, False)
        return deps

    def _record(self, idx, reads, writes):
        for r in reads:
            self.readers.setdefault(r, []).append(idx)
        for w in writes:
            self.lastw[w] = idx
            self.readers[w] = []

    def op(self, q, fn, reads=(), writes=(), cost=None, n=None):
        if cost is None and n is not None:
            cost = {"act": 170 + 0.833 * n, "dve": 155 + 1.04 * n, "pool": 130 + 2.2 * n, "pe": 1000.0}[q]
        idx = len(self.ops)
        deps = self._mkdeps(reads, writes, q)
        import sys as _sys
        self.ops.append(dict(q=q, kind="op", fn=fn, deps=deps, cost=cost if cost is not None else self.DEFCOST[q],
                             line=_sys._getframe(1).f_lineno))
        self._record(idx, reads, writes)
        return idx

    def dma(self, q, out, in_, reads=(), writes=(), nbytes=65536, **kw):
        idx = len(self.ops)
        deps = self._mkdeps(reads, writes)

        def fn(e, out=out, in_=in_, kw=kw):
            return e.dma_start(out=out, in_=in_, **kw)
        import sys as _sys
        self.ops.append(dict(q=q, kind="dma", fn=fn, deps=deps, cost=float(nbytes) / 330.0, line=_sys._getframe(1).f_lineno))
        self._record(idx, reads, writes)
        return idx

    def finish(self, q="sp"):
        self.final_wait = q

    def schedule(self, window=40, reorder=True):
        ops = self.ops
        n = len(ops)
        fin = [None] * n
        pend = {q: [i for i in range(n) if ops[i]["q"] == q] for q in self.QUEUES}
        head = {q: 0 for q in self.QUEUES}
        done = [False] * n
        qfree = {q: 0.0 for q in self.QUEUES}
        dma_free = 0.0
        order = {q: [] for q in self.QUEUES}
        remaining = n
        while remaining:
            best = None
            for q in self.QUEUES:
                lst = pend[q]
                h = head[q]
                while h < len(lst) and done[lst[h]]:
                    h += 1
                head[q] = h
                cnt = 0
                j = h
                while j < len(lst) and cnt < window:
                    i = lst[j]
                    j += 1
                    if done[i]:
                        continue
                    cnt += 1
                    t = qfree[q]
                    ok = True
                    for d in ops[i]["deps"]:
                        f = fin[d]
                        if f is None:
                            ok = False
                            break
                        if f > t:
                            t = f
                    if ok:
                        kt = t - 4000.0 if ops[i]["kind"] == "dma" else t
                        if best is None or (kt, i) < best[0]:
                            best = ((kt, i), q, i, t)
                    if not reorder:
                        break
            assert best is not None, "scheduler deadlock"
            _, q, i, t = best
            o = ops[i]
            if o["kind"] == "dma":
                issue = 1000.0 if q == "pool" else 120.0
                qfree[q] = t + issue
                xs = max(t + issue, dma_free)
                dma_free = xs + o["cost"]
                fin[i] = dma_free + 2000.0
            else:
                qfree[q] = t + o["cost"]
                fin[i] = qfree[q] + (100.0 if q != "pe" else 250.0)
            o["start"] = t
            o["fin"] = fin[i]
            done[i] = True
            order[q].append(i)
            remaining -= 1
        self.order = order
        self.sim_time = max(f for f in fin if f is not None) if n else 0.0
        return order

    def emit(self, window=40, reorder=True):
        ops = self.ops
        order = self.schedule(window=window, reorder=reorder)
        duse = {}
        for q in self.QUEUES:
            seq = 0
            k = 0
            for i in order[q]:
                o = ops[i]
                if o["kind"] == "op":
                    seq += 1
                    o["tok"] = (q, seq)
                else:
                    pool = self.dpool[q]
                    slot = k % len(pool)
                    k += 1
                    key = ("d", q, slot)
                    prev = duse.get(key, 0)
                    o["prev_tok"] = (key, 16 * prev) if prev else None
                    duse[key] = prev + 1
                    o["tok"] = (key, 16 * (prev + 1))
        self._duse = duse

        def semobj(semkey):
            if isinstance(semkey, tuple):
                return self.dpool[semkey[1]][semkey[2]]
            return self.sem[semkey]

        def run(q, e):
            seen = {}
            for i in order[q]:
                o = ops[i]
                need = {}
                for d, raw in o["deps"].items():
                    od = ops[d]
                    if od["kind"] == "op" and od["q"] == q:
                        if q == "pe":
                            continue
                    sk, val = od["tok"]
                    if need.get(sk, 0) < val:
                        need[sk] = val
                if o["kind"] == "dma" and o["prev_tok"] is not None:
                    sk, val = o["prev_tok"]
                    if need.get(sk, 0) < val:
                        need[sk] = val
                for sk, val in need.items():
                    if seen.get(sk, 0) >= val:
                        continue
                    seen[sk] = val
                    e.wait_ge(semobj(sk), val)
                ins = o["fn"](e)
                sk, val = o["tok"]
                if o["kind"] == "op":
                    ins.then_inc(self.sem[q], 1)
                else:
                    ins.then_inc(semobj(sk), 16)
            if self.final_wait == q:
                for key, u in duse.items():
                    if seen.get(key, 0) < 16 * u:
                        e.wait_ge(semobj(key), 16 * u)

        with self.nc.Block() as block:
            @block.tensor
            def _(e):
                run("pe", e)

            @block.scalar
            def _(e):
                run("act", e)

            @block.vector
            def _(e):
                run("dve", e)

            @block.gpsimd
            def _(e):
                run("pool", e)

            @block.sync
            def _(e):
                run("sp", e)


class _Stop(Exception):
    pass


def build_program(dbg=(), stop=None):
    nc = bass.Bass("TRN2", target_bir_lowering=False)

    def din(name, shape, dt=F32):
        return nc.dram_tensor(name, list(shape), dt, kind="ExternalInput").ap()

    def dout(name, shape, dt=F32):
        return nc.dram_tensor(name, list(shape), dt, kind="ExternalOutput").ap()

    xin = din("xin", [2, 1024, D])
    ck_in = din("ck", [256, 128])
    cv_in = din("cv", [256, 128])
    cT_in = din("cT", [128, 16])
    badaT_in = din("b_adaT", [128, 48])
    gn_in = din("gn", [128, 16])
    gqk_in = din("gqk", [128, 640])
    sinks_in = din("sinks_bc", [128, 8])
    w_ada = din("w_ada", [D, 6 * D])
    w_in = din("w_in", [D, 3328])
    w_f = din("w_f", [512, D])
    w_ao = din("w_ao", [512, D])
    w_out = din("w_out", [D, D])
    w_up = din("w_up", [D, 2 * DFF])
    w_down = din("w_down", [DFF, D])
    ident_in = din("ident", [128, 128])
    dftc_in = din("dft_c", [1024, 1024])
    dfts_in = din("dft_s", [1024, 1024])
    dftc256_in = din("dft_c256", [256, 256])
    dfts256_in = din("dft_s256", [256, 256])
    chan_in = din("chan", [128, 4, 128])
    ropec_in = din("rope_cos", [1024, 64])
    ropes_in = din("rope_sin", [1024, 64])
    mask_in = din("mask3", [128, 384])

    y_out = dout("y", [2, 1024, D])
    nk_out = dout("nk", [1024, 128])
    nv_out = dout("nv", [1024, 128])

    dbg_outs = {}

    with ExitStack() as st:
        S = Sched(nc, st)
        _cnt = [0]

        def sb(name, shape, dt):
            _cnt[0] += 1
            nb = int(np.prod(shape[1:])) * (4 if dt in (F32, I32) else 2)
            _LAST.setdefault('alloc', []).append((name, nb))
            return st.enter_context(nc.sbuf_tensor(f"sb_{name}", list(shape), dt))

        x_res = sb("x_res", [128, NT, D], F32)
        hT = sb("hT", [128, 8, 1024], BF16)
        mT = sb("mT", [128, 8, 1024], BF16)
        AR = sb("arena", [128, 22 * 1024], BF16)
        NSLOT = 5
        wslot = [sb(f"w{i}", [128, 4096], BF16) for i in range(NSLOT)]
        v_aug = sb("v_aug", [128, NT, 2, 65], BF16)
        cv_aug = sb("cv_aug", [128, 2, 2, 65], BF16)
        ckT = sb("ckT", [128, 2, 2, 128], BF16)
        kTz = sb("kTz", [128, 2, 1024], BF16)
        dftc256 = sb("dftc256", [128, 2, 256], BF16)
        dfts256 = sb("dfts256", [128, 2, 256], BF16)
        chan = sb("chan", [128, 4, 128], BF16)
        identf = sb("identf", [128, 128], F32)
        identb = sb("identb", [128, 128], BF16)
        ropec = sb("ropec", [128, NT, 64], F32)
        ropes = sb("ropes", [128, NT, 64], F32)
        mask3 = sb("mask3", [128, 384], BF16)
        gqk = sb("gqk", [128, 640], F32)
        esink = sb("esink", [128, 8], F32)
        cT = sb("cT", [128, 16], F32)
        siluT = sb("siluT", [128, 16], BF16)
        badaT = sb("badaT", [128, 48], F32)
        gn = sb("gn", [128, 16], F32)
        modT = sb("modT", [128, 48, 2], F32)
        s1T = sb("s1T", [128, 8, 2], F32)
        s2T = sb("s2T", [128, 8, 2], F32)
        gtrep = sb("gtrep", [128, 128], F32)
        gt_bc = [sb("gt1bc", [128, D], F32), sb("gt2bc", [128, D], F32)]
        xnb = [sb(f"xnb{i}", [128, D], BF16) for i in range(2)]
        ss = sb("ss", [128, 2 * NT], F32)
        vv = sb("vv", [128, 2 * NT], F32)
        rstd = sb("rstd", [128, 2 * NT], F32)
        neghalf = sb("neghalf", [128, 16], F32)
        sq = [sb(f"sq{i}", [128, 640], F32) for i in range(2)]
        ssqk = [sb(f"ssqk{i}", [128, 10], F32) for i in range(2)]
        vqk = [sb(f"vqk{i}", [128, 10], F32) for i in range(2)]
        rqk = [sb(f"rqk{i}", [128, 10], F32) for i in range(2)]
        qkn = [sb(f"qkn{i}", [128, 640], F32) for i in range(2)]
        rt2 = sb("rt2", [128, 640], F32)
        qb = [sb(f"qb{i}", [128, 640], BF16) for i in range(2)]
        vf = [sb(f"vf{i}", [128, 128], F32) for i in range(2)]
        ckb = sb("ckb", [128, 2, 128], BF16)
        den = [sb(f"den{i}", [128, 4], F32) for i in range(4)]
        rec = [sb(f"rec{i}", [128, 4], F32) for i in range(4)]
        tg = [sb(f"tg{i}", [128, 1024], BF16) for i in range(2)]
        sa = [sb(f"sa{i}", [128, 512], BF16) for i in range(2)]

        pbank = [st.enter_context(nc.psum_tensor(f"pb{i}", [128, 512], F32)) for i in range(8)]
        _tnext = [0]
        _mnext = [0]

        def tbank():
            i = _tnext[0] % N_TBANKS
            _tnext[0] += 1
            return i

        MB = list(range(N_TBANKS, 8))
        _mpool = [list(MB)]

        def mbank():
            pool = _mpool[0]
            i = pool[_mnext[0] % len(pool)]
            _mnext[0] += 1
            return i

        def PB(i):
            return pbank[i][:]

        def PBb(i):
            return pbank[i][:].bitcast(BF16)

        def ak(off, n):
            return [("A", b) for b in range(off // 128, (off + n - 1) // 128 + 1)]

        def qT_off(c, t0, n):
            return c * 1024 + t0 * 128, n * 128
        KT_OFF = 4 * 1024
        MIX_OFF = 5 * 1024
        OT_OFF = 9 * 1024
        ZF_OFF = 13 * 1024
        PTL_OFF = 18 * 1024
        PTC_OFF = 20 * 1024

        def dbg_dump(name, ap, shape, reads, dt=F32):
            if name not in dbg:
                return
            o = dout("dbg_" + name, shape, dt)
            dbg_outs[name] = (shape, dt)
            S.dma("sp", o, ap, reads=reads)

        chunks = []

        def add_chunk(parts):
            chunks.append(parts)
            return len(chunks) - 1

        emitted = [0]
        slot_of = {}
        free_slots = list(range(NSLOT))

        def pump():
            while emitted[0] < len(chunks) and free_slots:
                i = emitted[0]
                s = free_slots.pop(0)
                slot_of[i] = s
                for (mk_dst, src) in chunks[i]:
                    S.dma("pool", mk_dst(wslot[s]), src, writes=[("W", s)], nbytes=4 * int(np.prod(src.shape)))
                emitted[0] += 1

        def prefetch(upto=None):
            pump()

        def use(i):
            pump()
            assert i in slot_of, f"weight chunk {i} not resident"
            return wslot[slot_of[i]], ("W", slot_of[i])

        def release(i):
            free_slots.append(slot_of[i])
            pump()

        def kview(rows_k, cols):
            return lambda w, rows_k=rows_k, cols=cols: w[:, 0:rows_k * cols].rearrange("p (k c) -> p k c", c=cols)

        def kview_off(off, rows_k, cols):
            return lambda w, off=off, rows_k=rows_k, cols=cols: w[:, off:off + rows_k * cols].rearrange("p (k c) -> p k c", c=cols)

        def wsrc(wap, r0, nr, c0, ncol):
            return wap[r0:r0 + nr, c0:c0 + ncol].rearrange("(k p) c -> p k c", p=128)

        ada_chunks = [add_chunk([(kview(8, 512), wsrc(w_ada, 0, 1024, cc * 512, 512))]) for cc in range(4)]
        pass_chunks = []
        for p in range(2):
            pc = {}
            pc["zf"] = add_chunk([(kview(8, 512), wsrc(w_in, 0, 1024, 0, 512))])
            pc["q"] = add_chunk([(kview(8, 512), wsrc(w_in, 0, 1024, 512, 512))])
            pc["kv"] = add_chunk([(kview(8, 256), wsrc(w_in, 0, 1024, 1024, 256))])
            if p == 0:
                ada_chunks += [add_chunk([(kview(8, 512), wsrc(w_ada, 0, 1024, cc * 512, 512))]) for cc in range(4, 12)]
            if p == 1:
                for nh in range(2):
                    pc["dc", nh] = add_chunk([(kview(8, 512), wsrc(dftc_in, 0, 1024, nh * 512, 512))])
                    pc["ds", nh] = add_chunk([(kview(8, 512), wsrc(dfts_in, 0, 1024, nh * 512, 512))])
            for j in range(2):
                pc["gf", j] = add_chunk([(kview(8, 512), wsrc(w_in, 0, 1024, 1280 + 512 * j, 512))])
                pc["ga", j] = add_chunk([(kview(8, 512), wsrc(w_in, 0, 1024, 2304 + 512 * j, 512))])
                pc["fo", j] = add_chunk([(kview(4, 512), wsrc(w_f, 0, 512, 512 * j, 512)),
                                         (kview_off(2048, 4, 512), wsrc(w_ao, 0, 512, 512 * j, 512))])
            for ch in range(2):
                pc["wo", ch] = add_chunk([(kview(8, 512), wsrc(w_out, 0, 1024, 512 * ch, 512))])
            for jc in range(11):
                pc["up", jc] = add_chunk([
                    (lambda w: w[:, :].rearrange("p (k c) -> p k c", c=512)[:, :, 0:256], wsrc(w_up, 0, 1024, jc * 256, 256)),
                    (lambda w: w[:, :].rearrange("p (k c) -> p k c", c=512)[:, :, 256:512], wsrc(w_up, 0, 1024, DFF + jc * 256, 256)),
                ])
            for cq in range(4):
                pc["dn", cq, 0] = add_chunk([(kview(11, 256), wsrc(w_down, 0, 1408, cq * 256, 256))])
                pc["dn", cq, 1] = add_chunk([(kview(11, 256), wsrc(w_down, 1408, 1408, cq * 256, 256))])
            pass_chunks.append(pc)

        S.dma("sp", cT[:], cT_in, writes=["cT"])
        S.dma("sp", badaT[:], badaT_in, writes=["badaT"])
        S.dma("sp", gn[:], gn_in, writes=["gn"])
        S.dma("sp", identf[:], ident_in, writes=["identf"])
        S.dma("sp", gqk[:], gqk_in, writes=["gqk"])
        S.dma("sp", esink[:], sinks_in, writes=["esink"])
        S.dma("sp", ropec[:], ropec_in.rearrange("(t p) d -> p t d", p=128), writes=["ropec"])
        S.dma("sp", ropes[:], ropes_in.rearrange("(t p) d -> p t d", p=128), writes=["ropes"])
        S.dma("pool", identb[:], ident_in, writes=["identb"])
        S.dma("pool", chan[:], chan_in, writes=["chan"])
        S.dma("pool", mask3[:], mask_in, writes=["mask3"])
        S.dma("pool", dftc256[:], dftc256_in.rearrange("(k p) n -> p k n", p=128), writes=["dftc256"])
        S.dma("pool", dfts256[:], dfts256_in.rearrange("(k p) n -> p k n", p=128), writes=["dfts256"])
        pump()
        S.op("pool", lambda e: e.memset(neghalf[:], -0.5), writes=["neghalf"], n=16)
        S.op("pool", lambda e: e.memset(kTz[64:128, 0, :], 0.0), writes=["kTz_z0"], n=512)
        S.op("pool", lambda e: e.memset(kTz[0:64, 1, :], 0.0), writes=["kTz_z1"], n=512)
        S.op("pool", lambda e: e.memset(ckT[:], 0.0), writes=["ckT"], n=256)
        S.op("pool", lambda e: e.memset(v_aug[:, :, :, 64:65], 1.0), writes=["v_aug_ones"], n=16)
        S.op("pool", lambda e: e.memset(cv_aug[:, :, :, 64:65], 1.0), writes=["cv_aug"], n=8)

        S.op("act", lambda e: e.activation(out=siluT[:], in_=cT[:], func=AF.Silu), reads=["cT"], writes=["siluT"], n=16)
        def adaln_part(ccs):
            for cc in ccs:
                W, wk = use(ada_chunks[cc])
                Wv = W[:, :].rearrange("p (k c) -> p k c", c=512)
                b = mbank()
                mp = PB(b)[:, 0:8].rearrange("p (c j) -> p c j", j=2)

                def mm(e, Wv=Wv, mp=mp):
                    for f in range(4):
                        for k in range(8):
                            ins = e.matmul(mp[:, f, :], lhsT=Wv[:, k, f * 128:(f + 1) * 128],
                                           rhs=siluT[:, 2 * k:2 * k + 2], start=(k == 0), stop=(k == 7))
                    return ins
                S.op("pe", mm, reads=[wk, "siluT"], writes=[("P", b)], cost=1000)
                release(ada_chunks[cc])
                S.op("dve", lambda e, mp=mp, cc=cc: e.tensor_tensor(
                    out=modT[:, 4 * cc:4 * cc + 4, :], in0=mp,
                    in1=badaT[:, 4 * cc:4 * cc + 4].unsqueeze(2).to_broadcast([128, 4, 2]), op=ALU.add),
                    reads=[("P", b), "badaT"], writes=[("modT", cc)], n=8)
                if cc == 3:
                    S.op("dve", lambda e: e.scalar_tensor_tensor(
                        out=s1T[:], in0=modT[:, 8:16, :], scalar=1.0,
                        in1=gn[:, 0:8].unsqueeze(2).to_broadcast([128, 8, 2]), op0=ALU.add, op1=ALU.mult),
                        reads=[("modT", 2), ("modT", 3), "gn"], writes=["s1T"], n=16)
                if cc == 9:
                    S.op("dve", lambda e: e.scalar_tensor_tensor(
                        out=s2T[:], in0=modT[:, 32:40, :], scalar=1.0,
                        in1=gn[:, 8:16].unsqueeze(2).to_broadcast([128, 8, 2]), op0=ALU.add, op1=ALU.mult),
                        reads=[("modT", 8), ("modT", 9), "gn"], writes=["s2T"], n=16)

        adaln_part(range(4))

        S.op("act", lambda e: e.activation(out=esink[:], in_=esink[:], func=AF.Exp), reads=["esink"], writes=["esink"], n=8)

        S.dma("pool", ckb[:], ck_in.rearrange("(b p) d -> p b d", p=128), writes=["ckb"])
        tb = tbank()

        def trck(e, tb=tb):
            for b in range(2):
                ins = e.transpose(out=PBb(tb)[:, b * 128:(b + 1) * 128], in_=ckb[:, b, :], identity=identb[:])
            return ins
        S.op("pe", trck, reads=["ckb", "identb"], writes=[("P", tb)], cost=300)
        S.op("dve", lambda e, tb=tb: e.tensor_copy(out=ckT[0:64, 0, :, :], in_=PBb(tb)[0:64, 0:256].rearrange("p (b k) -> p b k", k=128)),
             reads=[("P", tb), "ckT"], writes=["ckT"], n=256)
        S.op("dve", lambda e, tb=tb: e.tensor_copy(out=ckT[64:128, 1, :, :], in_=PBb(tb)[64:128, 0:256].rearrange("p (b k) -> p b k", k=128)),
             reads=[("P", tb), "ckT"], writes=["ckT"], n=256)
        for b_ in range(2):
            S.dma("pool", cv_aug[:, b_, :, 0:64], cv_in[b_ * 128:(b_ + 1) * 128, :].rearrange("p (k d) -> p k d", d=64),
                  reads=["cv_aug"], writes=["cv_aug"])

        def build_gt_bc(p, which, dst, scale):
            base = which * 8
            for half in range(2):
                b = mbank()
                for kk in range(4):
                    k = half * 4 + kk
                    S.op("dve", lambda e, k=k: e.tensor_scalar(
                        out=gtrep[:], in0=modT[:, base + k, p:p + 1].to_broadcast([128, 128]),
                        scalar1=scale, scalar2=None, op0=ALU.mult),
                        reads=[("modT", 2 * which), ("modT", 2 * which + 1)], writes=["gtrep"], n=128)
                    S.op("pe", lambda e, b=b, kk=kk: e.transpose(out=PB(b)[:, kk * 128:(kk + 1) * 128], in_=gtrep[:], identity=identf[:]),
                         reads=["gtrep", "identf"], writes=[("P", b)], cost=450)
                S.op("act", lambda e, b=b, half=half: e.copy(out=dst[:, half * 512:(half + 1) * 512], in_=PB(b)),
                     reads=[("P", b)], writes=[("gtbc", id(dst), half)])

        def norm_tile(p, t, which, sT, bidx):
            col = which * NT + t
            S.op("act", lambda e: e.activation(out=xnb[t % 2][:], in_=x_res[:, t, :], func=AF.Square,
                                               accum_out=ss[:, col:col + 1]),
                 reads=[("x", t)], writes=[("xnb", t % 2), ("ss", col)], n=1024)
            S.op("pool", lambda e: e.tensor_scalar(out=vv[:, col:col + 1], in0=ss[:, col:col + 1],
                                                   scalar1=1.0 / D, scalar2=EPS, op0=ALU.mult, op1=ALU.add),
                 reads=[("ss", col)], writes=[("vv", col)], cost=200)
            S.op("pool", lambda e: e.tensor_tensor(out=rstd[:, col:col + 1], in0=vv[:, col:col + 1],
                                                   in1=neghalf[:, 0:1], op=ALU.pow),
                 reads=[("vv", col), "neghalf"], writes=[("rstd", col)], cost=780)
            xb = xnb[t % 2]
            xk = ("xnb", t % 2)
            if (p, which) in XN_ON_ACT:
                S.op("act", lambda e: e.activation(out=xb[:], in_=x_res[:, t, :], func=AF.Copy, scale=rstd[:, col:col + 1]),
                     reads=[("x", t), ("rstd", col)], writes=[xk], cost=1120)
            else:
                S.op("dve", lambda e: e.tensor_scalar(out=xb[:], in0=x_res[:, t, :], scalar1=rstd[:, col:col + 1],
                                                      scalar2=None, op0=ALU.mult),
                     reads=[("x", t), ("rstd", col)], writes=[xk], cost=713)
            tbA, tbB = tbank(), tbank()

            def tr(e):
                for k in range(8):
                    tb_ = tbA if k < 5 else tbB
                    kk_ = k if k < 5 else k - 5
                    ins = e.transpose(out=PBb(tb_)[:, kk_ * 128:(kk_ + 1) * 128], in_=xb[:, k * 128:(k + 1) * 128], identity=identb[:])
                return ins
            S.op("pe", tr, reads=[xk, "identb"], writes=[("P", tbA), ("P", tbB)], cost=700)
            mk_ = (["s1T", ("modT", 0), ("modT", 1)] if which == 0 else ["s2T", ("modT", 6), ("modT", 7)])
            for k in range(8):
                if k < 5:
                    S.op("dve", lambda e, k=k: e.tensor_scalar(
                        out=hT[:, k, t * 128:(t + 1) * 128], in0=PBb(tbA)[:, k * 128:(k + 1) * 128],
                        scalar1=sT[:, k, p:p + 1], scalar2=modT[:, bidx + k, p:p + 1], op0=ALU.mult, op1=ALU.add),
                        reads=[("P", tbA)] + mk_, writes=[("hT", t, k)], cost=280)
                else:
                    S.op("act", lambda e, k=k: e.activation(
                        out=hT[:, k, t * 128:(t + 1) * 128], in_=PBb(tbB)[:, (k - 5) * 128:(k - 4) * 128],
                        func=AF.Identity, scale=sT[:, k, p:p + 1], bias=modT[:, bidx + k, p:p + 1]),
                        reads=[("P", tbB)] + mk_, writes=[("hT", t, k)], cost=480)

        def hT_keys(t):
            return [("hT", t, k) for k in range(8)]

        def stop_at(name, p):
            if stop == (name, p):
                raise _Stop()

        try:
          for p in range(2):
            stop_at('p0', p)
            pc = pass_chunks[p]
            latent = (p == 1)
            if p == 1:
                build_gt_bc(p, 2, gt_bc[0], 0.5)
                build_gt_bc(p, 5, gt_bc[1], 1.0)
            gt1k = [("gtbc", id(gt_bc[0]), h) for h in range(2)]
            gt2k = [("gtbc", id(gt_bc[1]), h) for h in range(2)]

            for t in range(NT):
                S.dma("sp", x_res[:, t, :], xin[p, t * 128:(t + 1) * 128, :], writes=[("x", t)], nbytes=512 * 1024,
                      reads=([("modT", 2)] if (p == 0 and t >= 2) else []))
            Wzf, kzf = use(pc["zf"])
            Wq, kq = use(pc["q"])
            Wkv, kkv = use(pc["kv"])
            Wzfv = Wzf[:, :].rearrange("p (k c) -> p k c", c=512)
            Wqv = Wq[:, :].rearrange("p (k c) -> p k c", c=512)
            Wkvv = Wkv[:, 0:2048].rearrange("p (k c) -> p k c", c=256)
            norm_tile(p, 0, 0, s1T, 0)
            for t in range(NT):
                if t + 1 < NT:
                    norm_tile(p, t + 1, 0, s1T, 0)
                if p == 0:
                    adaln_part([4 + t])
                tcols = slice(t * 128, (t + 1) * 128)
                bzf, bq, bkv = mbank(), mbank(), mbank()

                def mmz(e, b=bzf, Wv=Wzfv, n=512, tcols=tcols):
                    for k in range(8):
                        ins = e.matmul(PB(b)[:, 0:n], lhsT=hT[:, k, tcols], rhs=Wv[:, k, :], start=(k == 0), stop=(k == 7))
                    return ins
                S.op("pe", mmz, reads=hT_keys(t) + [kzf], writes=[("P", bzf)], cost=1900)
                S.op("pe", lambda e, b=bq, Wv=Wqv, tcols=tcols: mmz(e, b, Wv, 512, tcols), reads=hT_keys(t) + [kq], writes=[("P", bq)], cost=1900)
                S.op("pe", lambda e, b=bkv, Wv=Wkvv, tcols=tcols: mmz(e, b, Wv, 256, tcols), reads=hT_keys(t) + [kkv], writes=[("P", bkv)], cost=870)
                zoff = ZF_OFF + t * 512
                S.op("act", lambda e, b=bzf, zoff=zoff: e.copy(out=AR[:, zoff:zoff + 512], in_=PB(b)),
                     reads=[("P", bzf)], writes=ak(zoff, 512))
                i2 = t % 2
                S.op("act", lambda e, b=bq, i2=i2: e.activation(out=sq[i2][:, 0:512], in_=PB(b), func=AF.Square),
                     reads=[("P", bq)], writes=[("sq", i2, 0)])
                S.op("act", lambda e, b=bkv, i2=i2: e.activation(out=sq[i2][:, 512:640], in_=PB(b)[:, 0:128], func=AF.Square),
                     reads=[("P", bkv)], writes=[("sq", i2, 1)], n=128)
                S.op("dve", lambda e, b=bq, i2=i2: e.tensor_tensor(out=qkn[i2][:, 0:512], in0=PB(b), in1=gqk[:, 0:512], op=ALU.mult),
                     reads=[("P", bq), "gqk"], writes=[("qkn", i2, 0)])
                S.op("dve", lambda e, b=bkv, i2=i2: e.tensor_tensor(out=qkn[i2][:, 512:640], in0=PB(b)[:, 0:128], in1=gqk[:, 512:640], op=ALU.mult),
                     reads=[("P", bkv), "gqk"], writes=[("qkn", i2, 1)], n=128)
                S.op("dve", lambda e, i2=i2: e.tensor_reduce(out=ssqk[i2][:], in_=sq[i2][:].rearrange("p (h d) -> p h d", d=64),
                                                             axis=AX.X, op=ALU.add),
                     reads=[("sq", i2, 0), ("sq", i2, 1)], writes=[("ssqk", i2)], n=640)
                S.op("pool", lambda e, i2=i2: e.tensor_scalar(out=vqk[i2][:], in0=ssqk[i2][:], scalar1=1.0 / 64, scalar2=EPS,
                                                              op0=ALU.mult, op1=ALU.add),
                     reads=[("ssqk", i2)], writes=[("vqk", i2)], cost=225)
                S.op("pool", lambda e, i2=i2: e.tensor_tensor(out=rqk[i2][:], in0=vqk[i2][:], in1=neghalf[:, 0:10], op=ALU.pow),
                     reads=[("vqk", i2), "neghalf"], writes=[("rqk", i2)], cost=1974)
                S.op("pool", lambda e, i2=i2: e.tensor_tensor(
                    out=qkn[i2][:].rearrange("p (h d) -> p h d", d=64), in0=qkn[i2][:].rearrange("p (h d) -> p h d", d=64),
                    in1=rqk[i2][:, 0:10].unsqueeze(2).to_broadcast([128, 10, 64]), op=ALU.mult),
                    reads=[("qkn", i2, 0), ("qkn", i2, 1), ("rqk", i2)], writes=[("qkn", i2, 0), ("qkn", i2, 1)], n=640)
                qkk = [("qkn", i2, 0), ("qkn", i2, 1)]
                S.op("act", lambda e, b=bkv, t=t: e.copy(out=v_aug[:, t, :, 0:64], in_=PB(b)[:, 128:256].rearrange("p (k d) -> p k d", d=64)),
                     reads=[("P", bkv), "v_aug_ones"], writes=[("v_aug", t)], n=128)
                if not latent:
                    S.op("act", lambda e, b=bkv, i2=i2: e.copy(out=vf[i2][:], in_=PB(b)[:, 128:256]),
                         reads=[("P", bkv)], writes=[("vf", i2)], n=128)
                    S.dma("sp", nv_out[t * 128:(t + 1) * 128, :], vf[i2][:], reads=[("vf", i2)])
                    S.dma("sp", nk_out[t * 128:(t + 1) * 128, :], qkn[i2][:, 512:640], reads=qkk)
                    src = qkn[i2]
                    srck = qkk
                else:
                    S.op("dve", lambda e, i2=i2, t=t: e.tensor_tensor(
                        out=sq[i2][:].rearrange("p (h d) -> p h d", d=64), in0=qkn[i2][:].rearrange("p (h d) -> p h d", d=64),
                        in1=ropec[:, t, :].unsqueeze(1).to_broadcast([128, 10, 64]), op=ALU.mult),
                        reads=qkk + ["ropec"], writes=[("sq", i2, 0), ("sq", i2, 1)], n=640)
                    for hf in range(2):
                        S.op("pool", lambda e, i2=i2, t=t, hf=hf: e.tensor_tensor(
                            out=rt2[:].rearrange("p (h a f d) -> p h a f d", a=2, f=2, d=16)[:, :, :, hf, :],
                            in0=qkn[i2][:].rearrange("p (h a f d) -> p h a f d", a=2, f=2, d=16)[:, :, :, 1 - hf, :],
                            in1=ropes[:, t, :].rearrange("p (a f d) -> p a f d", a=2, f=2)[:, :, hf, :].unsqueeze(1).to_broadcast([128, 10, 2, 16]),
                            op=ALU.mult),
                            reads=qkk + ["ropes"], writes=[("rt2", hf)], n=320)
                    S.op("dve", lambda e, i2=i2: e.tensor_tensor(out=sq[i2][:], in0=sq[i2][:], in1=rt2[:], op=ALU.add),
                         reads=[("sq", i2, 0), ("sq", i2, 1), ("rt2", 0), ("rt2", 1)], writes=[("sq", i2, 0), ("sq", i2, 1)], n=640)
                    src = sq[i2]
                    srck = [("sq", i2, 0), ("sq", i2, 1)]
                S.op("act", lambda e, i2=i2, src=src: e.copy(
                    out=qb[i2][:, 0:512].rearrange("p (g k d) -> p k g d", k=2, d=64),
                    in_=src[:, 0:512].rearrange("p (k g d) -> p k g d", k=2, d=64)),
                    reads=srck, writes=[("qb", i2, 0)])
                S.op("act", lambda e, i2=i2, src=src: e.copy(out=qb[i2][:, 512:640], in_=src[:, 512:640]),
                     reads=srck, writes=[("qb", i2, 1)], n=128)
                tb = tbank()

                def trq(e, i2=i2, tb=tb):
                    for c in range(5):
                        ins = e.transpose(out=PBb(tb)[:, c * 128:(c + 1) * 128], in_=qb[i2][:, c * 128:(c + 1) * 128], identity=identb[:])
                    return ins
                S.op("pe", trq, reads=[("qb", i2, 0), ("qb", i2, 1), "identb"], writes=[("P", tb)], cost=650)
                qkeys = []
                for c in range(4):
                    qkeys += ak(c * 1024 + t * 128, 128)
                S.op("dve", lambda e, tb=tb, t=t: e.tensor_copy(
                    out=AR[:, 0:4096].rearrange("p (c n) -> p c n", n=1024)[:, :, t * 128:(t + 1) * 128],
                    in_=PBb(tb)[:, 0:512].rearrange("p (c n) -> p c n", n=128)),
                    reads=[("P", tb)], writes=qkeys, cost=430)
                S.op("dve", lambda e, tb=tb, t=t: e.tensor_copy(out=kTz[0:64, 0, t * 128:(t + 1) * 128], in_=PBb(tb)[0:64, 512:640]),
                     reads=[("P", tb), "kTz_z0"], writes=[("kTz", t, 0)], cost=250)
                S.op("dve", lambda e, tb=tb, t=t: e.tensor_copy(out=kTz[64:128, 1, t * 128:(t + 1) * 128], in_=PBb(tb)[64:128, 512:640]),
                     reads=[("P", tb), "kTz_z1"], writes=[("kTz", t, 1)], cost=250)

            release(pc["zf"])
            release(pc["q"])
            release(pc["kv"])
            dbg_dump("hT", hT[:], [128, 8, 1024], [k for t in range(NT) for k in hT_keys(t)], BF16) if p == dbg_pass(dbg) else None
            dbg_dump("AR1", AR[:], [128, 22 * 1024], [("A", b) for b in range(176)], BF16) if p == dbg_pass(dbg) else None
            dbg_dump("v_aug", v_aug[:], [128, NT, 2, 65], [("v_aug", t) for t in range(NT)], BF16) if p == dbg_pass(dbg) else None

            if p == 0:
                build_gt_bc(p, 2, gt_bc[0], 0.5)
                build_gt_bc(p, 5, gt_bc[1], 1.0)
            stop_at('p1', p)
            Y12_OFF = 17 * 1024
            def fourier_ctx(s):
                for gp in range(2):
                    bo = mbank()
                    for gg in range(2):
                        g = gp * 2 + gg
                        b1 = mbank()

                        def mmy(e, b1=b1, g=g, s=s):
                            for (half, Dm) in ((0, dftc256), (1, dfts256)):
                                for tt in range(2):
                                    zoff = ZF_OFF + (2 * s + tt) * 512 + g * 128
                                    ins = e.matmul(PB(b1)[:, half * 256:(half + 1) * 256], lhsT=AR[:, zoff:zoff + 128],
                                                   rhs=Dm[:, tt, :], start=(tt == 0), stop=(tt == 1))
                            return ins
                        S.op("pe", mmy, reads=ak(ZF_OFF + 2 * s * 512, 1024) + ["dftc256", "dfts256"], writes=[("P", b1)], cost=520)
                        yo = Y12_OFF + (g % 2) * 512
                        S.op("act", lambda e, b1=b1, yo=yo: e.copy(out=AR[:, yo:yo + 512], in_=PB(b1)),
                             reads=[("P", b1)], writes=ak(yo, 512))

                        def mmc(e, bo=bo, gg=gg, yo=yo):
                            e.matmul(PB(bo)[:, gg * 256:(gg + 1) * 256], lhsT=chan[:, 0, :], rhs=AR[:, yo:yo + 256], start=True, stop=False)
                            return e.matmul(PB(bo)[:, gg * 256:(gg + 1) * 256], lhsT=chan[:, 1, :], rhs=AR[:, yo + 256:yo + 512], start=False, stop=True)
                        S.op("pe", mmc, reads=ak(yo, 512) + ["chan"], writes=[("P", bo)], cost=260)
                    mk = []
                    for gg in range(2):
                        mk += ak(MIX_OFF + (gp * 2 + gg) * 1024 + s * 256, 256)
                    S.op("dve", lambda e, bo=bo, gp=gp, s=s: e.tensor_copy(
                        out=AR[:, MIX_OFF + gp * 2048:MIX_OFF + (gp + 1) * 2048].rearrange("p (g n) -> p g n", n=1024)[:, :, s * 256:(s + 1) * 256],
                        in_=PB(bo).rearrange("p (g n) -> p g n", n=256)),
                        reads=[("P", bo)], writes=mk, n=512)
            if not latent:
                _mpool[0] = list(MB) + CTX_EXTRA_BANKS
                if not CTX_INTERLEAVE:
                    for s in range(4):
                        fourier_ctx(s)
            else:
                _mpool[0] = list(MB) + LATF_EXTRA_BANKS
                for nh in range(2):
                    Wc, kwc = use(pc["dc", nh])
                    Ws, kws = use(pc["ds", nh])
                    Wcv = Wc[:, :].rearrange("p (k c) -> p k c", c=512)
                    Wsv = Ws[:, :].rearrange("p (k c) -> p k c", c=512)
                    for g in range(4):
                        b1, b2 = mbank(), mbank()
                        for (b, Dm, dk) in ((b1, Wcv, kwc), (b2, Wsv, kws)):
                            def mmy(e, b=b, Dm=Dm, g=g):
                                for tt in range(8):
                                    zoff = ZF_OFF + tt * 512 + g * 128
                                    ins = e.matmul(PB(b), lhsT=AR[:, zoff:zoff + 128], rhs=Dm[:, tt, :],
                                                   start=(tt == 0), stop=(tt == 7))
                                return ins
                            S.op("pe", mmy, reads=ak(ZF_OFF, 4096) + [dk], writes=[("P", b)], cost=1900)
                        S.op("act", lambda e, b1=b1: e.copy(out=AR[:, Y12_OFF:Y12_OFF + 512], in_=PB(b1)),
                             reads=[("P", b1)], writes=ak(Y12_OFF, 512))
                        S.op("dve", lambda e, b2=b2: e.tensor_copy(out=AR[:, Y12_OFF + 512:Y12_OFF + 1024], in_=PB(b2)),
                             reads=[("P", b2)], writes=ak(Y12_OFF + 512, 512))
                        bo = mbank()

                        def mmc(e, bo=bo):
                            e.matmul(PB(bo), lhsT=chan[:, 2, :], rhs=AR[:, Y12_OFF:Y12_OFF + 512], start=True, stop=False)
                            return e.matmul(PB(bo), lhsT=chan[:, 3, :], rhs=AR[:, Y12_OFF + 512:Y12_OFF + 1024], start=False, stop=True)
                        S.op("pe", mmc, reads=ak(Y12_OFF, 1024) + ["chan"], writes=[("P", bo)], cost=480)
                        moff = MIX_OFF + g * 1024 + nh * 512
                        S.op("act", lambda e, bo=bo, moff=moff: e.copy(out=AR[:, moff:moff + 512], in_=PB(bo)),
                             reads=[("P", bo)], writes=ak(moff, 512))
                    release(pc["dc", nh])
                    release(pc["ds", nh])

            stop_at('f', p)
            def o_off(t):
                return ZF_OFF + t * 512

            def evac_pv(bank, ntile, t0, h, di):
                pv = PB(bank)[:, 0:ntile * 65].rearrange("p (n d) -> p n d", d=65)
                S.op("dve", lambda e: e.tensor_scalar(out=den[di][:, 0:ntile], in0=pv[:, :, 64], scalar1=esink[:, h:h + 1],
                                                      scalar2=None, op0=ALU.add),
                     reads=[("P", bank), "esink"], writes=[("den", di)], cost=220)
                S.op("dve", lambda e: e.reciprocal(out=rec[di][:, 0:ntile], in_=den[di][:, 0:ntile]),
                     reads=[("den", di)], writes=[("rec", di)], cost=170)
                okeys = []
                for i in range(ntile):
                    okeys += ak(o_off(t0 + i) + h * 64, 64)
                S.op("dve", lambda e: e.tensor_tensor(
                    out=AR[:, o_off(t0):o_off(t0) + ntile * 512].rearrange("p (n c) -> p n c", c=512)[:, :, h * 64:(h + 1) * 64],
                    in0=pv[:, :, 0:64], in1=rec[di][:, 0:ntile].unsqueeze(2).to_broadcast([128, ntile, 64]), op=ALU.mult),
                    reads=[("P", bank), ("rec", di)], writes=okeys, n=200)

            cntl = [0]
            def attn_ctx(s):
                for kv in range(2):
                    kp = slice(kv * 64, (kv + 1) * 64)
                    for g in range(4):
                        h = kv * 4 + g
                        bs = mbank()

                        def mms(e, bs=bs, kv=kv, g=g, s=s):
                            for jb in range(2):
                                ko = (2 * s + jb) * 128
                                qo = g * 1024 + s * 256
                                ins = e.matmul(PB(bs)[:, jb * 256:(jb + 1) * 256], lhsT=kTz[:, kv, ko:ko + 128], rhs=AR[:, qo:qo + 256],
                                               start=True, stop=True)
                            return ins
                        S.op("pe", mms, reads=[("kTz", 2 * s, kv), ("kTz", 2 * s + 1, kv)] + ak(g * 1024 + s * 256, 256), writes=[("P", bs)], cost=260)
                        pi = cntl[0] % 4
                        cntl[0] += 1
                        po = PTC_OFF + pi * 512
                        S.op("act", lambda e, bs=bs, po=po: e.activation(out=AR[:, po:po + 512], in_=PB(bs), func=AF.Exp, scale=0.125),
                             reads=[("P", bs)], writes=ak(po, 512))
                        bp = mbank()

                        def mmpv(e, bp=bp, po=po, s=s, kv=kv):
                            for qt in range(2):
                                for jb in range(2):
                                    ins = e.matmul(PB(bp)[:, qt * 65:(qt + 1) * 65],
                                                   lhsT=AR[:, po + jb * 256 + qt * 128:po + jb * 256 + (qt + 1) * 128],
                                                   rhs=v_aug[:, 2 * s + jb, kv, :], start=(jb == 0), stop=(jb == 1))
                            return ins
                        S.op("pe", mmpv, reads=ak(po, 512) + [("v_aug", 2 * s), ("v_aug", 2 * s + 1)], writes=[("P", bp)], cost=420)
                        evac_pv(bp, 2, 2 * s, h, cntl[0] % 4)
            if not latent:
                for s in range(4):
                    if CTX_INTERLEAVE:
                        fourier_ctx(s)
                    attn_ctx(s)
            else:
                _mpool[0] = MB[2:] + LAT_EXTRA_BANKS
                for kv in range(2):
                    kp = slice(kv * 64, (kv + 1) * 64)
                    for g in range(4):
                        h = kv * 4 + g
                        def cache_scores(qh, kv=kv, g=g):
                            for cb in range(2):
                                bs = mbank()
                                qo = g * 1024 + qh * 512
                                S.op("pe", lambda e, bs=bs, cb=cb, qo=qo, kv=kv: e.matmul(
                                    PB(bs), lhsT=ckT[:, kv, cb, :], rhs=AR[:, qo:qo + 512], start=True, stop=True),
                                    reads=["ckT"] + ak(qo, 512), writes=[("P", bs)], cost=240)
                                po = PTC_OFF + (qh * 2 + cb) * 512
                                S.op("act", lambda e, bs=bs, po=po: e.activation(out=AR[:, po:po + 512], in_=PB(bs), func=AF.Exp, scale=0.125),
                                     reads=[("P", bs)], writes=ak(po, 512))
                        cache_scores(0)
                        bpv = MB[0:2]

                        def pv_group(qt, kv=kv, g=g, bpv=bpv):
                            jbs = [jb for jb in (qt - 1, qt, qt + 1) if 0 <= jb < 8]
                            reads = []
                            for jb in jbs:
                                reads += ak(PTL_OFF + (jb % 4) * 384, 384) + [("v_aug", jb)]
                            reads += ak(PTC_OFF, 2048) + ["cv_aug"]
                            bank = bpv[qt // 4]

                            def fn(e):
                                n = len(jbs) + 2
                                i = 0
                                for jb in jbs:
                                    lo = PTL_OFF + (jb % 4) * 384 + (qt - jb + 1) * 128
                                    e.matmul(PB(bank)[:, (qt % 4) * 65:(qt % 4 + 1) * 65], lhsT=AR[:, lo:lo + 128],
                                             rhs=v_aug[:, jb, kv, :], start=(i == 0), stop=False)
                                    i += 1
                                for cb in range(2):
                                    lo = PTC_OFF + ((qt // 4) * 2 + cb) * 512 + (qt % 4) * 128
                                    ins = e.matmul(PB(bank)[:, (qt % 4) * 65:(qt % 4 + 1) * 65], lhsT=AR[:, lo:lo + 128],
                                                   rhs=cv_aug[:, cb, kv, :], start=False, stop=(cb == 1))
                                return ins
                            S.op("pe", fn, reads=reads, writes=[("P", bank)], cost=520)

                        for jb in range(8):
                            if jb == CACHE_QH1_AT:
                                cache_scores(1)
                            qlo, qhi = max(jb - 1, 0), min(jb + 1, 7)
                            nq = (qhi - qlo + 1) * 128
                            c0 = (qlo - (jb - 1)) * 128
                            bs = mbank()
                            ko = jb * 128
                            qo = g * 1024 + qlo * 128
                            S.op("pe", lambda e, bs=bs, ko=ko, qo=qo, nq=nq, c0=c0, kv=kv: e.matmul(
                                PB(bs)[:, c0:c0 + nq], lhsT=kTz[:, kv, ko:ko + 128], rhs=AR[:, qo:qo + nq], start=True, stop=True),
                                reads=[("kTz", jb, kv)] + ak(qo, nq), writes=[("P", bs)], cost=190)
                            po = PTL_OFF + (jb % 4) * 384
                            S.op("act", lambda e, bs=bs, po=po, c0=c0, nq=nq: e.activation(
                                out=AR[:, po + c0:po + c0 + nq], in_=PB(bs)[:, c0:c0 + nq], func=AF.Exp, scale=0.125),
                                reads=[("P", bs)], writes=ak(po, 384), n=384)
                            if 1 <= jb <= 6:
                                S.op("pool", lambda e, po=po: e.tensor_tensor(
                                    out=AR[:, po:po + 384].rearrange("p (b n) -> p b n", n=128)[:, 0:3:2, :],
                                    in0=AR[:, po:po + 384].rearrange("p (b n) -> p b n", n=128)[:, 0:3:2, :],
                                    in1=mask3[:, :].rearrange("p (b n) -> p b n", n=128)[:, 0:3:2, :], op=ALU.mult),
                                    reads=ak(po, 384) + ["mask3"], writes=ak(po, 384), n=256)
                            else:
                                mb_ = 2 if jb == 0 else 0
                                S.op("pool", lambda e, po=po, mb_=mb_: e.tensor_tensor(
                                    out=AR[:, po + mb_ * 128:po + (mb_ + 1) * 128], in0=AR[:, po + mb_ * 128:po + (mb_ + 1) * 128],
                                    in1=mask3[:, mb_ * 128:(mb_ + 1) * 128], op=ALU.mult),
                                    reads=ak(po, 384) + ["mask3"], writes=ak(po, 384), n=128)
                            if jb >= 1:
                                pv_group(jb - 1)
                                if jb - 1 == 3:
                                    evac_pv(bpv[0], 4, 0, h, (2 * h) % 4)
                        pv_group(7)
                        evac_pv(bpv[1], 4, 4, h, (2 * h + 1) % 4)
                _mpool[0] = list(MB)

            dbg_dump("o_tm", AR[:, ZF_OFF:ZF_OFF + 4096], [128, 4096], ak(ZF_OFF, 4096), BF16) if p == dbg_pass(dbg) else None

            _mpool[0] = list(MB)
            for t in range(NT):
                tb = tbank()

                def tro(e, tb=tb, t=t):
                    for c in range(4):
                        ins = e.transpose(out=PBb(tb)[:, c * 128:(c + 1) * 128], in_=AR[:, o_off(t) + c * 128:o_off(t) + (c + 1) * 128],
                                          identity=identb[:])
                    return ins
                S.op("pe", tro, reads=ak(o_off(t), 512) + ["identb"], writes=[("P", tb)], cost=520)
                okeys = []
                for c in range(4):
                    okeys += ak(OT_OFF + c * 1024 + t * 128, 128)
                S.op("act", lambda e, tb=tb, t=t: e.copy(
                    out=AR[:, OT_OFF:OT_OFF + 4096].rearrange("p (c n) -> p c n", n=1024)[:, :, t * 128:(t + 1) * 128],
                    in_=PBb(tb)[:, 0:512].rearrange("p (c n) -> p c n", n=128)),
                    reads=[("P", tb)], writes=okeys)

            dbg_dump("AR2", AR[:], [128, 22 * 1024], [("A", b) for b in range(176)], BF16) if p == dbg_pass(dbg) else None

            stop_at('a', p)
            _mpool[0] = list(MB) + P3_EXTRA_BANKS
            for j in range(2):
                Wgf, kgf = use(pc["gf", j])
                Wga, kga = use(pc["ga", j])
                Wfo, kfo = use(pc["fo", j])
                Wgfv = Wgf[:, :].rearrange("p (k c) -> p k c", c=512)
                Wgav = Wga[:, :].rearrange("p (k c) -> p k c", c=512)
                Wfov = Wfo[:, :].rearrange("p (k c) -> p k c", c=512)
                for fc in range(4):
                    f = j * 4 + fc
                    fcols = slice(fc * 128, (fc + 1) * 128)
                    for stl in range(2):
                        ncols = slice(stl * 512, (stl + 1) * 512)
                        hk = [k for t in range(4 * stl, 4 * stl + 4) for k in hT_keys(t)]
                        bg1, bg2, by1, by2 = mbank(), mbank(), mbank(), mbank()

                        def mmg(e, b, Wv, fcols=fcols, ncols=ncols):
                            for k in range(8):
                                ins = e.matmul(PB(b), lhsT=Wv[:, k, fcols], rhs=hT[:, k, ncols], start=(k == 0), stop=(k == 7))
                            return ins
                        S.op("pe", lambda e, b=bg1, Wv=Wgfv, mmg=mmg: mmg(e, b, Wv), reads=hk + [kgf], writes=[("P", bg1)], cost=1810)
                        S.op("pe", lambda e, b=bg2, Wv=Wgav, mmg=mmg: mmg(e, b, Wv), reads=hk + [kga], writes=[("P", bg2)], cost=1810)

                        def mmy2(e, b, koff, aoff, fcols=fcols, stl=stl, Wfov=Wfov):
                            for k in range(4):
                                o = aoff + k * 1024 + stl * 512
                                ins = e.matmul(PB(b), lhsT=Wfov[:, koff + k, fcols], rhs=AR[:, o:o + 512], start=(k == 0), stop=(k == 3))
                            return ins
                        mixk = [x for k in range(4) for x in ak(MIX_OFF + k * 1024 + stl * 512, 512)]
                        otk = [x for k in range(4) for x in ak(OT_OFF + k * 1024 + stl * 512, 512)]
                        S.op("pe", lambda e, b=by1, mmy2=mmy2: mmy2(e, b, 0, MIX_OFF), reads=mixk + [kfo], writes=[("P", by1)], cost=940)
                        S.op("pe", lambda e, b=by2, mmy2=mmy2: mmy2(e, b, 4, OT_OFF), reads=otk + [kfo], writes=[("P", by2)], cost=940)
                        ti = (f * 2 + stl) % 2
                        S.op("act", lambda e, b=bg1, ti=ti: e.activation(out=tg[ti][:, 0:512], in_=PB(b), func=AF.Tanh, scale=0.5),
                             reads=[("P", bg1)], writes=[("tg", ti, 0)])
                        S.op("act", lambda e, b=bg2, ti=ti: e.activation(out=tg[ti][:, 512:1024], in_=PB(b), func=AF.Tanh, scale=0.5),
                             reads=[("P", bg2)], writes=[("tg", ti, 1)])
                        S.op("dve", lambda e, b=by1, ti=ti: e.scalar_tensor_tensor(
                            out=tg[ti][:, 0:512], in0=tg[ti][:, 0:512], scalar=1.0, in1=PB(b), op0=ALU.add, op1=ALU.mult),
                            reads=[("tg", ti, 0), ("P", by1)], writes=[("tg", ti, 0)])
                        S.op("dve", lambda e, b=by2, ti=ti: e.scalar_tensor_tensor(
                            out=tg[ti][:, 512:1024], in0=tg[ti][:, 512:1024], scalar=1.0, in1=PB(b), op0=ALU.add, op1=ALU.mult),
                            reads=[("tg", ti, 1), ("P", by2)], writes=[("tg", ti, 1)])
                        S.op("dve", lambda e, ti=ti, f=f, ncols=ncols: e.tensor_tensor(
                            out=mT[:, f, ncols], in0=tg[ti][:, 0:512], in1=tg[ti][:, 512:1024], op=ALU.add),
                            reads=[("tg", ti, 0), ("tg", ti, 1)], writes=[("mT", f, stl)], cost=420)
                release(pc["gf", j])
                release(pc["ga", j])
                release(pc["fo", j])

            dbg_dump("mT", mT[:], [128, 8, 1024], [("mT", f, s_) for f in range(8) for s_ in range(2)], BF16) if p == dbg_pass(dbg) else None

            _mpool[0] = list(MB)
            stop_at('m', p)
            Wos = []
            for ch in range(2):
                Wo, kwo = use(pc["wo", ch])
                Wov_ = Wo[:, :].rearrange("p (k c) -> p k c", c=512)
                for kq in range(4):
                    S.op("dve", lambda e, Wov_=Wov_, ch=ch, kq=kq: e.tensor_tensor(
                        out=Wov_[:, 2 * kq:2 * kq + 2, :], in0=Wov_[:, 2 * kq:2 * kq + 2, :],
                        in1=gt_bc[0][:, ch * 512:(ch + 1) * 512].unsqueeze(1).to_broadcast([128, 2, 512]), op=ALU.mult),
                        reads=[kwo, gt1k[ch]], writes=[kwo], n=1024)
                Wos.append((Wov_, kwo))
            for t in range(NT):
                for ch in range(2):
                    Wov, kwo = Wos[ch]
                    b = mbank()

                    def mmo(e, b=b, t=t, Wov=Wov):
                        for k in range(8):
                            ins = e.matmul(PB(b), lhsT=mT[:, k, t * 128:(t + 1) * 128], rhs=Wov[:, k, :], start=(k == 0), stop=(k == 7))
                        return ins
                    S.op("pe", mmo, reads=[("mT", f, t // 4) for f in range(8)] + [kwo], writes=[("P", b)], cost=1900)
                    S.op("dve", lambda e, b=b, t=t, ch=ch: e.tensor_tensor(
                        out=x_res[:, t, ch * 512:(ch + 1) * 512], in0=PB(b), in1=x_res[:, t, ch * 512:(ch + 1) * 512], op=ALU.add),
                        reads=[("P", b), ("x", t)], writes=[("x", t)])
                norm_tile(p, t, 1, s2T, 24)
            release(pc["wo", 0])
            release(pc["wo", 1])

            dbg_dump("x1", x_res[:], [128, NT, D], [("x", t) for t in range(NT)]) if p == dbg_pass(dbg) else None
            stop_at('x1', p)
            stop_at('n2', p)

            _mpool[0] = list(MB) + P4_EXTRA_BANKS
            for jc in range(11):
                Wu, kwu = use(pc["up", jc])
                Wuv = Wu[:, :].rearrange("p (k c) -> p k c", c=512)
                for half in range(2):
                    jj = jc * 2 + half
                    for stl in range(2):
                        ncols = slice(stl * 512, (stl + 1) * 512)
                        hk = [k for t in range(4 * stl, 4 * stl + 4) for k in hT_keys(t)]
                        ba, bu = mbank(), mbank()

                        def mmu(e, b, c0, Wuv=Wuv, ncols=ncols):
                            for k in range(8):
                                ins = e.matmul(PB(b), lhsT=Wuv[:, k, c0:c0 + 128], rhs=hT[:, k, ncols], start=(k == 0), stop=(k == 7))
                            return ins
                        S.op("pe", lambda e, b=ba, c0=half * 128, mmu=mmu: mmu(e, b, c0), reads=hk + [kwu], writes=[("P", ba)], cost=1800)
                        S.op("pe", lambda e, b=bu, c0=256 + half * 128, mmu=mmu: mmu(e, b, c0), reads=hk + [kwu], writes=[("P", bu)], cost=1800)
                        si = (jj * 2 + stl) % 2
                        S.op("act", lambda e, b=ba, si=si: e.activation(out=sa[si][:], in_=PB(b), func=AF.Silu),
                             reads=[("P", ba)], writes=[("sa", si)])
                        ao = jj * 1024 + stl * 512
                        S.op("dve", lambda e, b=bu, si=si, ao=ao: e.tensor_tensor(out=AR[:, ao:ao + 512], in0=sa[si][:], in1=PB(b), op=ALU.mult),
                             reads=[("sa", si), ("P", bu)], writes=ak(ao, 512))
                release(pc["up", jc])

            stop_at('up', p)
            _mpool[0] = list(MB)
            for cq in range(4):
                Wd0, kd0 = use(pc["dn", cq, 0])
                Wd1, kd1 = use(pc["dn", cq, 1])
                Wd = [Wd0[:, 0:2816].rearrange("p (k c) -> p k c", c=256), Wd1[:, 0:2816].rearrange("p (k c) -> p k c", c=256)]
                for (Wd_, kd_) in ((Wd[0], kd0), (Wd[1], kd1)):
                    S.op("pool", lambda e, Wd_=Wd_, cq=cq: e.tensor_tensor(
                        out=Wd_, in0=Wd_, in1=gt_bc[1][:, cq * 256:(cq + 1) * 256].unsqueeze(1).to_broadcast([128, 11, 256]), op=ALU.mult),
                        reads=[kd_, gt2k[cq // 2]], writes=[kd_], n=2816)
                for t in range(NT):
                    b = mbank()

                    def mmd(e, b=b, t=t, Wd=Wd):
                        for k in range(22):
                            o = k * 1024 + t * 128
                            ins = e.matmul(PB(b)[:, 0:256], lhsT=AR[:, o:o + 128], rhs=Wd[k // 11][:, k % 11, :], start=(k == 0), stop=(k == 21))
                        return ins
                    S.op("pe", mmd, reads=[x for k in range(22) for x in ak(k * 1024 + t * 128, 128)] + [kd0, kd1], writes=[("P", b)], cost=2400)
                    S.op("dve", lambda e, b=b, t=t, cq=cq: e.tensor_tensor(
                        out=x_res[:, t, cq * 256:(cq + 1) * 256], in0=PB(b)[:, 0:256], in1=x_res[:, t, cq * 256:(cq + 1) * 256], op=ALU.add),
                        reads=[("P", b), ("x", t)], writes=[("x", t)], n=256)
                    if cq == 3:
                        S.dma("sp", y_out[p, t * 128:(t + 1) * 128, :], x_res[:, t, :], reads=[("x", t)], nbytes=512 * 1024)
                release(pc["dn", cq, 0])
                release(pc["dn", cq, 1])

        except _Stop:
            pass
        S.finish("sp")
        S.emit(window=SCHED_WINDOW, reorder=SCHED_REORDER)
        _LAST['S'] = S
    return nc, dbg_outs


def dbg_pass(dbg):
    for d in dbg:
        if isinstance(d, tuple) and d[0] == "pass":
            return d[1]
    return 0


def _constants():
    c = {}
    c["ident"] = np.eye(128, dtype=np.float32)
    n = np.arange(1024, dtype=np.float64)
    ang = 2.0 * np.pi * np.outer(n, n) / 1024.0
    c["dft_c"] = np.cos(ang).astype(np.float32)
    c["dft_s"] = np.sin(ang).astype(np.float32)
    n2 = np.arange(256, dtype=np.float64)
    ang2 = 2.0 * np.pi * np.outer(n2, n2) / 256.0
    c["dft_c256"] = np.cos(ang2).astype(np.float32)
    c["dft_s256"] = np.sin(ang2).astype(np.float32)
    m = np.arange(128, dtype=np.float64)
    angc = 2.0 * np.pi * np.outer(m, m) / 128.0
    chan = np.zeros((128, 4, 128), np.float32)
    for i, N in enumerate((256, 1024)):
        sc = 1.0 / np.sqrt(N * 128.0)
        chan[:, 2 * i, :] = np.cos(angc) * sc
        chan[:, 2 * i + 1, :] = -np.sin(angc) * sc
    c["chan"] = chan
    pos = np.arange(1024)
    row = (pos // 64).astype(np.float32)
    col = (pos % 64).astype(np.float32)
    inv = (10000.0 ** (-np.arange(0, 32, 2, dtype=np.float32) / 32)).astype(np.float32)
    ar = row[:, None] * inv[None, :]
    ac = col[:, None] * inv[None, :]
    c["rope_cos"] = np.concatenate([np.cos(ar), np.cos(ar), np.cos(ac), np.cos(ac)], axis=1).astype(np.float32)
    c["rope_sin"] = np.concatenate([-np.sin(ar), np.sin(ar), -np.sin(ac), np.sin(ac)], axis=1).astype(np.float32)
    a = np.arange(128)[:, None]
    b = np.arange(128)[None, :]
    mask3 = np.concatenate([(a <= b), np.ones((128, 128), bool), (b <= a)], axis=1).astype(np.float32)
    c["mask3"] = mask3
    return c


def make_in_maps(x_prompt, x_sample, cache_k, cache_v, c, c_ctx, w_ada, b_ada, g_norm1, g_norm2,
                 w_in, g_q, g_k, sinks, w_f, w_ao, w_out, w_up, w_down):
    f = lambda a: np.ascontiguousarray(np.asarray(a, dtype=np.float32))
    consts = _constants()
    shared = {
        "w_ada": f(w_ada[0]), "w_in": f(w_in[0]), "w_f": f(w_f[0]), "w_ao": f(w_ao[0]),
        "w_out": f(w_out[0]), "w_up": f(w_up[0]), "w_down": f(w_down[0]),
        "b_adaT": f(np.asarray(b_ada[0]).reshape(48, 128).T),
        "gn": f(np.concatenate([np.asarray(g_norm1[0]).reshape(8, 128).T, np.asarray(g_norm2[0]).reshape(8, 128).T], axis=1)),
        "gqk": f(np.broadcast_to(np.concatenate([np.tile(np.asarray(g_q[0]), 8), np.tile(np.asarray(g_k[0]), 2)])[None, :], (128, 640))),
        "sinks_bc": f(np.broadcast_to(np.asarray(sinks[0])[None, :], (128, 8))),
    }
    shared.update(consts)
    xp = np.asarray(x_prompt, dtype=np.float32)
    xs = np.asarray(x_sample, dtype=np.float32)
    maps = []
    for i in range(NCORES):
        m = dict(shared)
        m["xin"] = f(np.stack([xp[4 * i:4 * i + 4].reshape(1024, D), xs[i]], axis=0))
        m["ck"] = f(np.asarray(cache_k)[i, 0].reshape(256, 128))
        m["cv"] = f(np.asarray(cache_v)[i, 0].reshape(256, 128))
        cv2 = np.stack([np.asarray(c_ctx), np.asarray(c)[i]], axis=1)
        m["cT"] = f(cv2.reshape(8, 128, 2).transpose(1, 0, 2).reshape(128, 16))
        maps.append(m)
    return maps


_NC_CACHE = {}


def kernel(**inputs):
    if "nc" not in _NC_CACHE:
        _NC_CACHE["nc"] = build_program()[0]
    nc = _NC_CACHE["nc"]
    in_maps = make_in_maps(**inputs)
    res = run_bass_kernel_spmd(nc, in_maps, core_ids=list(range(NCORES)))
    r = res.results
    y_prompt = np.concatenate([r[i]["y"][0].reshape(4, 256, D) for i in range(NCORES)], axis=0)
    y_sample = np.stack([r[i]["y"][1] for i in range(NCORES)], axis=0)
    nk = np.concatenate([r[i]["nk"].reshape(4, 1, 256, 2, 64) for i in range(NCORES)], axis=0)
    nv = np.concatenate([r[i]["nv"].reshape(4, 1, 256, 2, 64) for i in range(NCORES)], axis=0)
    return (y_prompt.astype(np.float32), y_sample.astype(np.float32), nk.astype(np.float32), nv.astype(np.float32))
```

```python
import numpy as np
from contextlib import ExitStack
import concourse.bass as bass
import concourse.mybir as mybir
from concourse.bass_utils import run_bass_kernel_spmd
from concourse.alu_op_type import AluOpType as ALU

F32 = mybir.dt.float32
BF16 = mybir.dt.bfloat16
I32 = mybir.dt.int32
AF = mybir.ActivationFunctionType
AX = mybir.AxisListType

D = 1024
NT = 8
DFF = 2816
EPS = 1e-6
NCORES = 8
SCHED_WINDOW = 40
N_TBANKS = 3
CTX_INTERLEAVE = True
FFN_HEAD_GROUP = 2
CACHE_QH1_AT = 4
LAT_EXTRA_BANKS = [0, 1, 2]
CTX_EXTRA_BANKS = [0, 1, 2]
P3_EXTRA_BANKS = [0, 1, 2]
P4_EXTRA_BANKS = []
LATF_EXTRA_BANKS = [2]
SCHED_REORDER = True
_LAST = {}


class Sched:
    QUEUES = ["pe", "act", "dve", "pool", "sp"]
    DEFCOST = {"pe": 1000.0, "act": 600.0, "dve": 690.0, "pool": 1260.0, "sp": 100.0}

    def __init__(self, nc, stack, n_sp_slots=16, n_pool_slots=24):
        self.nc = nc
        self.sem = {e: stack.enter_context(nc.semaphore(f"s_{e}")) for e in self.QUEUES}
        self.dpool = {
            "sp": [stack.enter_context(nc.semaphore(f"dsp_{i}")) for i in range(n_sp_slots)],
            "pool": [stack.enter_context(nc.semaphore(f"dpl_{i}")) for i in range(n_pool_slots)],
        }
        self.ops = []
        self.lastw = {}
        self.readers = {}
        self.final_wait = None

    def _mkdeps(self, reads, writes, q=None):
        deps = {}

        def add(i, raw):
            deps[i] = deps.get(i, False) or raw
        for r in reads:
            i = self.lastw.get(r)
            if i is not None:
                add(i, True)
            if isinstance(r, tuple) and r[0] == "P":
                for i in self.readers.get(r, ()):
                    if self.ops[i]["q"] != q:
                        add(i, False)
        for w in writes:
            i = self.lastw.get(w)
            if i is not None:
                add(i, False)
            for i in self.readers.get(w, ()):
                add(i, False)
        return deps

    def _record(self, idx, reads, writes):
        for r in reads:
            self.readers.setdefault(r, []).append(idx)
        for w in writes:
            self.lastw[w] = idx
            self.readers[w] = []

    def op(self, q, fn, reads=(), writes=(), cost=None, n=None):
        if cost is None and n is not None:
            cost = {"act": 170 + 0.833 * n, "dve": 155 + 1.04 * n, "pool": 130 + 2.2 * n, "pe": 1000.0}[q]
        idx = len(self.ops)
        deps = self._mkdeps(reads, writes, q)
        import sys as _sys
        self.ops.append(dict(q=q, kind="op", fn=fn, deps=deps, cost=cost if cost is not None else self.DEFCOST[q],
                             line=_sys._getframe(1).f_lineno))
        self._record(idx, reads, writes)
        return idx

    def dma(self, q, out, in_, reads=(), writes=(), nbytes=65536, **kw):
        idx = len(self.ops)
        deps = self._mkdeps(reads, writes)

        def fn(e, out=out, in_=in_, kw=kw):
            return e.dma_start(out=out, in_=in_, **kw)
        import sys as _sys
        self.ops.append(dict(q=q, kind="dma", fn=fn, deps=deps, cost=float(nbytes) / 330.0, line=_sys._getframe(1).f_lineno))
        self._record(idx, reads, writes)
        return idx

    def finish(self, q="sp"):
        self.final_wait = q

    def schedule(self, window=40, reorder=True):
        ops = self.ops
        n = len(ops)
        fin = [None] * n
        pend = {q: [i for i in range(n) if ops[i]["q"] == q] for q in self.QUEUES}
        head = {q: 0 for q in self.QUEUES}
        done = [False] * n
        qfree = {q: 0.0 for q in self.QUEUES}
        dma_free = 0.0
        order = {q: [] for q in self.QUEUES}
        remaining = n
        while remaining:
            best = None
            for q in self.QUEUES:
                lst = pend[q]
                h = head[q]
                while h < len(lst) and done[lst[h]]:
                    h += 1
                head[q] = h
                cnt = 0
                j = h
                while j < len(lst) and cnt < window:
                    i = lst[j]
                    j += 1
                    if done[i]:
                        continue
                    cnt += 1
                    t = qfree[q]
                    ok = True
                    for d in ops[i]["deps"]:
                        f = fin[d]
                        if f is None:
                            ok = False
                            break
                        if f > t:
                            t = f
                    if ok:
                        kt = t - 4000.0 if ops[i]["kind"] == "dma" else t
                        if best is None or (kt, i) < best[0]:
                            best = ((kt, i), q, i, t)
                    if not reorder:
                        break
            assert best is not None, "scheduler deadlock"
            _, q, i, t = best
            o = ops[i]
            if o["kind"] == "dma":
                issue = 1000.0 if q == "pool" else 120.0
                qfree[q] = t + issue
                xs = max(t + issue, dma_free)
                dma_free = xs + o["cost"]
                fin[i] = dma_free + 2000.0
            else:
                qfree[q] = t + o["cost"]
                fin[i] = qfree[q] + (100.0 if q != "pe" else 250.0)
            o["start"] = t
            o["fin"] = fin[i]
            done[i] = True
            order[q].append(i)
            remaining -= 1
        self.order = order
        self.sim_time = max(f for f in fin if f is not None) if n else 0.0
        return order

    def emit(self, window=40, reorder=True):
        ops = self.ops
        order = self.schedule(window=window, reorder=reorder)
        duse = {}
        for q in self.QUEUES:
            seq = 0
            k = 0
            for i in order[q]:
                o = ops[i]
                if o["kind"] == "op":
                    seq += 1
                    o["tok"] = (q, seq)
                else:
                    pool = self.dpool[q]
                    slot = k % len(pool)
                    k += 1
                    key = ("d", q, slot)
                    prev = duse.get(key, 0)
                    o["prev_tok"] = (key, 16 * prev) if prev else None
                    duse[key] = prev + 1
                    o["tok"] = (key, 16 * (prev + 1))
        self._duse = duse

        def semobj(semkey):
            if isinstance(semkey, tuple):
                return self.dpool[semkey[1]][semkey[2]]
            return self.sem[semkey]

        def run(q, e):
            seen = {}
            for i in order[q]:
                o = ops[i]
                need = {}
                for d, raw in o["deps"].items():
                    od = ops[d]
                    if od["kind"] == "op" and od["q"] == q:
                        if q == "pe":
                            continue
                    sk, val = od["tok"]
                    if need.get(sk, 0) < val:
                        need[sk] = val
                if o["kind"] == "dma" and o["prev_tok"] is not None:
                    sk, val = o["prev_tok"]
                    if need.get(sk, 0) < val:
                        need[sk] = val
                for sk, val in need.items():
                    if seen.get(sk, 0) >= val:
                        continue
                    seen[sk] = val
                    e.wait_ge(semobj(sk), val)
                ins = o["fn"](e)
                sk, val = o["tok"]
                if o["kind"] == "op":
                    ins.then_inc(self.sem[q], 1)
                else:
                    ins.then_inc(semobj(sk), 16)
            if self.final_wait == q:
                for key, u in duse.items():
                    if seen.get(key, 0) < 16 * u:
                        e.wait_ge(semobj(key), 16 * u)

        with self.nc.Block() as block:
            @block.tensor
            def _(e):
                run("pe", e)

            @block.scalar
            def _(e):
                run("act", e)

            @block.vector
            def _(e):
                run("dve", e)

            @block.gpsimd
            def _(e):
                run("pool", e)

            @block.sync
            def _(e):
                run("sp", e)


class _Stop(Exception):
    pass


def build_program(dbg=(), stop=None):
    nc = bass.Bass("TRN2", target_bir_lowering=False)

    def din(name, shape, dt=F32):
        return nc.dram_tensor(name, list(shape), dt, kind="ExternalInput").ap()

    def dout(name, shape, dt=F32):
        return nc.dram_tensor(name, list(shape), dt, kind="ExternalOutput").ap()

    xin = din("xin", [2, 1024, D])
    ck_in = din("ck", [256, 128])
    cv_in = din("cv", [256, 128])
    cT_in = din("cT", [128, 16])
    badaT_in = din("b_adaT", [128, 48])
    gn_in = din("gn", [128, 16])
    gqk_in = din("gqk", [128, 640])
    sinks_in = din("sinks_bc", [128, 8])
    w_ada = din("w_ada", [D, 6 * D])
    w_in = din("w_in", [D, 3328])
    w_f = din("w_f", [512, D])
    w_ao = din("w_ao", [512, D])
    w_out = din("w_out", [D, D])
    w_up = din("w_up", [D, 2 * DFF])
    w_down = din("w_down", [DFF, D])
    ident_in = din("ident", [128, 128])
    dftc_in = din("dft_c", [1024, 1024])
    dfts_in = din("dft_s", [1024, 1024])
    dftc256_in = din("dft_c256", [256, 256])
    dfts256_in = din("dft_s256", [256, 256])
    chan_in = din("chan", [128, 4, 128])
    ropec_in = din("rope_cos", [1024, 64])
    ropes_in = din("rope_sin", [1024, 64])
    mask_in = din("mask3", [128, 384])

    y_out = dout("y", [2, 1024, D])
    nk_out = dout("nk", [1024, 128])
    nv_out = dout("nv", [1024, 128])

    dbg_outs = {}

    with ExitStack() as st:
        S = Sched(nc, st)
        _cnt = [0]

        def sb(name, shape, dt):
            _cnt[0] += 1
            nb = int(np.prod(shape[1:])) * (4 if dt in (F32, I32) else 2)
            _LAST.setdefault('alloc', []).append((name, nb))
            return st.enter_context(nc.sbuf_tensor(f"sb_{name}", list(shape), dt))

        x_res = sb("x_res", [128, NT, D], F32)
        hT = sb("hT", [128, 8, 1024], BF16)
        mT = sb("mT", [128, 8, 1024], BF16)
        AR = sb("arena", [128, 22 * 1024], BF16)
        NSLOT = 5
        wslot = [sb(f"w{i}", [128, 4096], BF16) for i in range(NSLOT)]
        v_aug = sb("v_aug", [128, NT, 2, 65], BF16)
        cv_aug = sb("cv_aug", [128, 2, 2, 65], BF16)
        ckT = sb("ckT", [128, 2, 2, 128], BF16)
        kTz = sb("kTz", [128, 2, 1024], BF16)
        dftc256 = sb("dftc256", [128, 2, 256], BF16)
        dfts256 = sb("dfts256", [128, 2, 256], BF16)
        chan = sb("chan", [128, 4, 128], BF16)
        identf = sb("identf", [128, 128], F32)
        identb = sb("identb", [128, 128], BF16)
        ropec = sb("ropec", [128, NT, 64], F32)
        ropes = sb("ropes", [128, NT, 64], F32)
        mask3 = sb("mask3", [128, 384], BF16)
        gqk = sb("gqk", [128, 640], F32)
        esink = sb("esink", [128, 8], F32)
        cT = sb("cT", [128, 16], F32)
        siluT = sb("siluT", [128, 16], BF16)
        badaT = sb("badaT", [128, 48], F32)
        gn = sb("gn", [128, 16], F32)
        modT = sb("modT", [128, 48, 2], F32)
        s1T = sb("s1T", [128, 8, 2], F32)
        s2T = sb("s2T", [128, 8, 2], F32)
        gtrep = sb("gtrep", [128, 128], F32)
        gt_bc = [sb("gt1bc", [128, D], F32), sb("gt2bc", [128, D], F32)]
        xnb = [sb(f"xnb{i}", [128, D], BF16) for i in range(2)]
        ss = sb("ss", [128, 2 * NT], F32)
        vv = sb("vv", [128, 2 * NT], F32)
        rstd = sb("rstd", [128, 2 * NT], F32)
        neghalf = sb("neghalf", [128, 16], F32)
        sq = [sb(f"sq{i}", [128, 640], F32) for i in range(2)]
        ssqk = [sb(f"ssqk{i}", [128, 10], F32) for i in range(2)]
        vqk = [sb(f"vqk{i}", [128, 10], F32) for i in range(2)]
        rqk = [sb(f"rqk{i}", [128, 10], F32) for i in range(2)]
        qkn = [sb(f"qkn{i}", [128, 640], F32) for i in range(2)]
        rt2 = sb("rt2", [128, 640], F32)
        qb = [sb(f"qb{i}", [128, 640], BF16) for i in range(2)]
        vf = [sb(f"vf{i}", [128, 128], F32) for i in range(2)]
        ckb = sb("ckb", [128, 2, 128], BF16)
        den = [sb(f"den{i}", [128, 4], F32) for i in range(4)]
        rec = [sb(f"rec{i}", [128, 4], F32) for i in range(4)]
        tg = [sb(f"tg{i}", [128, 1024], BF16) for i in range(2)]
        sa = [sb(f"sa{i}", [128, 512], BF16) for i in range(2)]

        pbank = [st.enter_context(nc.psum_tensor(f"pb{i}", [128, 512], F32)) for i in range(8)]
        _tnext = [0]
        _mnext = [0]

        def tbank():
            i = _tnext[0] % N_TBANKS
            _tnext[0] += 1
            return i

        MB = list(range(N_TBANKS, 8))
        _mpool = [list(MB)]

        def mbank():
            pool = _mpool[0]
            i = pool[_mnext[0] % len(pool)]
            _mnext[0] += 1
            return i

        def PB(i):
            return pbank[i][:]

        def PBb(i):
            return pbank[i][:].bitcast(BF16)

        def ak(off, n):
            return [("A", b) for b in range(off // 128, (off + n - 1) // 128 + 1)]

        def qT_off(c, t0, n):
            return c * 1024 + t0 * 128, n * 128
        KT_OFF = 4 * 1024
        MIX_OFF = 5 * 1024
        OT_OFF = 9 * 1024
        ZF_OFF = 13 * 1024
        PTL_OFF = 18 * 1024
        PTC_OFF = 20 * 1024

        def dbg_dump(name, ap, shape, reads, dt=F32):
            if name not in dbg:
                return
            o = dout("dbg_" + name, shape, dt)
            dbg_outs[name] = (shape, dt)
            S.dma("sp", o, ap, reads=reads)

        chunks = []

        def add_chunk(parts):
            chunks.append(parts)
            return len(chunks) - 1

        emitted = [0]
        slot_of = {}
        free_slots = list(range(NSLOT))

        def pump():
            while emitted[0] < len(chunks) and free_slots:
                i = emitted[0]
                s = free_slots.pop(0)
                slot_of[i] = s
                for (mk_dst, src) in chunks[i]:
                    S.dma("pool", mk_dst(wslot[s]), src, writes=[("W", s)], nbytes=4 * int(np.prod(src.shape)))
                emitted[0] += 1

        def prefetch(upto=None):
            pump()

        def use(i):
            pump()
            assert i in slot_of, f"weight chunk {i} not resident"
            return wslot[slot_of[i]], ("W", slot_of[i])

        def release(i):
            free_slots.append(slot_of[i])
            pump()

        def kview(rows_k, cols):
            return lambda w, rows_k=rows_k, cols=cols: w[:, 0:rows_k * cols].rearrange("p (k c) -> p k c", c=cols)

        def kview_off(off, rows_k, cols):
            return lambda w, off=off, rows_k=rows_k, cols=cols: w[:, off:off + rows_k * cols].rearrange("p (k c) -> p k c", c=cols)

        def wsrc(wap, r0, nr, c0, ncol):
            return wap[r0:r0 + nr, c0:c0 + ncol].rearrange("(k p) c -> p k c", p=128)

        ada_chunks = [add_chunk([(kview(8, 512), wsrc(w_ada, 0, 1024, cc * 512, 512))]) for cc in range(4)]
        pass_chunks = []
        for p in range(2):
            pc = {}
            pc["zf"] = add_chunk([(kview(8, 512), wsrc(w_in, 0, 1024, 0, 512))])
            pc["q"] = add_chunk([(kview(8, 512), wsrc(w_in, 0, 1024, 512, 512))])
            pc["kv"] = add_chunk([(kview(8, 256), wsrc(w_in, 0, 1024, 1024, 256))])
            if p == 0:
                ada_chunks += [add_chunk([(kview(8, 512), wsrc(w_ada, 0, 1024, cc * 512, 512))]) for cc in range(4, 12)]
            if p == 1:
                for nh in range(2):
                    pc["dc", nh] = add_chunk([(kview(8, 512), wsrc(dftc_in, 0, 1024, nh * 512, 512))])
                    pc["ds", nh] = add_chunk([(kview(8, 512), wsrc(dfts_in, 0, 1024, nh * 512, 512))])
            for j in range(2):
                pc["gf", j] = add_chunk([(kview(8, 512), wsrc(w_in, 0, 1024, 1280 + 512 * j, 512))])
                pc["ga", j] = add_chunk([(kview(8, 512), wsrc(w_in, 0, 1024, 2304 + 512 * j, 512))])
                pc["fo", j] = add_chunk([(kview(4, 512), wsrc(w_f, 0, 512, 512 * j, 512)),
                                         (kview_off(2048, 4, 512), wsrc(w_ao, 0, 512, 512 * j, 512))])
            for ch in range(2):
                pc["wo", ch] = add_chunk([(kview(8, 512), wsrc(w_out, 0, 1024, 512 * ch, 512))])
            for jc in range(11):
                pc["up", jc] = add_chunk([
                    (lambda w: w[:, :].rearrange("p (k c) -> p k c", c=512)[:, :, 0:256], wsrc(w_up, 0, 1024, jc * 256, 256)),
                    (lambda w: w[:, :].rearrange("p (k c) -> p k c", c=512)[:, :, 256:512], wsrc(w_up, 0, 1024, DFF + jc * 256, 256)),
                ])
            for cq in range(4):
                pc["dn", cq, 0] = add_chunk([(kview(11, 256), wsrc(w_down, 0, 1408, cq * 256, 256))])
                pc["dn", cq, 1] = add_chunk([(kview(11, 256), wsrc(w_down, 1408, 1408, cq * 256, 256))])
            pass_chunks.append(pc)

        S.dma("sp", cT[:], cT_in, writes=["cT"])
        S.dma("sp", badaT[:], badaT_in, writes=["badaT"])
        S.dma("sp", gn[:], gn_in, writes=["gn"])
        S.dma("sp", identf[:], ident_in, writes=["identf"])
        S.dma("sp", gqk[:], gqk_in, writes=["gqk"])
        S.dma("sp", esink[:], sinks_in, writes=["esink"])
        S.dma("sp", ropec[:], ropec_in.rearrange("(t p) d -> p t d", p=128), writes=["ropec"])
        S.dma("sp", ropes[:], ropes_in.rearrange("(t p) d -> p t d", p=128), writes=["ropes"])
        S.dma("pool", identb[:], ident_in, writes=["identb"])
        S.dma("pool", chan[:], chan_in, writes=["chan"])
        S.dma("pool", mask3[:], mask_in, writes=["mask3"])
        S.dma("pool", dftc256[:], dftc256_in.rearrange("(k p) n -> p k n", p=128), writes=["dftc256"])
        S.dma("pool", dfts256[:], dfts256_in.rearrange("(k p) n -> p k n", p=128), writes=["dfts256"])
        pump()
        S.op("pool", lambda e: e.memset(neghalf[:], -0.5), writes=["neghalf"], n=16)
        S.op("pool", lambda e: e.memset(kTz[64:128, 0, :], 0.0), writes=["kTz_z0"], n=512)
        S.op("pool", lambda e: e.memset(kTz[0:64, 1, :], 0.0), writes=["kTz_z1"], n=512)
        S.op("pool", lambda e: e.memset(ckT[:], 0.0), writes=["ckT"], n=256)
        S.op("pool", lambda e: e.memset(v_aug[:, :, :, 64:65], 1.0), writes=["v_aug_ones"], n=16)
        S.op("pool", lambda e: e.memset(cv_aug[:, :, :, 64:65], 1.0), writes=["cv_aug"], n=8)

        S.op("act", lambda e: e.activation(out=siluT[:], in_=cT[:], func=AF.Silu), reads=["cT"], writes=["siluT"], n=16)
        def adaln_part(ccs):
            for cc in ccs:
                W, wk = use(ada_chunks[cc])
                Wv = W[:, :].rearrange("p (k c) -> p k c", c=512)
                b = mbank()
                mp = PB(b)[:, 0:8].rearrange("p (c j) -> p c j", j=2)

                def mm(e, Wv=Wv, mp=mp):
                    for f in range(4):
                        for k in range(8):
                            ins = e.matmul(mp[:, f, :], lhsT=Wv[:, k, f * 128:(f + 1) * 128],
                                           rhs=siluT[:, 2 * k:2 * k + 2], start=(k == 0), stop=(k == 7))
                    return ins
                S.op("pe", mm, reads=[wk, "siluT"], writes=[("P", b)], cost=1000)
                release(ada_chunks[cc])
                S.op("dve", lambda e, mp=mp, cc=cc: e.tensor_tensor(
                    out=modT[:, 4 * cc:4 * cc + 4, :], in0=mp,
                    in1=badaT[:, 4 * cc:4 * cc + 4].unsqueeze(2).to_broadcast([128, 4, 2]), op=ALU.add),
                    reads=[("P", b), "badaT"], writes=[("modT", cc)], n=8)
                if cc == 3:
                    S.op("dve", lambda e: e.scalar_tensor_tensor(
                        out=s1T[:], in0=modT[:, 8:16, :], scalar=1.0,
                        in1=gn[:, 0:8].unsqueeze(2).to_broadcast([128, 8, 2]), op0=ALU.add, op1=ALU.mult),
                        reads=[("modT", 2), ("modT", 3), "gn"], writes=["s1T"], n=16)
                if cc == 9:
                    S.op("dve", lambda e: e.scalar_tensor_tensor(
                        out=s2T[:], in0=modT[:, 32:40, :], scalar=1.0,
                        in1=gn[:, 8:16].unsqueeze(2).to_broadcast([128, 8, 2]), op0=ALU.add, op1=ALU.mult),
                        reads=[("modT", 8), ("modT", 9), "gn"], writes=["s2T"], n=16)

        adaln_part(range(4))

        S.op("act", lambda e: e.activation(out=esink[:], in_=esink[:], func=AF.Exp), reads=["esink"], writes=["esink"], n=8)

        S.dma("pool", ckb[:], ck_in.rearrange("(b p) d -> p b d", p=128), writes=["ckb"])
        tb = tbank()

        def trck(e, tb=tb):
            for b in range(2):
                ins = e.transpose(out=PBb(tb)[:, b * 128:(b + 1) * 128], in_=ckb[:, b, :], identity=identb[:])
            return ins
        S.op("pe", trck, reads=["ckb", "identb"], writes=[("P", tb)], cost=300)
        S.op("dve", lambda e, tb=tb: e.tensor_copy(out=ckT[0:64, 0, :, :], in_=PBb(tb)[0:64, 0:256].rearrange("p (b k) -> p b k", k=128)),
             reads=[("P", tb), "ckT"], writes=["ckT"], n=256)
        S.op("dve", lambda e, tb=tb: e.tensor_copy(out=ckT[64:128, 1, :, :], in_=PBb(tb)[64:128, 0:256].rearrange("p (b k) -> p b k", k=128)),
             reads=[("P", tb), "ckT"], writes=["ckT"], n=256)
        for b_ in range(2):
            S.dma("pool", cv_aug[:, b_, :, 0:64], cv_in[b_ * 128:(b_ + 1) * 128, :].rearrange("p (k d) -> p k d", d=64),
                  reads=["cv_aug"], writes=["cv_aug"])

        def build_gt_bc(p, which, dst, scale):
            base = which * 8
            for half in range(2):
                b = mbank()
                for kk in range(4):
                    k = half * 4 + kk
                    S.op("dve", lambda e, k=k: e.tensor_scalar(
                        out=gtrep[:], in0=modT[:, base + k, p:p + 1].to_broadcast([128, 128]),
                        scalar1=scale, scalar2=None, op0=ALU.mult),
                        reads=[("modT", 2 * which), ("modT", 2 * which + 1)], writes=["gtrep"], n=128)
                    S.op("pe", lambda e, b=b, kk=kk: e.transpose(out=PB(b)[:, kk * 128:(kk + 1) * 128], in_=gtrep[:], identity=identf[:]),
                         reads=["gtrep", "identf"], writes=[("P", b)], cost=450)
                S.op("act", lambda e, b=b, half=half: e.copy(out=dst[:, half * 512:(half + 1) * 512], in_=PB(b)),
                     reads=[("P", b)], writes=[("gtbc", id(dst), half)])

        def norm_tile(p, t, which, sT, bidx):
            col = which * NT + t
            S.op("act", lambda e: e.activation(out=xnb[t % 2][:], in_=x_res[:, t, :], func=AF.Square,
                                               accum_out=ss[:, col:col + 1]),
                 reads=[("x", t)], writes=[("xnb", t % 2), ("ss", col)], n=1024)
            S.op("pool", lambda e: e.tensor_scalar(out=vv[:, col:col + 1], in0=ss[:, col:col + 1],
                                                   scalar1=1.0 / D, scalar2=EPS, op0=ALU.mult, op1=ALU.add),
                 reads=[("ss", col)], writes=[("vv", col)], cost=200)
            S.op("pool", lambda e: e.tensor_tensor(out=rstd[:, col:col + 1], in0=vv[:, col:col + 1],
                                                   in1=neghalf[:, 0:1], op=ALU.pow),
                 reads=[("vv", col), "neghalf"], writes=[("rstd", col)], cost=780)
            xb = xnb[t % 2]
            xk = ("xnb", t % 2)
            S.op("dve", lambda e: e.tensor_scalar(out=xb[:], in0=x_res[:, t, :], scalar1=rstd[:, col:col + 1],
                                                  scalar2=None, op0=ALU.mult),
                 reads=[("x", t), ("rstd", col)], writes=[xk], cost=713)
            tbA, tbB = tbank(), tbank()

            def tr(e):
                for k in range(8):
                    tb_ = tbA if k < 5 else tbB
                    kk_ = k if k < 5 else k - 5
                    ins = e.transpose(out=PBb(tb_)[:, kk_ * 128:(kk_ + 1) * 128], in_=xb[:, k * 128:(k + 1) * 128], identity=identb[:])
                return ins
            S.op("pe", tr, reads=[xk, "identb"], writes=[("P", tbA), ("P", tbB)], cost=700)
            mk_ = (["s1T", ("modT", 0), ("modT", 1)] if which == 0 else ["s2T", ("modT", 6), ("modT", 7)])
            for k in range(8):
                if k < 5:
                    S.op("dve", lambda e, k=k: e.tensor_scalar(
                        out=hT[:, k, t * 128:(t + 1) * 128], in0=PBb(tbA)[:, k * 128:(k + 1) * 128],
                        scalar1=sT[:, k, p:p + 1], scalar2=modT[:, bidx + k, p:p + 1], op0=ALU.mult, op1=ALU.add),
                        reads=[("P", tbA)] + mk_, writes=[("hT", t, k)], cost=280)
                else:
                    S.op("act", lambda e, k=k: e.activation(
                        out=hT[:, k, t * 128:(t + 1) * 128], in_=PBb(tbB)[:, (k - 5) * 128:(k - 4) * 128],
                        func=AF.Identity, scale=sT[:, k, p:p + 1], bias=modT[:, bidx + k, p:p + 1]),
                        reads=[("P", tbB)] + mk_, writes=[("hT", t, k)], cost=480)

        def hT_keys(t):
            return [("hT", t, k) for k in range(8)]

        def stop_at(name, p):
            if stop == (name, p):
                raise _Stop()

        try:
          for p in range(2):
            stop_at('p0', p)
            pc = pass_chunks[p]
            latent = (p == 1)
            if p == 1:
                build_gt_bc(p, 2, gt_bc[0], 0.5)
                build_gt_bc(p, 5, gt_bc[1], 1.0)
            gt1k = [("gtbc", id(gt_bc[0]), h) for h in range(2)]
            gt2k = [("gtbc", id(gt_bc[1]), h) for h in range(2)]

            for t in range(NT):
                S.dma("sp", x_res[:, t, :], xin[p, t * 128:(t + 1) * 128, :], writes=[("x", t)], nbytes=512 * 1024,
                      reads=([("modT", 2)] if (p == 0 and t >= 2) else []))
            Wzf, kzf = use(pc["zf"])
            Wq, kq = use(pc["q"])
            Wkv, kkv = use(pc["kv"])
            Wzfv = Wzf[:, :].rearrange("p (k c) -> p k c", c=512)
            Wqv = Wq[:, :].rearrange("p (k c) -> p k c", c=512)
            Wkvv = Wkv[:, 0:2048].rearrange("p (k c) -> p k c", c=256)
            norm_tile(p, 0, 0, s1T, 0)
            for t in range(NT):
                if t + 1 < NT:
                    norm_tile(p, t + 1, 0, s1T, 0)
                if p == 0:
                    adaln_part([4 + t])
                tcols = slice(t * 128, (t + 1) * 128)
                bzf, bq, bkv = mbank(), mbank(), mbank()

                def mmz(e, b=bzf, Wv=Wzfv, n=512, tcols=tcols):
                    for k in range(8):
                        ins = e.matmul(PB(b)[:, 0:n], lhsT=hT[:, k, tcols], rhs=Wv[:, k, :], start=(k == 0), stop=(k == 7))
                    return ins
                S.op("pe", mmz, reads=hT_keys(t) + [kzf], writes=[("P", bzf)], cost=1900)
                S.op("pe", lambda e, b=bq, Wv=Wqv, tcols=tcols: mmz(e, b, Wv, 512, tcols), reads=hT_keys(t) + [kq], writes=[("P", bq)], cost=1900)
                S.op("pe", lambda e, b=bkv, Wv=Wkvv, tcols=tcols: mmz(e, b, Wv, 256, tcols), reads=hT_keys(t) + [kkv], writes=[("P", bkv)], cost=870)
                zoff = ZF_OFF + t * 512
                S.op("act", lambda e, b=bzf, zoff=zoff: e.copy(out=AR[:, zoff:zoff + 512], in_=PB(b)),
                     reads=[("P", bzf)], writes=ak(zoff, 512))
                i2 = t % 2
                S.op("act", lambda e, b=bq, i2=i2: e.activation(out=sq[i2][:, 0:512], in_=PB(b), func=AF.Square),
                     reads=[("P", bq)], writes=[("sq", i2, 0)])
                S.op("act", lambda e, b=bkv, i2=i2: e.activation(out=sq[i2][:, 512:640], in_=PB(b)[:, 0:128], func=AF.Square),
                     reads=[("P", bkv)], writes=[("sq", i2, 1)], n=128)
                S.op("dve", lambda e, b=bq, i2=i2: e.tensor_tensor(out=qkn[i2][:, 0:512], in0=PB(b), in1=gqk[:, 0:512], op=ALU.mult),
                     reads=[("P", bq), "gqk"], writes=[("qkn", i2, 0)])
                S.op("dve", lambda e, b=bkv, i2=i2: e.tensor_tensor(out=qkn[i2][:, 512:640], in0=PB(b)[:, 0:128], in1=gqk[:, 512:640], op=ALU.mult),
                     reads=[("P", bkv), "gqk"], writes=[("qkn", i2, 1)], n=128)
                S.op("dve", lambda e, i2=i2: e.tensor_reduce(out=ssqk[i2][:], in_=sq[i2][:].rearrange("p (h d) -> p h d", d=64),
                                                             axis=AX.X, op=ALU.add),
                     reads=[("sq", i2, 0), ("sq", i2, 1)], writes=[("ssqk", i2)], n=640)
                S.op("pool", lambda e, i2=i2: e.tensor_scalar(out=vqk[i2][:], in0=ssqk[i2][:], scalar1=1.0 / 64, scalar2=EPS,
                                                              op0=ALU.mult, op1=ALU.add),
                     reads=[("ssqk", i2)], writes=[("vqk", i2)], cost=225)
                S.op("pool", lambda e, i2=i2: e.tensor_tensor(out=rqk[i2][:], in0=vqk[i2][:], in1=neghalf[:, 0:10], op=ALU.pow),
                     reads=[("vqk", i2), "neghalf"], writes=[("rqk", i2)], cost=1974)
                S.op("pool", lambda e, i2=i2: e.tensor_tensor(
                    out=qkn[i2][:].rearrange("p (h d) -> p h d", d=64), in0=qkn[i2][:].rearrange("p (h d) -> p h d", d=64),
                    in1=rqk[i2][:, 0:10].unsqueeze(2).to_broadcast([128, 10, 64]), op=ALU.mult),
                    reads=[("qkn", i2, 0), ("qkn", i2, 1), ("rqk", i2)], writes=[("qkn", i2, 0), ("qkn", i2, 1)], n=640)
                qkk = [("qkn", i2, 0), ("qkn", i2, 1)]
                S.op("act", lambda e, b=bkv, t=t: e.copy(out=v_aug[:, t, :, 0:64], in_=PB(b)[:, 128:256].rearrange("p (k d) -> p k d", d=64)),
                     reads=[("P", bkv), "v_aug_ones"], writes=[("v_aug", t)], n=128)
                if not latent:
                    S.op("act", lambda e, b=bkv, i2=i2: e.copy(out=vf[i2][:], in_=PB(b)[:, 128:256]),
                         reads=[("P", bkv)], writes=[("vf", i2)], n=128)
                    S.dma("sp", nv_out[t * 128:(t + 1) * 128, :], vf[i2][:], reads=[("vf", i2)])
                    S.dma("sp", nk_out[t * 128:(t + 1) * 128, :], qkn[i2][:, 512:640], reads=qkk)
                    src = qkn[i2]
                    srck = qkk
                else:
                    S.op("dve", lambda e, i2=i2, t=t: e.tensor_tensor(
                        out=sq[i2][:].rearrange("p (h d) -> p h d", d=64), in0=qkn[i2][:].rearrange("p (h d) -> p h d", d=64),
                        in1=ropec[:, t, :].unsqueeze(1).to_broadcast([128, 10, 64]), op=ALU.mult),
                        reads=qkk + ["ropec"], writes=[("sq", i2, 0), ("sq", i2, 1)], n=640)
                    for hf in range(2):
                        S.op("pool", lambda e, i2=i2, t=t, hf=hf: e.tensor_tensor(
                            out=rt2[:].rearrange("p (h a f d) -> p h a f d", a=2, f=2, d=16)[:, :, :, hf, :],
                            in0=qkn[i2][:].rearrange("p (h a f d) -> p h a f d", a=2, f=2, d=16)[:, :, :, 1 - hf, :],
                            in1=ropes[:, t, :].rearrange("p (a f d) -> p a f d", a=2, f=2)[:, :, hf, :].unsqueeze(1).to_broadcast([128, 10, 2, 16]),
                            op=ALU.mult),
                            reads=qkk + ["ropes"], writes=[("rt2", hf)], n=320)
                    S.op("dve", lambda e, i2=i2: e.tensor_tensor(out=sq[i2][:], in0=sq[i2][:], in1=rt2[:], op=ALU.add),
                         reads=[("sq", i2, 0), ("sq", i2, 1), ("rt2", 0), ("rt2", 1)], writes=[("sq", i2, 0), ("sq", i2, 1)], n=640)
                    src = sq[i2]
                    srck = [("sq", i2, 0), ("sq", i2, 1)]
                S.op("act", lambda e, i2=i2, src=src: e.copy(
                    out=qb[i2][:, 0:512].rearrange("p (g k d) -> p k g d", k=2, d=64),
                    in_=src[:, 0:512].rearrange("p (k g d) -> p k g d", k=2, d=64)),
                    reads=srck, writes=[("qb", i2, 0)])
                S.op("act", lambda e, i2=i2, src=src: e.copy(out=qb[i2][:, 512:640], in_=src[:, 512:640]),
                     reads=srck, writes=[("qb", i2, 1)], n=128)
                tb = tbank()

                def trq(e, i2=i2, tb=tb):
                    for c in range(5):
                        ins = e.transpose(out=PBb(tb)[:, c * 128:(c + 1) * 128], in_=qb[i2][:, c * 128:(c + 1) * 128], identity=identb[:])
                    return ins
                S.op("pe", trq, reads=[("qb", i2, 0), ("qb", i2, 1), "identb"], writes=[("P", tb)], cost=650)
                qkeys = []
                for c in range(4):
                    qkeys += ak(c * 1024 + t * 128, 128)
                S.op("dve", lambda e, tb=tb, t=t: e.tensor_copy(
                    out=AR[:, 0:4096].rearrange("p (c n) -> p c n", n=1024)[:, :, t * 128:(t + 1) * 128],
                    in_=PBb(tb)[:, 0:512].rearrange("p (c n) -> p c n", n=128)),
                    reads=[("P", tb)], writes=qkeys, cost=430)
                S.op("dve", lambda e, tb=tb, t=t: e.tensor_copy(out=kTz[0:64, 0, t * 128:(t + 1) * 128], in_=PBb(tb)[0:64, 512:640]),
                     reads=[("P", tb), "kTz_z0"], writes=[("kTz", t, 0)], cost=250)
                S.op("dve", lambda e, tb=tb, t=t: e.tensor_copy(out=kTz[64:128, 1, t * 128:(t + 1) * 128], in_=PBb(tb)[64:128, 512:640]),
                     reads=[("P", tb), "kTz_z1"], writes=[("kTz", t, 1)], cost=250)

            release(pc["zf"])
            release(pc["q"])
            release(pc["kv"])
            dbg_dump("hT", hT[:], [128, 8, 1024], [k for t in range(NT) for k in hT_keys(t)], BF16) if p == dbg_pass(dbg) else None
            dbg_dump("AR1", AR[:], [128, 22 * 1024], [("A", b) for b in range(176)], BF16) if p == dbg_pass(dbg) else None
            dbg_dump("v_aug", v_aug[:], [128, NT, 2, 65], [("v_aug", t) for t in range(NT)], BF16) if p == dbg_pass(dbg) else None

            if p == 0:
                build_gt_bc(p, 2, gt_bc[0], 0.5)
                build_gt_bc(p, 5, gt_bc[1], 1.0)
            stop_at('p1', p)
            Y12_OFF = 17 * 1024
            def fourier_ctx(s):
                for gp in range(2):
                    bo = mbank()
                    for gg in range(2):
                        g = gp * 2 + gg
                        b1 = mbank()

                        def mmy(e, b1=b1, g=g, s=s):
                            for (half, Dm) in ((0, dftc256), (1, dfts256)):
                                for tt in range(2):
                                    zoff = ZF_OFF + (2 * s + tt) * 512 + g * 128
                                    ins = e.matmul(PB(b1)[:, half * 256:(half + 1) * 256], lhsT=AR[:, zoff:zoff + 128],
                                                   rhs=Dm[:, tt, :], start=(tt == 0), stop=(tt == 1))
                            return ins
                        S.op("pe", mmy, reads=ak(ZF_OFF + 2 * s * 512, 1024) + ["dftc256", "dfts256"], writes=[("P", b1)], cost=520)
                        yo = Y12_OFF + (g % 2) * 512
                        S.op("act", lambda e, b1=b1, yo=yo: e.copy(out=AR[:, yo:yo + 512], in_=PB(b1)),
                             reads=[("P", b1)], writes=ak(yo, 512))

                        def mmc(e, bo=bo, gg=gg, yo=yo):
                            e.matmul(PB(bo)[:, gg * 256:(gg + 1) * 256], lhsT=chan[:, 0, :], rhs=AR[:, yo:yo + 256], start=True, stop=False)
                            return e.matmul(PB(bo)[:, gg * 256:(gg + 1) * 256], lhsT=chan[:, 1, :], rhs=AR[:, yo + 256:yo + 512], start=False, stop=True)
                        S.op("pe", mmc, reads=ak(yo, 512) + ["chan"], writes=[("P", bo)], cost=260)
                    mk = []
                    for gg in range(2):
                        mk += ak(MIX_OFF + (gp * 2 + gg) * 1024 + s * 256, 256)
                    S.op("dve", lambda e, bo=bo, gp=gp, s=s: e.tensor_copy(
                        out=AR[:, MIX_OFF + gp * 2048:MIX_OFF + (gp + 1) * 2048].rearrange("p (g n) -> p g n", n=1024)[:, :, s * 256:(s + 1) * 256],
                        in_=PB(bo).rearrange("p (g n) -> p g n", n=256)),
                        reads=[("P", bo)], writes=mk, n=512)
            if not latent:
                _mpool[0] = list(MB) + CTX_EXTRA_BANKS
                if not CTX_INTERLEAVE:
                    for s in range(4):
                        fourier_ctx(s)
            else:
                _mpool[0] = list(MB) + LATF_EXTRA_BANKS
                for nh in range(2):
                    Wc, kwc = use(pc["dc", nh])
                    Ws, kws = use(pc["ds", nh])
                    Wcv = Wc[:, :].rearrange("p (k c) -> p k c", c=512)
                    Wsv = Ws[:, :].rearrange("p (k c) -> p k c", c=512)
                    for g in range(4):
                        b1, b2 = mbank(), mbank()
                        for (b, Dm, dk) in ((b1, Wcv, kwc), (b2, Wsv, kws)):
                            def mmy(e, b=b, Dm=Dm, g=g):
                                for tt in range(8):
                                    zoff = ZF_OFF + tt * 512 + g * 128
                                    ins = e.matmul(PB(b), lhsT=AR[:, zoff:zoff + 128], rhs=Dm[:, tt, :],
                                                   start=(tt == 0), stop=(tt == 7))
                                return ins
                            S.op("pe", mmy, reads=ak(ZF_OFF, 4096) + [dk], writes=[("P", b)], cost=1900)
                        S.op("act", lambda e, b1=b1: e.copy(out=AR[:, Y12_OFF:Y12_OFF + 512], in_=PB(b1)),
                             reads=[("P", b1)], writes=ak(Y12_OFF, 512))
                        S.op("dve", lambda e, b2=b2: e.tensor_copy(out=AR[:, Y12_OFF + 512:Y12_OFF + 1024], in_=PB(b2)),
                             reads=[("P", b2)], writes=ak(Y12_OFF + 512, 512))
                        bo = mbank()

                        def mmc(e, bo=bo):
                            e.matmul(PB(bo), lhsT=chan[:, 2, :], rhs=AR[:, Y12_OFF:Y12_OFF + 512], start=True, stop=False)
                            return e.matmul(PB(bo), lhsT=chan[:, 3, :], rhs=AR[:, Y12_OFF + 512:Y12_OFF + 1024], start=False, stop=True)
                        S.op("pe", mmc, reads=ak(Y12_OFF, 1024) + ["chan"], writes=[("P", bo)], cost=480)
                        moff = MIX_OFF + g * 1024 + nh * 512
                        S.op("act", lambda e, bo=bo, moff=moff: e.copy(out=AR[:, moff:moff + 512], in_=PB(bo)),
                             reads=[("P", bo)], writes=ak(moff, 512))
                    release(pc["dc", nh])
                    release(pc["ds", nh])

            stop_at('f', p)
            def o_off(t):
                return ZF_OFF + t * 512

            def evac_pv(bank, ntile, t0, h, di):
                pv = PB(bank)[:, 0:ntile * 65].rearrange("p (n d) -> p n d", d=65)
                S.op("dve", lambda e: e.tensor_scalar(out=den[di][:, 0:ntile], in0=pv[:, :, 64], scalar1=esink[:, h:h + 1],
                                                      scalar2=None, op0=ALU.add),
                     reads=[("P", bank), "esink"], writes=[("den", di)], cost=220)
                S.op("dve", lambda e: e.reciprocal(out=rec[di][:, 0:ntile], in_=den[di][:, 0:ntile]),
                     reads=[("den", di)], writes=[("rec", di)], cost=170)
                okeys = []
                for i in range(ntile):
                    okeys += ak(o_off(t0 + i) + h * 64, 64)
                S.op("dve", lambda e: e.tensor_tensor(
                    out=AR[:, o_off(t0):o_off(t0) + ntile * 512].rearrange("p (n c) -> p n c", c=512)[:, :, h * 64:(h + 1) * 64],
                    in0=pv[:, :, 0:64], in1=rec[di][:, 0:ntile].unsqueeze(2).to_broadcast([128, ntile, 64]), op=ALU.mult),
                    reads=[("P", bank), ("rec", di)], writes=okeys, n=200)

            cntl = [0]
            def attn_ctx(s):
                for kv in range(2):
                    kp = slice(kv * 64, (kv + 1) * 64)
                    for g in range(4):
                        h = kv * 4 + g
                        bs = mbank()

                        def mms(e, bs=bs, kv=kv, g=g, s=s):
                            for jb in range(2):
                                ko = (2 * s + jb) * 128
                                qo = g * 1024 + s * 256
                                ins = e.matmul(PB(bs)[:, jb * 256:(jb + 1) * 256], lhsT=kTz[:, kv, ko:ko + 128], rhs=AR[:, qo:qo + 256],
                                               start=True, stop=True)
                            return ins
                        S.op("pe", mms, reads=[("kTz", 2 * s, kv), ("kTz", 2 * s + 1, kv)] + ak(g * 1024 + s * 256, 256), writes=[("P", bs)], cost=260)
                        pi = cntl[0] % 4
                        cntl[0] += 1
                        po = PTC_OFF + pi * 512
                        S.op("act", lambda e, bs=bs, po=po: e.activation(out=AR[:, po:po + 512], in_=PB(bs), func=AF.Exp, scale=0.125),
                             reads=[("P", bs)], writes=ak(po, 512))
                        bp = mbank()

                        def mmpv(e, bp=bp, po=po, s=s, kv=kv):
                            for qt in range(2):
                                for jb in range(2):
                                    ins = e.matmul(PB(bp)[:, qt * 65:(qt + 1) * 65],
                                                   lhsT=AR[:, po + jb * 256 + qt * 128:po + jb * 256 + (qt + 1) * 128],
                                                   rhs=v_aug[:, 2 * s + jb, kv, :], start=(jb == 0), stop=(jb == 1))
                            return ins
                        S.op("pe", mmpv, reads=ak(po, 512) + [("v_aug", 2 * s), ("v_aug", 2 * s + 1)], writes=[("P", bp)], cost=420)
                        evac_pv(bp, 2, 2 * s, h, cntl[0] % 4)
            if not latent:
                for s in range(4):
                    if CTX_INTERLEAVE:
                        fourier_ctx(s)
                    attn_ctx(s)
            else:
                _mpool[0] = MB[2:] + LAT_EXTRA_BANKS
                for kv in range(2):
                    kp = slice(kv * 64, (kv + 1) * 64)
                    for g in range(4):
                        h = kv * 4 + g
                        def cache_scores(qh, kv=kv, g=g):
                            for cb in range(2):
                                bs = mbank()
                                qo = g * 1024 + qh * 512
                                S.op("pe", lambda e, bs=bs, cb=cb, qo=qo, kv=kv: e.matmul(
                                    PB(bs), lhsT=ckT[:, kv, cb, :], rhs=AR[:, qo:qo + 512], start=True, stop=True),
                                    reads=["ckT"] + ak(qo, 512), writes=[("P", bs)], cost=240)
                                po = PTC_OFF + (qh * 2 + cb) * 512
                                S.op("act", lambda e, bs=bs, po=po: e.activation(out=AR[:, po:po + 512], in_=PB(bs), func=AF.Exp, scale=0.125),
                                     reads=[("P", bs)], writes=ak(po, 512))
                        cache_scores(0)
                        bpv = MB[0:2]

                        def pv_group(qt, kv=kv, g=g, bpv=bpv):
                            jbs = [jb for jb in (qt - 1, qt, qt + 1) if 0 <= jb < 8]
                            reads = []
                            for jb in jbs:
                                reads += ak(PTL_OFF + (jb % 4) * 384, 384) + [("v_aug", jb)]
                            reads += ak(PTC_OFF, 2048) + ["cv_aug"]
                            bank = bpv[qt // 4]

                            def fn(e):
                                n = len(jbs) + 2
                                i = 0
                                for jb in jbs:
                                    lo = PTL_OFF + (jb % 4) * 384 + (qt - jb + 1) * 128
                                    e.matmul(PB(bank)[:, (qt % 4) * 65:(qt % 4 + 1) * 65], lhsT=AR[:, lo:lo + 128],
                                             rhs=v_aug[:, jb, kv, :], start=(i == 0), stop=False)
                                    i += 1
                                for cb in range(2):
                                    lo = PTC_OFF + ((qt // 4) * 2 + cb) * 512 + (qt % 4) * 128
                                    ins = e.matmul(PB(bank)[:, (qt % 4) * 65:(qt % 4 + 1) * 65], lhsT=AR[:, lo:lo + 128],
                                                   rhs=cv_aug[:, cb, kv, :], start=False, stop=(cb == 1))
                                return ins
                            S.op("pe", fn, reads=reads, writes=[("P", bank)], cost=520)

                        for jb in range(8):
                            if jb == CACHE_QH1_AT:
                                cache_scores(1)
                            qlo, qhi = max(jb - 1, 0), min(jb + 1, 7)
                            nq = (qhi - qlo + 1) * 128
                            c0 = (qlo - (jb - 1)) * 128
                            bs = mbank()
                            ko = jb * 128
                            qo = g * 1024 + qlo * 128
                            S.op("pe", lambda e, bs=bs, ko=ko, qo=qo, nq=nq, c0=c0, kv=kv: e.matmul(
                                PB(bs)[:, c0:c0 + nq], lhsT=kTz[:, kv, ko:ko + 128], rhs=AR[:, qo:qo + nq], start=True, stop=True),
                                reads=[("kTz", jb, kv)] + ak(qo, nq), writes=[("P", bs)], cost=190)
                            po = PTL_OFF + (jb % 4) * 384
                            S.op("act", lambda e, bs=bs, po=po, c0=c0, nq=nq: e.activation(
                                out=AR[:, po + c0:po + c0 + nq], in_=PB(bs)[:, c0:c0 + nq], func=AF.Exp, scale=0.125),
                                reads=[("P", bs)], writes=ak(po, 384), n=384)
                            if 1 <= jb <= 6:
                                S.op("pool", lambda e, po=po: e.tensor_tensor(
                                    out=AR[:, po:po + 384].rearrange("p (b n) -> p b n", n=128)[:, 0:3:2, :],
                                    in0=AR[:, po:po + 384].rearrange("p (b n) -> p b n", n=128)[:, 0:3:2, :],
                                    in1=mask3[:, :].rearrange("p (b n) -> p b n", n=128)[:, 0:3:2, :], op=ALU.mult),
                                    reads=ak(po, 384) + ["mask3"], writes=ak(po, 384), n=256)
                            else:
                                mb_ = 2 if jb == 0 else 0
                                S.op("pool", lambda e, po=po, mb_=mb_: e.tensor_tensor(
                                    out=AR[:, po + mb_ * 128:po + (mb_ + 1) * 128], in0=AR[:, po + mb_ * 128:po + (mb_ + 1) * 128],
                                    in1=mask3[:, mb_ * 128:(mb_ + 1) * 128], op=ALU.mult),
                                    reads=ak(po, 384) + ["mask3"], writes=ak(po, 384), n=128)
                            if jb >= 1:
                                pv_group(jb - 1)
                                if jb - 1 == 3:
                                    evac_pv(bpv[0], 4, 0, h, (2 * h) % 4)
                        pv_group(7)
                        evac_pv(bpv[1], 4, 4, h, (2 * h + 1) % 4)
                _mpool[0] = list(MB)

            dbg_dump("o_tm", AR[:, ZF_OFF:ZF_OFF + 4096], [128, 4096], ak(ZF_OFF, 4096), BF16) if p == dbg_pass(dbg) else None

            _mpool[0] = list(MB)
            for t in range(NT):
                tb = tbank()

                def tro(e, tb=tb, t=t):
                    for c in range(4):
                        ins = e.transpose(out=PBb(tb)[:, c * 128:(c + 1) * 128], in_=AR[:, o_off(t) + c * 128:o_off(t) + (c + 1) * 128],
                                          identity=identb[:])
                    return ins
                S.op("pe", tro, reads=ak(o_off(t), 512) + ["identb"], writes=[("P", tb)], cost=520)
                okeys = []
                for c in range(4):
                    okeys += ak(OT_OFF + c * 1024 + t * 128, 128)
                S.op("act", lambda e, tb=tb, t=t: e.copy(
                    out=AR[:, OT_OFF:OT_OFF + 4096].rearrange("p (c n) -> p c n", n=1024)[:, :, t * 128:(t + 1) * 128],
                    in_=PBb(tb)[:, 0:512].rearrange("p (c n) -> p c n", n=128)),
                    reads=[("P", tb)], writes=okeys)

            dbg_dump("AR2", AR[:], [128, 22 * 1024], [("A", b) for b in range(176)], BF16) if p == dbg_pass(dbg) else None

            stop_at('a', p)
            _mpool[0] = list(MB) + P3_EXTRA_BANKS
            for j in range(2):
                Wgf, kgf = use(pc["gf", j])
                Wga, kga = use(pc["ga", j])
                Wfo, kfo = use(pc["fo", j])
                Wgfv = Wgf[:, :].rearrange("p (k c) -> p k c", c=512)
                Wgav = Wga[:, :].rearrange("p (k c) -> p k c", c=512)
                Wfov = Wfo[:, :].rearrange("p (k c) -> p k c", c=512)
                for fc in range(4):
                    f = j * 4 + fc
                    fcols = slice(fc * 128, (fc + 1) * 128)
                    for stl in range(2):
                        ncols = slice(stl * 512, (stl + 1) * 512)
                        hk = [k for t in range(4 * stl, 4 * stl + 4) for k in hT_keys(t)]
                        bg1, bg2, by1, by2 = mbank(), mbank(), mbank(), mbank()

                        def mmg(e, b, Wv, fcols=fcols, ncols=ncols):
                            for k in range(8):
                                ins = e.matmul(PB(b), lhsT=Wv[:, k, fcols], rhs=hT[:, k, ncols], start=(k == 0), stop=(k == 7))
                            return ins
                        S.op("pe", lambda e, b=bg1, Wv=Wgfv, mmg=mmg: mmg(e, b, Wv), reads=hk + [kgf], writes=[("P", bg1)], cost=1810)
                        S.op("pe", lambda e, b=bg2, Wv=Wgav, mmg=mmg: mmg(e, b, Wv), reads=hk + [kga], writes=[("P", bg2)], cost=1810)

                        def mmy2(e, b, koff, aoff, fcols=fcols, stl=stl, Wfov=Wfov):
                            for k in range(4):
                                o = aoff + k * 1024 + stl * 512
                                ins = e.matmul(PB(b), lhsT=Wfov[:, koff + k, fcols], rhs=AR[:, o:o + 512], start=(k == 0), stop=(k == 3))
                            return ins
                        mixk = [x for k in range(4) for x in ak(MIX_OFF + k * 1024 + stl * 512, 512)]
                        otk = [x for k in range(4) for x in ak(OT_OFF + k * 1024 + stl * 512, 512)]
                        S.op("pe", lambda e, b=by1, mmy2=mmy2: mmy2(e, b, 0, MIX_OFF), reads=mixk + [kfo], writes=[("P", by1)], cost=940)
                        S.op("pe", lambda e, b=by2, mmy2=mmy2: mmy2(e, b, 4, OT_OFF), reads=otk + [kfo], writes=[("P", by2)], cost=940)
                        ti = (f * 2 + stl) % 2
                        S.op("act", lambda e, b=bg1, ti=ti: e.activation(out=tg[ti][:, 0:512], in_=PB(b), func=AF.Tanh, scale=0.5),
                             reads=[("P", bg1)], writes=[("tg", ti, 0)])
                        S.op("act", lambda e, b=bg2, ti=ti: e.activation(out=tg[ti][:, 512:1024], in_=PB(b), func=AF.Tanh, scale=0.5),
                             reads=[("P", bg2)], writes=[("tg", ti, 1)])
                        S.op("dve", lambda e, b=by1, ti=ti: e.scalar_tensor_tensor(
                            out=tg[ti][:, 0:512], in0=tg[ti][:, 0:512], scalar=1.0, in1=PB(b), op0=ALU.add, op1=ALU.mult),
                            reads=[("tg", ti, 0), ("P", by1)], writes=[("tg", ti, 0)])
                        S.op("dve", lambda e, b=by2, ti=ti: e.scalar_tensor_tensor(
                            out=tg[ti][:, 512:1024], in0=tg[ti][:, 512:1024], scalar=1.0, in1=PB(b), op0=ALU.add, op1=ALU.mult),
                            reads=[("tg", ti, 1), ("P", by2)], writes=[("tg", ti, 1)])
                        S.op("dve", lambda e, ti=ti, f=f, ncols=ncols: e.tensor_tensor(
                            out=mT[:, f, ncols], in0=tg[ti][:, 0:512], in1=tg[ti][:, 512:1024], op=ALU.add),
                            reads=[("tg", ti, 0), ("tg", ti, 1)], writes=[("mT", f, stl)], cost=420)
                release(pc["gf", j])
                release(pc["ga", j])
                release(pc["fo", j])

            dbg_dump("mT", mT[:], [128, 8, 1024], [("mT", f, s_) for f in range(8) for s_ in range(2)], BF16) if p == dbg_pass(dbg) else None

            _mpool[0] = list(MB)
            stop_at('m', p)
            Wos = []
            for ch in range(2):
                Wo, kwo = use(pc["wo", ch])
                Wov_ = Wo[:, :].rearrange("p (k c) -> p k c", c=512)
                for kq in range(4):
                    S.op("dve", lambda e, Wov_=Wov_, ch=ch, kq=kq: e.tensor_tensor(
                        out=Wov_[:, 2 * kq:2 * kq + 2, :], in0=Wov_[:, 2 * kq:2 * kq + 2, :],
                        in1=gt_bc[0][:, ch * 512:(ch + 1) * 512].unsqueeze(1).to_broadcast([128, 2, 512]), op=ALU.mult),
                        reads=[kwo, gt1k[ch]], writes=[kwo], n=1024)
                Wos.append((Wov_, kwo))
            for t in range(NT):
                for ch in range(2):
                    Wov, kwo = Wos[ch]
                    b = mbank()

                    def mmo(e, b=b, t=t, Wov=Wov):
                        for k in range(8):
                            ins = e.matmul(PB(b), lhsT=mT[:, k, t * 128:(t + 1) * 128], rhs=Wov[:, k, :], start=(k == 0), stop=(k == 7))
                        return ins
                    S.op("pe", mmo, reads=[("mT", f, t // 4) for f in range(8)] + [kwo], writes=[("P", b)], cost=1900)
                    S.op("dve", lambda e, b=b, t=t, ch=ch: e.tensor_tensor(
                        out=x_res[:, t, ch * 512:(ch + 1) * 512], in0=PB(b), in1=x_res[:, t, ch * 512:(ch + 1) * 512], op=ALU.add),
                        reads=[("P", b), ("x", t)], writes=[("x", t)])
                norm_tile(p, t, 1, s2T, 24)
            release(pc["wo", 0])
            release(pc["wo", 1])

            dbg_dump("x1", x_res[:], [128, NT, D], [("x", t) for t in range(NT)]) if p == dbg_pass(dbg) else None
            stop_at('x1', p)
            stop_at('n2', p)

            _mpool[0] = list(MB) + P4_EXTRA_BANKS
            def ffn_unit(jc, half, stl, Wuv, kwu):
                jj = jc * 2 + half
                ncols = slice(stl * 512, (stl + 1) * 512)
                hk = [k for t in range(4 * stl, 4 * stl + 4) for k in hT_keys(t)]
                ba, bu = mbank(), mbank()

                def mmu(e, b, c0, Wuv=Wuv, ncols=ncols):
                    for k in range(8):
                        ins = e.matmul(PB(b), lhsT=Wuv[:, k, c0:c0 + 128], rhs=hT[:, k, ncols], start=(k == 0), stop=(k == 7))
                    return ins
                S.op("pe", lambda e, b=ba, c0=half * 128, mmu=mmu: mmu(e, b, c0), reads=hk + [kwu], writes=[("P", ba)], cost=1800)
                S.op("pe", lambda e, b=bu, c0=256 + half * 128, mmu=mmu: mmu(e, b, c0), reads=hk + [kwu], writes=[("P", bu)], cost=1800)
                si = ffn_cnt[0] % 2
                ffn_cnt[0] += 1
                S.op("act", lambda e, b=ba, si=si: e.activation(out=sa[si][:], in_=PB(b), func=AF.Silu),
                     reads=[("P", ba)], writes=[("sa", si)])
                ao = jj * 1024 + stl * 512
                S.op("dve", lambda e, b=bu, si=si, ao=ao: e.tensor_tensor(out=AR[:, ao:ao + 512], in0=sa[si][:], in1=PB(b), op=ALU.mult),
                     reads=[("sa", si), ("P", bu)], writes=ak(ao, 512))

            ffn_cnt = [0]
            jc0 = 0
            if FFN_HEAD_GROUP > 1:
                G = FFN_HEAD_GROUP
                ws = []
                for jc in range(G):
                    Wu, kwu = use(pc["up", jc])
                    ws.append((Wu[:, :].rearrange("p (k c) -> p k c", c=512), kwu))
                for stl in range(2):
                    for jc in range(G):
                        for half in range(2):
                            ffn_unit(jc, half, stl, ws[jc][0], ws[jc][1])
                for jc in range(G):
                    release(pc["up", jc])
                jc0 = G
            for jc in range(jc0, 11):
                Wu, kwu = use(pc["up", jc])
                Wuv = Wu[:, :].rearrange("p (k c) -> p k c", c=512)
                for half in range(2):
                    for stl in range(2):
                        ffn_unit(jc, half, stl, Wuv, kwu)
                release(pc["up", jc])

            stop_at('up', p)
            _mpool[0] = list(MB)
            for cq in range(4):
                Wd0, kd0 = use(pc["dn", cq, 0])
                Wd1, kd1 = use(pc["dn", cq, 1])
                Wd = [Wd0[:, 0:2816].rearrange("p (k c) -> p k c", c=256), Wd1[:, 0:2816].rearrange("p (k c) -> p k c", c=256)]
                for (Wd_, kd_) in ((Wd[0], kd0), (Wd[1], kd1)):
                    S.op("pool", lambda e, Wd_=Wd_, cq=cq: e.tensor_tensor(
                        out=Wd_, in0=Wd_, in1=gt_bc[1][:, cq * 256:(cq + 1) * 256].unsqueeze(1).to_broadcast([128, 11, 256]), op=ALU.mult),
                        reads=[kd_, gt2k[cq // 2]], writes=[kd_], n=2816)
                for t in range(NT):
                    b = mbank()

                    def mmd(e, b=b, t=t, Wd=Wd):
                        for k in range(22):
                            o = k * 1024 + t * 128
                            ins = e.matmul(PB(b)[:, 0:256], lhsT=AR[:, o:o + 128], rhs=Wd[k // 11][:, k % 11, :], start=(k == 0), stop=(k == 21))
                        return ins
                    S.op("pe", mmd, reads=[x for k in range(22) for x in ak(k * 1024 + t * 128, 128)] + [kd0, kd1], writes=[("P", b)], cost=2400)
                    S.op("dve", lambda e, b=b, t=t, cq=cq: e.tensor_tensor(
                        out=x_res[:, t, cq * 256:(cq + 1) * 256], in0=PB(b)[:, 0:256], in1=x_res[:, t, cq * 256:(cq + 1) * 256], op=ALU.add),
                        reads=[("P", b), ("x", t)], writes=[("x", t)], n=256)
                    if cq == 3:
                        S.dma("sp", y_out[p, t * 128:(t + 1) * 128, :], x_res[:, t, :], reads=[("x", t)], nbytes=512 * 1024)
                release(pc["dn", cq, 0])
                release(pc["dn", cq, 1])

        except _Stop:
            pass
        S.finish("sp")
        S.emit(window=SCHED_WINDOW, reorder=SCHED_REORDER)
        _LAST['S'] = S
    return nc, dbg_outs


def dbg_pass(dbg):
    for d in dbg:
        if isinstance(d, tuple) and d[0] == "pass":
            return d[1]
    return 0


def _constants():
    c = {}
    c["ident"] = np.eye(128, dtype=np.float32)
    n = np.arange(1024, dtype=np.float64)
    ang = 2.0 * np.pi * np.outer(n, n) / 1024.0
    c["dft_c"] = np.cos(ang).astype(np.float32)
    c["dft_s"] = np.sin(ang).astype(np.float32)
    n2 = np.arange(256, dtype=np.float64)
    ang2 = 2.0 * np.pi * np.outer(n2, n2) / 256.0
    c["dft_c256"] = np.cos(ang2).astype(np.float32)
    c["dft_s256"] = np.sin(ang2).astype(np.float32)
    m = np.arange(128, dtype=np.float64)
    angc = 2.0 * np.pi * np.outer(m, m) / 128.0
    chan = np.zeros((128, 4, 128), np.float32)
    for i, N in enumerate((256, 1024)):
        sc = 1.0 / np.sqrt(N * 128.0)
        chan[:, 2 * i, :] = np.cos(angc) * sc
        chan[:, 2 * i + 1, :] = -np.sin(angc) * sc
    c["chan"] = chan
    pos = np.arange(1024)
    row = (pos // 64).astype(np.float32)
    col = (pos % 64).astype(np.float32)
    inv = (10000.0 ** (-np.arange(0, 32, 2, dtype=np.float32) / 32)).astype(np.float32)
    ar = row[:, None] * inv[None, :]
    ac = col[:, None] * inv[None, :]
    c["rope_cos"] = np.concatenate([np.cos(ar), np.cos(ar), np.cos(ac), np.cos(ac)], axis=1).astype(np.float32)
    c["rope_sin"] = np.concatenate([-np.sin(ar), np.sin(ar), -np.sin(ac), np.sin(ac)], axis=1).astype(np.float32)
    a = np.arange(128)[:, None]
    b = np.arange(128)[None, :]
    mask3 = np.concatenate([(a <= b), np.ones((128, 128), bool), (b <= a)], axis=1).astype(np.float32)
    c["mask3"] = mask3
    return c


def make_in_maps(x_prompt, x_sample, cache_k, cache_v, c, c_ctx, w_ada, b_ada, g_norm1, g_norm2,
                 w_in, g_q, g_k, sinks, w_f, w_ao, w_out, w_up, w_down):
    f = lambda a: np.ascontiguousarray(np.asarray(a, dtype=np.float32))
    consts = _constants()
    shared = {
        "w_ada": f(w_ada[0]), "w_in": f(w_in[0]), "w_f": f(w_f[0]), "w_ao": f(w_ao[0]),
        "w_out": f(w_out[0]), "w_up": f(w_up[0]), "w_down": f(w_down[0]),
        "b_adaT": f(np.asarray(b_ada[0]).reshape(48, 128).T),
        "gn": f(np.concatenate([np.asarray(g_norm1[0]).reshape(8, 128).T, np.asarray(g_norm2[0]).reshape(8, 128).T], axis=1)),
        "gqk": f(np.broadcast_to(np.concatenate([np.tile(np.asarray(g_q[0]), 8), np.tile(np.asarray(g_k[0]), 2)])[None, :], (128, 640))),
        "sinks_bc": f(np.broadcast_to(np.asarray(sinks[0])[None, :], (128, 8))),
    }
    shared.update(consts)
    xp = np.asarray(x_prompt, dtype=np.float32)
    xs = np.asarray(x_sample, dtype=np.float32)
    maps = []
    for i in range(NCORES):
        m = dict(shared)
        m["xin"] = f(np.stack([xp[4 * i:4 * i + 4].reshape(1024, D), xs[i]], axis=0))
        m["ck"] = f(np.asarray(cache_k)[i, 0].reshape(256, 128))
        m["cv"] = f(np.asarray(cache_v)[i, 0].reshape(256, 128))
        cv2 = np.stack([np.asarray(c_ctx), np.asarray(c)[i]], axis=1)
        m["cT"] = f(cv2.reshape(8, 128, 2).transpose(1, 0, 2).reshape(128, 16))
        maps.append(m)
    return maps


_NC_CACHE = {}


def kernel(**inputs):
    if "nc" not in _NC_CACHE:
        _NC_CACHE["nc"] = build_program()[0]
    nc = _NC_CACHE["nc"]
    in_maps = make_in_maps(**inputs)
    res = run_bass_kernel_spmd(nc, in_maps, core_ids=list(range(NCORES)))
    r = res.results
    y_prompt = np.concatenate([r[i]["y"][0].reshape(4, 256, D) for i in range(NCORES)], axis=0)
    y_sample = np.stack([r[i]["y"][1] for i in range(NCORES)], axis=0)
    nk = np.concatenate([r[i]["nk"].reshape(4, 1, 256, 2, 64) for i in range(NCORES)], axis=0)
    nv = np.concatenate([r[i]["nv"].reshape(4, 1, 256, 2, 64) for i in range(NCORES)], axis=0)
    return (y_prompt.astype(np.float32), y_sample.astype(np.float32), nk.astype(np.float32), nv.astype(np.float32))
```
